# Optimizing a Trainium2 kernel written in Bass

```python
import math
import jax, jax.numpy as jnp
from jax import lax
import numpy as np

D_MODEL = 1024
BATCH = 32
SEQ = 256
DEPTH = 2
DEC_BATCH = 8
DEC_SEQ = 1024
PAST_LEN = 256

GRID_W = 64
N_LRU_LAYERS = (DEPTH + 1) // 2
N_CM_LAYERS = DEPTH // 2
LRU_WIDTH = D_MODEL
LRU_HEADS = 16
LRU_HEAD_DIM = LRU_WIDTH // LRU_HEADS
CONV_WIDTH = 4
LRU_C = 8.0
CHUNK = 128
CM_WIDTH = D_MODEL
CM_GROUPS = 8
CM_GROUP_DIM = CM_WIDTH // CM_GROUPS
D_FF = int(math.ceil(8 * D_MODEL / 3 / 256) * 256)
N_MOD = 6
EPS = 1e-6

kernel_name = "hybrid_rglru_chunkmlp_diffusion_step"


def rmsnorm(x, g):
    xf = x.astype(jnp.float32)
    y = xf * lax.rsqrt(jnp.mean(xf * xf, axis=-1, keepdims=True) + EPS)
    return (y * g.astype(jnp.float32)).astype(x.dtype)


def layernorm(x, g, b):
    xf = x.astype(jnp.float32)
    mu = jnp.mean(xf, axis=-1, keepdims=True)
    var = jnp.mean(jnp.square(xf - mu), axis=-1, keepdims=True)
    y = (xf - mu) * lax.rsqrt(var + EPS)
    return (y * g.astype(jnp.float32) + b.astype(jnp.float32)).astype(x.dtype)


def adaln(cond, w, b):
    m = jax.nn.silu(cond) @ w + b
    return m.reshape(cond.shape[0], N_MOD, D_MODEL)


def conv_centred(x, w, b):
    y = lax.conv_general_dilated(
        x, w[:, None, :].astype(x.dtype), window_strides=(1,),
        padding=[((CONV_WIDTH - 1) // 2, CONV_WIDTH - 1 - (CONV_WIDTH - 1) // 2)],
        dimension_numbers=("NWC", "WIO", "NWC"), feature_group_count=x.shape[-1])
    return y + b


def linear_scan(a, bx, h0, reverse):
    def step(h, ab):
        h = ab[0] * h + ab[1]
        return h, h
    h_last, hs = lax.scan(step, h0, (jnp.swapaxes(a, 0, 1), jnp.swapaxes(bx, 0, 1)), reverse=reverse)
    return h_last, jnp.swapaxes(hs, 0, 1)


def rglru_mixer(h, h0, w_in, conv_w, conv_b, ga_w, ga_b, gx_w, gx_b, lam, w_out):
    B, S, _ = h.shape
    x, y = jnp.split(h @ w_in, 2, axis=-1)
    y = jax.nn.gelu(y)
    x = conv_centred(x, conv_w, conv_b)
    xh = x.reshape(B, S, LRU_HEADS, LRU_HEAD_DIM)
    r = jax.nn.sigmoid(jnp.einsum('bshd,zhde->zbshe', xh, ga_w).reshape(2, B, S, LRU_WIDTH)
                       + ga_b[:, None, None, :])
    ig = jax.nn.sigmoid(jnp.einsum('bshd,zhde->zbshe', xh, gx_w).reshape(2, B, S, LRU_WIDTH)
                        + gx_b[:, None, None, :])
    log_a = -LRU_C * r.astype(jnp.float32) * jax.nn.softplus(-lam.astype(jnp.float32))[:, None, None, :]
    a = jnp.exp(log_a)
    bx = jnp.sqrt(-jnp.expm1(2.0 * log_a)) * (ig * x[None]).astype(jnp.float32)
    h0 = h0.astype(jnp.float32)
    hf_last, hf = linear_scan(a[0], bx[0], h0[:, 0], reverse=False)
    hb_last, hb = linear_scan(a[1], bx[1], h0[:, 1], reverse=True)
    out = ((hf + hb).astype(h.dtype) * y) @ w_out
    return out, jnp.stack([hf_last, hb_last], axis=1)


def chunk_mlp_mixer(h, n_chunks, w_in, b_in, ln_g, ln_b, w_s, b_s, w_out):
    B, S, _ = h.shape
    u, v = jnp.split(jax.nn.gelu(h @ w_in + b_in), 2, axis=-1)
    v = layernorm(v, ln_g, ln_b)
    vc = v.reshape(B, n_chunks, CHUNK, CM_GROUPS, CM_GROUP_DIM)
    mixed = jnp.einsum('gpq,bnqgd->bnpgd', w_s, vc) + jnp.swapaxes(b_s, 0, 1)[None, None, :, :, None]
    return (u * mixed.reshape(B, S, CM_WIDTH)) @ w_out


def swiglu(h, w_gate, w_up, w_down):
    return (jax.nn.silu(h @ w_gate) * (h @ w_up)) @ w_down


def setup_inputs(seed: int = 0) -> dict:
    key = jax.random.key(seed)
    ks = iter(jax.random.split(key, 40))

    def nrm(shape, scale):
        return jax.random.normal(next(ks), shape, jnp.float32) * scale

    def gain(shape):
        return 1.0 + nrm(shape, 0.05)

    D, W, F = D_MODEL, LRU_WIDTH, D_FF
    u = jax.random.uniform(next(ks), (N_LRU_LAYERS, 2, W), jnp.float32, 0.9, 0.999)
    s = u ** (1.0 / LRU_C)
    lam = jnp.log(s) - jnp.log1p(-s)
    return {
        "x_prompt": nrm((BATCH, SEQ, D), 1.0),
        "x_sample": nrm((DEC_BATCH, DEC_SEQ, D), 1.0),
        "state_lru": nrm((DEC_BATCH, N_LRU_LAYERS, 2, W), 1.0),
        "c": nrm((DEC_BATCH, D), 1.0),
        "c_ctx": nrm((D,), 1.0),
        "mod_w": nrm((DEPTH, D, N_MOD * D), 0.5 * D ** -0.5),
        "mod_b": nrm((DEPTH, N_MOD * D), 0.02),
        "norm_mix": gain((DEPTH, D)),
        "norm_ffn": gain((DEPTH, D)),
        "lru_w_in": nrm((N_LRU_LAYERS, D, 2 * W), D ** -0.5),
        "lru_conv_w": nrm((N_LRU_LAYERS, CONV_WIDTH, W), CONV_WIDTH ** -0.5),
        "lru_conv_b": nrm((N_LRU_LAYERS, W), 0.02),
        "lru_ga_w": nrm((N_LRU_LAYERS, 2, LRU_HEADS, LRU_HEAD_DIM, LRU_HEAD_DIM), LRU_HEAD_DIM ** -0.5),
        "lru_ga_b": nrm((N_LRU_LAYERS, 2, W), 0.02),
        "lru_gx_w": nrm((N_LRU_LAYERS, 2, LRU_HEADS, LRU_HEAD_DIM, LRU_HEAD_DIM), LRU_HEAD_DIM ** -0.5),
        "lru_gx_b": nrm((N_LRU_LAYERS, 2, W), 0.02),
        "lru_lambda": lam,
        "lru_w_out": nrm((N_LRU_LAYERS, W, D), W ** -0.5),
        "cm_w_in": nrm((N_CM_LAYERS, D, 2 * CM_WIDTH), D ** -0.5),
        "cm_b_in": nrm((N_CM_LAYERS, 2 * CM_WIDTH), 0.02),
        "cm_ln_g": gain((N_CM_LAYERS, CM_WIDTH)),
        "cm_ln_b": nrm((N_CM_LAYERS, CM_WIDTH), 0.02),
        "cm_w_s": nrm((N_CM_LAYERS, CM_GROUPS, CHUNK, CHUNK), CHUNK ** -0.5),
        "cm_b_s": gain((N_CM_LAYERS, CM_GROUPS, CHUNK)),
        "cm_w_out": nrm((N_CM_LAYERS, CM_WIDTH, D), CM_WIDTH ** -0.5),
        "ffn_w_gate": nrm((DEPTH, D, F), D ** -0.5),
        "ffn_w_up": nrm((DEPTH, D, F), D ** -0.5),
        "ffn_w_down": nrm((DEPTH, F, D), F ** -0.5),
        "final_norm": gain((D,)),
    }


def reference(x_prompt, x_sample, state_lru, c, c_ctx, mod_w, mod_b, norm_mix, norm_ffn,
              lru_w_in, lru_conv_w, lru_conv_b, lru_ga_w, lru_ga_b, lru_gx_w, lru_gx_b,
              lru_lambda, lru_w_out, cm_w_in, cm_b_in, cm_ln_g, cm_ln_b, cm_w_s, cm_b_s,
              cm_w_out, ffn_w_gate, ffn_w_up, ffn_w_down, final_norm):
    xp, xs = x_prompt, x_sample
    ctx_len = xp.shape[1]
    rows = xs.shape[1] // GRID_W
    n_chunks_ctx = ctx_len // CHUNK
    n_chunks_lat = rows * GRID_W // CHUNK
    new_lru_states = []

    for i in range(DEPTH):
        mp = adaln(c_ctx[None, :], mod_w[i], mod_b[i])[:, :, None, :]
        ms = adaln(c, mod_w[i], mod_b[i])[:, :, None, :]
        hp = rmsnorm(xp, norm_mix[i]) * (1.0 + mp[:, 1]) + mp[:, 0]
        hs = rmsnorm(xs, norm_mix[i]) * (1.0 + ms[:, 1]) + ms[:, 0]
        if i % 2 == 0:
            j = i // 2
            p = (lru_w_in[j], lru_conv_w[j], lru_conv_b[j], lru_ga_w[j], lru_ga_b[j],
                 lru_gx_w[j], lru_gx_b[j], lru_lambda[j], lru_w_out[j])
            h0 = jnp.zeros((xp.shape[0], 2, LRU_WIDTH), jnp.float32)
            op, st = rglru_mixer(hp, h0, *p)
            os_, _ = rglru_mixer(hs, state_lru[:, j], *p)
            new_lru_states.append(st)
        else:
            j = i // 2
            p = (cm_w_in[j], cm_b_in[j], cm_ln_g[j], cm_ln_b[j], cm_w_s[j], cm_b_s[j], cm_w_out[j])
            op = chunk_mlp_mixer(hp, n_chunks_ctx, *p)
            os_ = chunk_mlp_mixer(hs, n_chunks_lat, *p)
        xp = xp + mp[:, 2] * op
        xs = xs + ms[:, 2] * os_
        hp = rmsnorm(xp, norm_ffn[i]) * (1.0 + mp[:, 4]) + mp[:, 3]
        hs = rmsnorm(xs, norm_ffn[i]) * (1.0 + ms[:, 4]) + ms[:, 3]
        xp = xp + mp[:, 5] * swiglu(hp, ffn_w_gate[i], ffn_w_up[i], ffn_w_down[i])
        xs = xs + ms[:, 5] * swiglu(hs, ffn_w_gate[i], ffn_w_up[i], ffn_w_down[i])

    y_prompt = rmsnorm(xp, final_norm)
    y_sample = rmsnorm(xs, final_norm)
    new_state_lru = jnp.stack(new_lru_states, axis=1)
    return (y_prompt, y_sample, new_state_lru)
```

```python
import numpy as np
import concourse.bass as bass
import concourse.mybir as mybir
from concourse.bass_utils import run_bass_kernel_spmd
from contextlib import ExitStack

F32 = mybir.dt.float32
BF16 = mybir.dt.bfloat16
AF = mybir.ActivationFunctionType
ALU = mybir.AluOpType

NCORES = 8
D = 1024
KC = 8
T = 2048
NTT = 4
TT = 512
FF = 2816
FCH = 22
EPS = 1e-6
FGROUPS = [(0, 8), (8, 16), (16, 22)]

def R_MODB(l, j, c=0): return l * 48 + j * 8 + c
def R_NMIX(l, c=0): return 96 + l * 8 + c
def R_NFFN(l, c=0): return 112 + l * 8 + c
def R_FN(c=0): return 128 + c
def R_CONVW(k, c=0): return 136 + k * 8 + c
def R_CONVB(c=0): return 168 + c
def R_GAB(z, c=0): return 176 + z * 8 + c
def R_GXB(z, c=0): return 192 + z * 8 + c
def R_LAM(z, c=0): return 208 + z * 8 + c
def R_BINU(c=0): return 224 + c
def R_LNG(c=0): return 232 + c
def R_ST(z, c=0): return 240 + z * 8 + c
def R_C(c=0): return 256 + c
def R_CCTX(c=0): return 264 + c
NROWS = 384


class Tok:
    __slots__ = ("sid", "sem", "val")

    def __init__(self, sid, sem, val):
        self.sid, self.sem, self.val = sid, sem, val


class Res:
    __slots__ = ("w", "r")

    def __init__(self):
        self.w = None
        self.r = {}


class _H:
    def __init__(self, call):
        self.call = call

    def then_inc(self, sem, n):
        self.call.append((sem, n))
        return self


class RecEng:
    def __init__(self):
        self.calls = []

    def __getattr__(self, name):
        def f(*a, **k):
            c = [name, a, k]
            self.calls.append(c)
            return _H(c)
        return f


def replay(calls, e):
    for c in calls:
        ins = getattr(e, c[0])(*c[1], **c[2])
        for (sem, n) in c[3:]:
            ins.then_inc(sem, n)


class Eng:
    def __init__(self, name, sid, sem):
        self.name, self.sid, self.sem = name, sid, sem
        self.count = 0
        self.waited = {}
        self.q = []

    def wait(self, tok):
        if tok is None:
            return
        if self.waited.get(tok.sid, 0) >= tok.val:
            return
        self.waited[tok.sid] = tok.val
        self.q.append([["wait_ge", (tok.sem, tok.val), {}]])


class Prog:
    def __init__(self, dry):
        self.dry = dry
        self.engs = {}
        self.dsem = {}
        self.pe_fill = None
        self.nsid = 0

    def add_engine(self, name, sem):
        self.engs[name] = Eng(name, self.nsid, sem)
        self.nsid += 1

    def add_dsem(self, name, sem):
        self.dsem[name] = [self.nsid, sem, 0]
        self.nsid += 1

    def _deps(self, rd, wr, extra):
        deps = list(extra)
        for r in rd:
            if r.w is not None:
                deps.append(r.w)
        for w in wr:
            if w.w is not None:
                deps.append(w.w)
            deps.extend(w.r.values())
        return deps

    def _commit(self, tok, rd, wr):
        for r in rd:
            old = r.r.get(tok.sid)
            if old is None or old.val < tok.val:
                r.r[tok.sid] = tok
        for w in wr:
            w.w = tok
            w.r = {}

    def op(self, ename, fn, rd=(), wr=(), extra=()):
        E = self.engs[ename]
        for d in self._deps(rd, wr, extra):
            if ename == "pe" and d.sid == E.sid:
                continue
            E.wait(d)
        E.count += 1
        sem = E.sem
        rec = RecEng()
        fn(rec).then_inc(sem, 1)
        E.q.append(rec.calls)
        tok = Tok(E.sid, sem, E.count)
        self._commit(tok, rd, wr)
        return tok

    def group(self, fns, rd=(), wr=(), extra=()):
        E = self.engs["pe"]
        if self.pe_fill is not None:
            E.q.append(self.pe_fill())
        for d in self._deps(rd, wr, extra):
            if d.sid == E.sid:
                continue
            E.wait(d)
        rec = RecEng()
        for f in fns[:-1]:
            f(rec)
        E.count += 1
        sem = E.sem
        fns[-1](rec).then_inc(sem, 1)
        E.q.append(rec.calls)
        tok = Tok(E.sid, sem, E.count)
        self._commit(tok, rd, wr)
        return tok

    def dma(self, qname, semname, fn, rd=(), wr=(), extra=()):
        E = self.engs[qname]
        for d in self._deps(rd, wr, extra):
            E.wait(d)
        ds = self.dsem[semname]
        ds[2] += 16
        sem = ds[1]
        rec = RecEng()
        fn(rec).then_inc(sem, 16)
        E.q.append(rec.calls)
        tok = Tok(ds[0], sem, ds[2])
        self._commit(tok, rd, wr)
        return tok

    def fence(self, extra_toks=()):
        toks = [Tok(E.sid, E.sem, E.count) for E in self.engs.values() if E.count > 0 and E.sem is not None]
        toks += [Tok(ds[0], ds[1], ds[2]) for ds in self.dsem.values() if ds[2] > 0]
        toks += list(extra_toks)
        for E in self.engs.values():
            for t in toks:
                if t.sid != E.sid or E.name != "pe":
                    E.wait(t)


class WStream:
    NS = 4

    def __init__(self, P, A, plan):
        self.P, self.A, self.ring, self.plan = P, A, A["ring"], plan
        self.rec = []
        self.res = [Res() for _ in range(self.NS)]
        self.idx = 0
        self.issued = 0
        self.released = set()

    def _can_issue(self, k):
        return k < self.NS or (k - self.NS) in self.released

    def _issue(self, k):
        s = k % self.NS
        out_ap, in_ap = self.plan[k](self.A, self.ring[:, s, :])
        self.P.dma("pool", "ring%d" % s, lambda e: e.dma_start(out=out_ap, in_=in_ap), wr=[self.res[s]])

    def pump(self):
        if self.plan is None:
            return
        while self.issued < len(self.plan) and self._can_issue(self.issued):
            self._issue(self.issued)
            self.issued += 1

    def next(self, desc):
        self.rec.append(desc)
        i = self.idx
        self.idx += 1
        if self.plan is not None:
            self.pump()
            assert self.issued > i, "ring deadlock: piece %d cannot be issued" % i
        return self.ring[:, i % self.NS, :], self.res[i % self.NS], i

    def done(self, i):
        self.released.add(i)
        self.pump()


def emit_program(P, A, W):
    x_fm, h, S1, reg, ssb, tmpp, pcol = A["x_fm"], A["h"], A["S1"], A["reg"], A["ssb"], A["tmpp"], A["pcol"]
    banks = A["banks"]
    identF, identB, onesB = A["identF"], A["identB"], A["onesB"]

    XR = {(m, tt): Res() for m in range(KC) for tt in range(NTT)}
    HR = {(m, tt): Res() for m in range(KC) for tt in range(NTT)}
    SR = {(m, tt): Res() for m in range(KC) for tt in range(NTT)}
    BANK = [Res() for _ in range(8)]
    SSB = [Res() for _ in range(NTT)]
    TMP = [Res() for _ in range(4)]
    CONST = Res()
    pinned = set()
    bstate = {"i": 0}

    def getbank(cls=None):
        if cls is not None:
            lo, n = cls
            k = bstate.get(cls, 0)
            bstate[cls] = (k + 1) % n
            return lo + k
        while True:
            b = bstate["i"]
            bstate["i"] = (b + 1) % 8
            if b not in pinned:
                return b

    if A.get("y_in_loop", True):
        ng, nf, ny = A.get("bank_split", (3, 2, 2))
        CLS_G = (0, ng)
        CLS_F = (ng, nf)
        CLS_Y = (ng + nf, ny)
    else:
        CLS_G = (0, 4)
        CLS_F = (4, 3)
        CLS_Y = None

    tstate = {"i": 0}

    def gettmp(n=3):
        i = tstate["i"] % n
        tstate["i"] += 1
        return i

    def tsl(tt):
        return slice(tt * TT, (tt + 1) * TT)

    def grp(tt):
        return 0 if tt < 2 else 1

    P.op("pool", lambda e: e.memset(identF[:], 0.0), wr=[CONST])
    P.op("pool", lambda e: e.affine_select(out=identF[:], in_=identF[:], compare_op=ALU.not_equal, fill=1.0,
                                           base=0, pattern=[[-1, 128]], channel_multiplier=1), wr=[CONST])
    P.op("pool", lambda e: e.memset(onesB[:], 1.0), wr=[CONST])
    P.op("dve", lambda e: e.tensor_copy(out=identB[:], in_=identF[:]), rd=[CONST], wr=[CONST])
    P.op("pool", lambda e: e.memset(A["mhalf"][:], -0.5), wr=[CONST])
    P.op("pool", lambda e: e.memset(A["onesrowF"][:], 1.0), wr=[CONST])

    PT = Res()
    ptst = A["ptst"]
    P.dma("sp", "setup", lambda e: e.dma_start(out=ptst[:], in_=A["d_ptab"].rearrange("(g p) f -> p g f", p=128)), wr=[PT])
    b = getbank()
    bk = banks[b]
    P.group([(lambda e, g=g: e.transpose(out=bk[:, g * 128:(g + 1) * 128], in_=ptst[:, g, :], identity=identF[:]))
             for g in range(3)], rd=[PT, CONST], wr=[BANK[b]])
    P.op("dve", lambda e: e.tensor_copy(out=pcol[:, 0:384], in_=bk[:, 0:384]), rd=[BANK[b]], wr=[CONST])

    def col(r, n=1):
        return pcol[:, r:r + n]

    dcol = A["dcol"]
    NSP4 = 0
    HGA = 16
    HGX = 32
    H0X2 = 48
    SPT = 64
    condT = A["condT"]
    P.op("act", lambda e: e.activation(out=condT[:, :, 0], in_=col(R_CCTX(), 8), func=AF.Silu), rd=[CONST], wr=[CONST])
    P.op("act", lambda e: e.activation(out=condT[:, :, 1], in_=col(R_C(), 8), func=AF.Silu), rd=[CONST], wr=[CONST])
    P.op("act", lambda e: e.activation(out=dcol[:, SPT:SPT + 16], in_=col(R_LAM(0), 16), func=AF.Exp, scale=-1.0),
         rd=[CONST], wr=[CONST])
    P.op("act", lambda e: e.activation(out=dcol[:, SPT:SPT + 16], in_=dcol[:, SPT:SPT + 16], func=AF.Ln, bias=1.0),
         rd=[CONST], wr=[CONST])
    P.op("dve", lambda e: e.tensor_scalar(out=dcol[:, NSP4:NSP4 + 16], in0=dcol[:, SPT:SPT + 16], scalar1=-4.0, scalar2=None,
                                          op0=ALU.mult), rd=[CONST], wr=[CONST])
    P.op("dve", lambda e: e.tensor_scalar(out=dcol[:, HGA:HGA + 32], in0=col(R_GAB(0), 32), scalar1=0.5, scalar2=None,
                                          op0=ALU.mult), rd=[CONST], wr=[CONST])
    P.op("dve", lambda e: e.tensor_scalar(out=dcol[:, H0X2:H0X2 + 16], in0=col(R_ST(0), 16), scalar1=2.0, scalar2=None,
                                          op0=ALU.mult), rd=[CONST], wr=[CONST])

    modT = A["modT"]
    amod = A["amod"]
    MODR = [Res(), Res()]
    MROW = [Res(), Res()]

    MODBANK = 7
    MODCLS = [None]

    def mod_finalize(l, m0, m1, last):
        bk3 = banks[MODBANK][:, l * 96:(l + 1) * 96].rearrange("p (m g) -> p m g", g=2)
        for g in range(2):
            P.op("dve", lambda e: e.tensor_tensor(out=modT[:, l, m0:m1, g], in0=bk3[:, m0:m1, g], in1=col(R_MODB(l, 0) + m0, m1 - m0),
                                                  op=ALU.add), rd=[BANK[MODBANK], CONST], wr=[MODR[l]])
        for g in range(2):
            if m0 <= 8 and m1 >= 16:
                P.op("dve", lambda e: e.scalar_tensor_tensor(out=amod[:, l, 0, :, g], in0=modT[:, l, 8:16, g], scalar=1.0,
                                                             in1=col(R_NMIX(l), 8), op0=ALU.add, op1=ALU.mult),
                     rd=[CONST], wr=[MODR[l]])
            if m0 <= 32 and m1 >= 40:
                P.op("dve", lambda e: e.scalar_tensor_tensor(out=amod[:, l, 1, :, g], in0=modT[:, l, 32:40, g], scalar=1.0,
                                                             in1=col(R_NFFN(l), 8), op0=ALU.add, op1=ALU.mult),
                     rd=[CONST], wr=[MODR[l]])
        if last:
            pinned.discard(MODBANK)

    def mod_pieces(l, split=None, unpin=True):
        b = MODBANK
        pinned.add(b)
        bk = banks[b]
        mrow = A["mrow"]
        prev_finish = None
        for pi in range(24):
            def desc(A_, slot, l=l, pi=pi):
                return (slot.rearrange("p (kc n) -> p kc n", n=256),
                        A_["d_modw"][l, :, pi * 256:(pi + 1) * 256].rearrange("(kc p) n -> p kc n", p=128))
            wv, wres, wi = W.next(desc)
            wv3 = wv.rearrange("p (kc n) -> p kc n", n=256)
            b2 = getbank(MODCLS[0])
            P.group([(lambda e, kc=kc: e.matmul(banks[b2][0:2, 0:256], lhsT=condT[:, kc, :], rhs=wv3[:, kc, :],
                                                start=(kc == 0), stop=(kc == KC - 1))) for kc in range(KC)],
                    rd=[wres, CONST], wr=[BANK[b2]])
            W.done(wi)
            q = pi % 2
            P.op("dve", lambda e: e.tensor_copy(out=mrow[0:2, q * 256:(q + 1) * 256], in_=banks[b2][0:2, 0:256]),
                 rd=[BANK[b2]], wr=[MROW[q]])

            def finish(pi=pi, q=q):
                c0 = l * 96 + 2 * (2 * pi)
                P.group([(lambda e, mm=mm: e.transpose(out=bk[:, c0 + 2 * mm:c0 + 2 * mm + 2],
                                                       in_=mrow[0:2, q * 256 + mm * 128:q * 256 + (mm + 1) * 128],
                                                       identity=identF[0:2, 0:2])) for mm in range(2)],
                        rd=[MROW[q], CONST], wr=[BANK[b]])
            if prev_finish is not None:
                prev_finish()
            prev_finish = finish
            if split is not None and pi == split - 1:
                prev_finish()
                prev_finish = None
                mod_finalize(l, 0, 2 * split, False)
            yield
        if prev_finish is not None:
            prev_finish()
        mod_finalize(l, 0 if split is None else 2 * split, 48, unpin)
        yield

    xstg = [reg[:, i * 1024:(i + 1) * 1024] for i in range(4)]
    XSTG = [Res() for _ in range(4)]

    def prologue(side=None, nside=0):
        for j in range(16):
            s = j % 4
            P.dma("sp", "xs%d" % s, lambda e, j=j, s=s: e.dma_start(out=xstg[s], in_=A["d_xin"][j * 128:(j + 1) * 128, :]),
                  wr=[XSTG[s]])
            for hb in range(2):
                b = getbank()
                bk = banks[b]
                P.group([(lambda e, q=q, hb=hb, s=s, bk=bk: e.transpose(out=bk[:, q * 128:(q + 1) * 128],
                                                                         in_=xstg[s][:, (hb * 4 + q) * 128:(hb * 4 + q + 1) * 128],
                                                                         identity=identF[:])) for q in range(4)],
                        rd=[XSTG[s], CONST], wr=[BANK[b]])
                tt = j // 4
                dst = x_fm[:, hb * 4:hb * 4 + 4, j * 128:(j + 1) * 128]
                src = bk[:].rearrange("p (a b) -> p a b", b=128)
                eng = "dve" if hb == 0 else "act"
                if eng == "dve":
                    P.op("dve", lambda e, dst=dst, src=src: e.tensor_copy(out=dst, in_=src), rd=[BANK[b]],
                         wr=[XR[(m, tt)] for m in range(hb * 4, hb * 4 + 4)])
                else:
                    P.op("act", lambda e, dst=dst, src=src: e.activation(out=dst, in_=src, func=AF.Copy), rd=[BANK[b]],
                         wr=[XR[(m, tt)] for m in range(hb * 4, hb * 4 + 4)])
            if side is not None and j < nside:
                next(side, None)
            if j % 4 == 3:
                norm_stats(j // 4)

    def norm_stats(tt):
        P.op("act", lambda e: e.activation(out=h[:, :, tsl(tt)], in_=x_fm[:, :, tsl(tt)], func=AF.Square),
             rd=[XR[(m, tt)] for m in range(KC)], wr=[HR[(m, tt)] for m in range(KC)])
        b = getbank()
        bk = banks[b]
        P.group([(lambda e, kc=kc: e.matmul(bk[:], lhsT=onesB[:], rhs=h[:, kc, tsl(tt)], start=(kc == 0),
                                            stop=(kc == KC - 1))) for kc in range(KC)],
                rd=[HR[(m, tt)] for m in range(KC)] + [CONST], wr=[BANK[b]])
        P.op("dve", lambda e: e.tensor_scalar(out=ssb[:, tsl(tt)], in0=bk[:], scalar1=1.0 / D, scalar2=EPS,
                                              op0=ALU.mult, op1=ALU.add), rd=[BANK[b]], wr=[SSB[tt]])
        P.op("act", lambda e: e.activation(out=ssb[:, tsl(tt)], in_=ssb[:, tsl(tt)], func=AF.Ln), wr=[SSB[tt]])
        P.op("act", lambda e: e.activation(out=ssb[:, tsl(tt)], in_=ssb[:, tsl(tt)], func=AF.Exp, scale=-0.5), wr=[SSB[tt]])

    def norm_apply(l, kind, tt):
        jsh = 0 if kind == 0 else 3
        g = grp(tt)
        for kc in range(KC):
            ti = gettmp(3)
            tv = tmpp[:, ti, :]
            P.op("dve", lambda e: e.scalar_tensor_tensor(
                out=tv, in0=x_fm[:, kc, tsl(tt)], scalar=amod[:, l, kind, kc, g:g + 1], in1=ssb[:, tsl(tt)],
                op0=ALU.mult, op1=ALU.mult), rd=[XR[(kc, tt)], SSB[tt], MODR[l]], wr=[TMP[ti]])
            P.op("act", lambda e: e.activation(
                out=h[:, kc, tsl(tt)], in_=tv, func=AF.Identity, bias=modT[:, l, jsh * 8 + kc, g:g + 1], scale=1.0),
                rd=[TMP[ti], MODR[l]], wr=[HR[(kc, tt)]])

    def norm_tile(l, kind, tt):
        norm_stats(tt)
        norm_apply(l, kind, tt)

    def norm_mod(l, kind):
        for tt in range(NTT):
            norm_tile(l, kind, tt)

    def k1024_desc(wkey, c0):
        key, li = wkey
        def desc(A_, slot):
            w2 = A_[key] if li is None else A_[key][li]
            return (slot.rearrange("p (kc n) -> p kc n", n=256),
                    w2[:, c0:c0 + 256].rearrange("(kc p) n -> p kc n", p=128))
        return desc

    def proj_piece(wap2d, c0, src, SRC, evac):
        wv, wres, wi = W.next(k1024_desc(wap2d, c0))
        wv3 = wv.rearrange("p (kc n) -> p kc n", n=256)
        for mm in range(2):
            for tt in range(NTT):
                b = getbank()
                bk = banks[b]
                P.group([(lambda e, kc=kc, mm=mm, tt=tt, bk=bk: e.matmul(bk[:], lhsT=wv3[:, kc, mm * 128:(mm + 1) * 128],
                                                                        rhs=src[:, kc, tsl(tt)], start=(kc == 0), stop=(kc == KC - 1)))
                         for kc in range(KC)], rd=[wres] + [SRC[(kc, tt)] for kc in range(KC)], wr=[BANK[b]])
                evac(mm, tt, b)
        W.done(wi)

    def resid_evac(l, jgate, m, tt, b):
        g = grp(tt)
        bk = banks[b]
        P.op("dve", lambda e: e.scalar_tensor_tensor(out=x_fm[:, m, tsl(tt)], in0=bk[:], scalar=modT[:, l, jgate * 8 + m, g:g + 1],
                                                     in1=x_fm[:, m, tsl(tt)], op0=ALU.mult, op1=ALU.add),
             rd=[BANK[b], MODR[l]], wr=[XR[(m, tt)]])

    def lagged(after_tt):
        def f(tt, pi):
            if after_tt is None:
                return
            if pi == 1 and tt >= 1:
                after_tt(tt - 1)
            if pi == 3 and tt == NTT - 1:
                after_tt(tt)
        return f

    def out_proj(l, wap2d, after_tt=None):
        pcs = [W.next(k1024_desc(wap2d, pi * 256)) for pi in range(4)]
        lag = lagged(after_tt)
        for tt in range(NTT):
            for pi in range(4):
                wv3 = pcs[pi][0].rearrange("p (kc n) -> p kc n", n=256)
                for mm in range(2):
                    b = getbank()
                    P.group([(lambda e, kc=kc: e.matmul(banks[b][:], lhsT=wv3[:, kc, mm * 128:(mm + 1) * 128],
                                                        rhs=S1[:, kc, tsl(tt)], start=(kc == 0), stop=(kc == KC - 1)))
                             for kc in range(KC)], rd=[pcs[pi][1]] + [SR[(kc, tt)] for kc in range(KC)], wr=[BANK[b]])
                    resid_evac(l, 2, pi * 2 + mm, tt, b)
                lag(tt, pi)
        for pc in pcs:
            W.done(pc[2])

    def ffn(l, side=None, after_tt=None):
        wg, wu = ("d_ffn_g", l), ("d_ffn_u", l)
        for (j0, j1) in FGROUPS:
            nj = j1 - j0
            for jp in range(j0, j1, 2):
                wgv, wgres, wgi = W.next(k1024_desc(wg, jp * 128))
                wuv, wures, wui = W.next(k1024_desc(wu, jp * 128))
                wg3 = wgv.rearrange("p (kc n) -> p kc n", n=256)
                wu3 = wuv.rearrange("p (kc n) -> p kc n", n=256)
                for mm in range(2):
                    jj = jp + mm - j0
                    for tt in range(NTT):
                        bg = getbank()
                        bu = getbank()
                        P.group([(lambda e, kc=kc, mm=mm, tt=tt, bg=bg: e.matmul(banks[bg][:], lhsT=wg3[:, kc, mm * 128:(mm + 1) * 128],
                                                                                rhs=h[:, kc, tsl(tt)], start=(kc == 0), stop=(kc == KC - 1)))
                                 for kc in range(KC)], rd=[wgres] + [HR[(kc, tt)] for kc in range(KC)], wr=[BANK[bg]])
                        P.group([(lambda e, kc=kc, mm=mm, tt=tt, bu=bu: e.matmul(banks[bu][:], lhsT=wu3[:, kc, mm * 128:(mm + 1) * 128],
                                                                                rhs=h[:, kc, tsl(tt)], start=(kc == 0), stop=(kc == KC - 1)))
                                 for kc in range(KC)], rd=[wures] + [HR[(kc, tt)] for kc in range(KC)], wr=[BANK[bu]])
                        ti = gettmp(4)
                        sg = tmpp[:, ti, 0:256].bitcast(BF16)
                        P.op("act", lambda e, bg=bg, sg=sg: e.activation(out=sg, in_=banks[bg][:], func=AF.Silu),
                             rd=[BANK[bg]], wr=[TMP[ti]])
                        P.op("dve", lambda e, bu=bu, sg=sg, jj=jj, tt=tt: e.tensor_tensor(out=S1[:, jj, tsl(tt)], in0=banks[bu][:], in1=sg,
                                                                                         op=ALU.mult),
                             rd=[BANK[bu], TMP[ti]], wr=[SR[(jj, tt)]])
                W.done(wgi)
                W.done(wui)
                if side is not None:
                    for _ in range(3):
                        next(side, None)
            dpcs = []
            for pi in range(4):
                def desc(A_, slot, pi=pi, j0=j0, nj=nj, l=l):
                    return (slot[:, 0:nj * 256].rearrange("p (jj n) -> p jj n", n=256),
                            A_["d_ffn_d"][l][j0 * 128:(j0 + nj) * 128, pi * 256:(pi + 1) * 256].rearrange("(jj p) n -> p jj n", p=128))
                dpcs.append(W.next(desc))
            last = (j1 == FCH)
            lag = lagged(after_tt if last else None)
            order = [(tt, pi) for tt in range(NTT) for pi in range(4)] if last else [(tt, pi) for pi in range(4) for tt in range(NTT)]
            for oi, (tt, pi) in enumerate(order):
                wv, wres, wi = dpcs[pi]
                wv3 = wv[:, 0:nj * 256].rearrange("p (jj n) -> p jj n", n=256)
                for mm in range(2):
                    m = pi * 2 + mm
                    b = getbank()
                    P.group([(lambda e, jj=jj: e.matmul(banks[b][:], lhsT=wv3[:, jj, mm * 128:(mm + 1) * 128],
                                                        rhs=S1[:, jj, tsl(tt)], start=(jj == 0), stop=(jj == nj - 1)))
                             for jj in range(nj)], rd=[wres] + [SR[(jj, tt)] for jj in range(nj)], wr=[BANK[b]])
                    resid_evac(l, 5, m, tt, b)
                if last:
                    lag(tt, pi)
                if not last and tt == NTT - 1:
                    W.done(wi)
            if last:
                for pc in dpcs:
                    W.done(pc[2])
        if side is not None:
            for _ in side:
                pass

    def lru_mixer(l):
        w_in = ("d_lru_w_in", None)
        o = 0
        xrp = reg[:, o:o + 520].bitcast(BF16); o += 520
        xrs = reg[:, o:o + 520].bitcast(BF16); o += 520
        xcb = [reg[:, o:o + 512].bitcast(BF16), reg[:, o + 512:o + 1024].bitcast(BF16)]; o += 1024
        T1both = reg[:, o:o + 2048]
        T1 = [reg[:, o:o + 1024], reg[:, o + 1024:o + 2048]]; o += 2048
        T3 = [reg[:, o:o + 1024], reg[:, o + 1024:o + 2048]]; o += 2048
        T2both = reg[:, o:o + 2048]
        T2 = [reg[:, o:o + 1024], reg[:, o + 1024:o + 2048]]; o += 2048
        dcv = reg[:, o:o + 512].bitcast(BF16).rearrange("p (u k n) -> p u k n", u=2, k=4); o += 512
        HS = [reg[:, o:o + 1024], T2[1]]; o += 1024
        assert o <= 9744
        xc = [ssb[:, 0:1024], ssb[:, 1024:2048]]
        gwt = tmpp[:].rearrange("p a b -> p (a b)").bitcast(BF16).rearrange("p (g cc i n) -> p g cc i n", g=2, cc=4, i=4)
        XRP, XRS = Res(), Res()
        XCR, XCBR = [[Res(), Res()], [Res(), Res()]], [[Res(), Res()], [Res(), Res()]]
        T1R = [[Res(), Res()], [Res(), Res()]]
        T3R = [[Res(), Res()], [Res(), Res()]]
        T2R = [[Res(), Res()], [Res(), Res()]]
        HSR = [[Res(), Res()], T2R[1]]
        DCV = [Res(), Res()]
        GWR = Res()
        LRES.extend([XRP, XRS] + XCBR[0] + XCBR[1] + T1R[0] + T1R[1] + T3R[0] + T3R[1] + T2R[0] + T2R[1] + HSR[0] + DCV)
        xrp3 = xrp[:, 0:1036].rearrange("p (s w) -> p s w", w=259)

        P.op("pool", lambda e: e.memset(xrp[:], 0.0), wr=[XRP])
        P.op("pool", lambda e: e.memset(xrs[:], 0.0), wr=[XRS])
        for g in range(2):
            P.dma("pool", "gwld", lambda e, g=g: e.dma_start(out=gwt[:, g].rearrange("p cc i n -> p (cc i n)"), in_=A["d_gwh"][g]),
                  wr=[GWR] + TMP)

        side1 = MODGEN[0]
        YINL = A.get("y_in_loop", True)
        MODCLS[0] = CLS_Y
        if not YINL:
            for pi in range(4):
                def evac(mm, tt, b, pi=pi):
                    m = pi * 2 + mm
                    P.op("act", lambda e: e.activation(out=S1[:, m, tsl(tt)], in_=banks[b][:], func=AF.Gelu_apprx_tanh),
                         rd=[BANK[b]], wr=[SR[(m, tt)]])
                proj_piece(w_in, 1024 + pi * 256, h, HR, evac)
                for _ in range(4):
                    next(side1, None)

        def y_part(c, tts):
            def ydesc(A_, slot):
                return (slot[:, 0:1024].rearrange("p (kc n) -> p kc n", n=128),
                        A_["d_lru_w_in"][:, 1024 + c * 128:1024 + (c + 1) * 128].rearrange("(kc p) n -> p kc n", p=128))
            wyv, wyres, wyi = W.next(ydesc)
            wy3 = wyv[:, 0:1024].rearrange("p (kc n) -> p kc n", n=128)
            ybanks = []
            for tt in tts:
                b = getbank(CLS_Y)
                ybanks.append(b)
                P.group([(lambda e, kc=kc: e.matmul(banks[b][:], lhsT=wy3[:, kc, :], rhs=h[:, kc, tsl(tt)],
                                                    start=(kc == 0), stop=(kc == KC - 1))) for kc in range(KC)],
                        rd=[wyres] + [HR[(kc, tt)] for kc in range(KC)], wr=[BANK[b]])
            W.done(wyi)

            def evac():
                for tt, b in zip(tts, ybanks):
                    P.op("act", lambda e: e.activation(out=S1[:, c, tsl(tt)], in_=banks[b][:], func=AF.Gelu_apprx_tanh),
                         rd=[BANK[b]], wr=[SR[(c, tt)]])
            return evac

        stfm = A["stfm"]
        STR = Res()
        units = [(c, gi) for c in range(KC) for gi in range(2)]
        NU = len(units)
        st = {}
        wxidx = {}

        def front_mm(ui):
            c, gi = units[ui]
            u = c % 2
            tts = (0, 1) if gi == 0 else (2, 3)
            def wxdesc(A_, slot, c=c):
                return (slot[:, 0:1024].rearrange("p (kc n) -> p kc n", n=128),
                        A_["d_lru_w_in"][:, c * 128:(c + 1) * 128].rearrange("(kc p) n -> p kc n", p=128))
            wxv, wxres, wxi = W.next(wxdesc)
            st["wx"] = (wxv[:, 0:1024].rearrange("p (kc n) -> p kc n", n=128), wxres, wxi)
            if gi == 0:
                for k in range(4):
                    P.op("dve", lambda e, k=k: e.tensor_scalar(out=dcv[:, u, k, :], in0=identF[:], scalar1=col(R_CONVW(k, c)),
                                                               scalar2=None, op0=ALU.mult), rd=[CONST], wr=[DCV[u]])
            wx3, wxres, wxi = st["wx"]
            for tt in tts:
                b = getbank(CLS_F)
                P.group([(lambda e, kc=kc: e.matmul(banks[b][:], lhsT=wx3[:, kc, :],
                                                    rhs=h[:, kc, tsl(tt)], start=(kc == 0), stop=(kc == KC - 1)))
                         for kc in range(KC)], rd=[wxres] + [HR[(kc, tt)] for kc in range(KC)], wr=[BANK[b]])
                if gi == 0:
                    dst = xrp3[:, 2 * tt:2 * tt + 2, 1:257]
                    src = banks[b][:].rearrange("p (s w) -> p s w", w=256)
                    P.op("dve", lambda e: e.tensor_copy(out=dst, in_=src), rd=[BANK[b]], wr=[XRP])
                else:
                    dst = xrs[:, 1 + (tt - 2) * TT:1 + (tt - 1) * TT]
                    P.op("dve", lambda e: e.tensor_copy(out=dst, in_=banks[b][:]), rd=[BANK[b]], wr=[XRS])
            W.done(wxi)

        def front_conv(ui):
            c, gi = units[ui]
            p = ui % 2
            u = c % 2
            tts = (0, 1) if gi == 0 else (2, 3)
            for ti_, tt in enumerate(tts):
                b = getbank(CLS_F)
                if gi == 0:
                    rhs_k = lambda k: xrp3[:, 2 * tt:2 * tt + 2, k:k + 256]
                    outv = banks[b][:].rearrange("p (s w) -> p s w", w=256)
                else:
                    rhs_k = lambda k: xrs[:, (tt - 2) * TT + k:(tt - 2) * TT + k + TT]
                    outv = banks[b][:]
                P.group([(lambda e, k=k: e.matmul(outv, lhsT=dcv[:, u, k, :], rhs=rhs_k(k), start=(k == 0), stop=(k == 3)))
                         for k in range(4)], rd=[XRP if gi == 0 else XRS, DCV[u]], wr=[BANK[b]])
                hs = slice(ti_ * TT, (ti_ + 1) * TT)
                if A.get("xcevac_eng", "dve") == "dve":
                    P.op("dve", lambda e: e.tensor_scalar(out=xc[p][:, hs], in0=banks[b][:], scalar1=col(R_CONVB(c)),
                                                          scalar2=None, op0=ALU.add), rd=[BANK[b], CONST], wr=[XCR[p][ti_], SSB[2 * p + ti_]])
                else:
                    P.op("act", lambda e: e.activation(out=xc[p][:, hs], in_=banks[b][:], func=AF.Identity, bias=col(R_CONVB(c)),
                                                       scale=1.0), rd=[BANK[b], CONST], wr=[XCR[p][ti_], SSB[2 * p + ti_]])

        def front_cast(ui):
            p = ui % 2
            if A.get("cast_one", True) and A.get("cast_eng", "act") == "act":
                P.op("act", lambda e: e.activation(out=xcb[p][:], in_=xc[p][:], func=AF.Copy), rd=XCR[p], wr=XCBR[p])
                return
            for ti_ in range(2):
                hs = slice(ti_ * TT, (ti_ + 1) * TT)
                if A.get("cast_eng", "act") == "act":
                    P.op("act", lambda e: e.activation(out=xcb[p][:, hs], in_=xc[p][:, hs], func=AF.Copy),
                         rd=[XCR[p][ti_]], wr=[XCBR[p][ti_]])
                else:
                    P.op(A.get("cast_eng"), lambda e: e.tensor_copy(out=xcb[p][:, hs], in_=xc[p][:, hs]),
                         rd=[XCR[p][ti_]], wr=[XCBR[p][ti_]])

        def back_act(ui, z):
            c, gi = units[ui]
            p = ui % 2
            nwarm = A.get("warm", 0)
            if nwarm and ui < NU - 2:
                E = P.engs["pe"]
                rec = RecEng()
                for _ in range(nwarm):
                    rec.matmul(banks[7][:, 128:512], lhsT=onesB[:, :], rhs=h[:, 0, 0:384], start=True, stop=True)
                E.q.append(rec.calls)
            g4 = gwt[:, c // 4, c % 4]
            for (idx, dst, dstR, bcol) in ((z, T1[z], T1R[z], HGA), (2 + z, T3[z], T3R[z], HGX)):
                bb = [getbank(CLS_G), getbank(CLS_G)]
                for ti_ in range(2):
                    hs = slice(ti_ * TT, (ti_ + 1) * TT)
                    P.group([lambda e: e.matmul(banks[bb[ti_]][:], lhsT=g4[:, idx, :], rhs=xcb[p][:, hs], start=True, stop=True)],
                            rd=[GWR, XCBR[p][ti_]] + TMP, wr=[BANK[bb[ti_]]])
                for ti_ in range(2):
                    hs = slice(ti_ * TT, (ti_ + 1) * TT)
                    P.op("act", lambda e: e.activation(out=dst[:, hs], in_=banks[bb[ti_]][:], func=AF.Tanh,
                                                       bias=dcol[:, bcol + z * 8 + c:bcol + z * 8 + c + 1], scale=0.5),
                         rd=[BANK[bb[ti_]], CONST], wr=[dstR[ti_]])
            nsp = dcol[:, NSP4 + z * 8 + c:NSP4 + z * 8 + c + 1]
            P.op("act", lambda e: e.activation(out=T1[z][:], in_=T1[z][:], func=AF.Exp, bias=nsp, scale=nsp),
                 rd=[CONST], wr=T1R[z])
            if A.get("sq_merge", True):
                pass
            elif A.get("e2_eng", "act") == "act":
                P.op("act", lambda e: e.activation(out=T2[z][:], in_=T1[z][:], func=AF.Square), rd=T1R[z], wr=T2R[z])
            else:
                P.op("pool", lambda e: e.tensor_tensor(out=T2[z][:], in0=T1[z][:], in1=T1[z][:], op=ALU.mult), rd=T1R[z], wr=T2R[z])

        def back_sqrt(ui, z):
            if A.get("sq_merge", True):
                if z == 0:
                    P.op("act", lambda e: e.activation(out=T2both, in_=T1both, func=AF.Square),
                         rd=T1R[0] + T1R[1], wr=T2R[0] + T2R[1])
                    if A.get("sqrt_split", True):
                        for zz in range(2):
                            P.op("act", lambda e: e.activation(out=T2[zz][:], in_=T2[zz][:], func=AF.Sqrt, bias=1.0, scale=-1.0),
                                 wr=T2R[zz])
                    else:
                        P.op("act", lambda e: e.activation(out=T2both, in_=T2both, func=AF.Sqrt, bias=1.0, scale=-1.0),
                             wr=T2R[0] + T2R[1])
                return
            P.op("act", lambda e: e.activation(out=T2[z][:], in_=T2[z][:], func=AF.Sqrt, bias=1.0, scale=-1.0), wr=T2R[z])

        def back_gi(ui, z):
            c, gi = units[ui]
            p = ui % 2
            P.op("dve", lambda e: e.scalar_tensor_tensor(out=T3[z][:], in0=T3[z][:], scalar=1.0, in1=xc[p][:], op0=ALU.add,
                                                         op1=ALU.mult), rd=XCR[p] + [SSB[2 * p], SSB[2 * p + 1]], wr=T3R[z])

        def back_dve(ui, z):
            c, gi = units[ui]
            p = ui % 2
            P.op("dve", lambda e: e.tensor_tensor(out=T3[z][:], in0=T3[z][:], in1=T2[z][:], op=ALU.mult), rd=T2R[z], wr=T3R[z])
            if gi == 0:
                a4 = T1[z][:].rearrange("p (s w) -> p s w", w=256)
                P.op("dve", lambda e: e.memset(a4[:, :, 0 if z == 0 else 255], 0.0), wr=T1R[z])
                sl_ = slice(None) if z == 0 else slice(None, None, -1)
                P.op("dve", lambda e: e.tensor_tensor_scan(out=HS[z][:, sl_], data0=T1[z][:, sl_], data1=T3[z][:, sl_],
                                                           initial=0.0, op0=ALU.mult, op1=ALU.add),
                     rd=T1R[z] + T3R[z], wr=HSR[z])
                h4 = HS[z][:].rearrange("p (s w) -> p s w", w=256)
                P.op("pool", lambda e: e.tensor_scalar(out=stfm[:, c, :, z], in0=h4[:, :, 255 if z == 0 else 0], scalar1=0.5,
                                                       scalar2=None, op0=ALU.mult), rd=HSR[z], wr=[STR])
            else:
                sl_ = slice(None) if z == 0 else slice(None, None, -1)
                P.op("dve", lambda e: e.tensor_tensor_scan(out=HS[z][:, sl_], data0=T1[z][:, sl_], data1=T3[z][:, sl_],
                                                           initial=dcol[:, H0X2 + z * 8 + c:H0X2 + z * 8 + c + 1],
                                                           op0=ALU.mult, op1=ALU.add),
                     rd=T1R[z] + T3R[z] + [CONST], wr=HSR[z])

        def combine_add(ui):
            P.op(A.get("add_eng", "dve"), lambda e: e.tensor_tensor(out=HS[0][:], in0=HS[0][:], in1=HS[1][:], op=ALU.add), rd=HSR[1], wr=HSR[0])

        def combine_stt(ui):
            c, gi = units[ui]
            tts = (0, 1) if gi == 0 else (2, 3)
            tsl2 = slice(gi * 1024, (gi + 1) * 1024)
            P.op("dve", lambda e: e.scalar_tensor_tensor(out=S1[:, c, tsl2], in0=HS[0][:], scalar=0.5, in1=S1[:, c, tsl2],
                                                         op0=ALU.mult, op1=ALU.mult),
                 rd=HSR[0], wr=[SR[(c, tts[0])], SR[(c, tts[1])]])

        front_mm(0)
        front_conv(0)
        front_cast(0)
        front_mm(1)
        sched = A.get("lru_sched", "a0 g0 fc a1 cs g1 fm md sq ca d0 d1 ad").split()
        nfill = A.get("fill", 0)

        def filler():
            rec = RecEng()
            for _ in range(nfill):
                rec.matmul(banks[7][:, 128:512], lhsT=onesB[:, :], rhs=h[:, 0, 0:384], start=True, stop=True)
            return rec.calls
        if YINL:
            y_part(0, (0, 1))()
            y_part(0, (2, 3))()
        yev = [None]
        for ui in range(NU):
            P.pe_fill = filler if (nfill and ui < NU - 2) else None
            if "md" not in sched:
                for _ in range(3 if YINL else (2 if ui % 2 == 0 else 1)):
                    next(side1, None)
            for tok in sched:
                if tok == "md":
                    for _ in range(3):
                        next(side1, None)
                elif tok == "ym":
                    if YINL and units[ui][0] + 1 < KC:
                        yev[0] = y_part(units[ui][0] + 1, (0, 1) if units[ui][1] == 0 else (2, 3))
                elif tok == "a0":
                    back_act(ui, 0)
                elif tok == "a1":
                    back_act(ui, 1)
                elif tok == "g0":
                    back_gi(ui, 0)
                elif tok == "g1":
                    back_gi(ui, 1)
                elif tok == "fc" and ui + 1 < NU:
                    front_conv(ui + 1)
                elif tok == "sq":
                    back_sqrt(ui, 0)
                    back_sqrt(ui, 1)
                elif tok == "s0":
                    back_sqrt(ui, 0)
                elif tok == "s1":
                    back_sqrt(ui, 1)
                elif tok == "ca" and ui + 1 < NU:
                    front_cast(ui + 1)
                elif tok == "cs" and ui >= 1:
                    combine_stt(ui - 1)
                elif tok == "d0":
                    back_dve(ui, 0)
                elif tok == "d1":
                    back_dve(ui, 1)
                elif tok == "fm" and ui + 2 < NU:
                    front_mm(ui + 2)
                elif tok == "ad":
                    combine_add(ui)
            if YINL and units[ui][0] + 1 < KC:
                if yev[0] is None:
                    yev[0] = y_part(units[ui][0] + 1, (0, 1) if units[ui][1] == 0 else (2, 3))
                yev[0]()
                yev[0] = None
        P.pe_fill = None
        MODCLS[0] = None
        combine_stt(NU - 1)
        if side1 is not None:
            for _ in side1:
                pass

        strow = xc[0][0:8, :]
        for half in range(2):
            b = getbank()
            P.group([(lambda e, c4=c4: e.transpose(out=banks[b][0:8, c4 * 128:(c4 + 1) * 128],
                                                   in_=stfm[:, half * 4 + c4, :, :].rearrange("p s z -> p (s z)"),
                                                   identity=identF[:])) for c4 in range(4)], rd=[STR, CONST], wr=[BANK[b]])
            P.op("dve", lambda e: e.tensor_copy(out=strow[:, half * 512:(half + 1) * 512], in_=banks[b][0:8, :]),
                 rd=[BANK[b]], wr=XCR[0] + [SSB[0], SSB[1]])
        A["_st_tok"] = P.dma("sp", "stout", lambda e: e.dma_start(out=A["d_stout"][:, :], in_=strow[:]), rd=XCR[0] + [SSB[0], SSB[1]])
        out_proj(l, ("d_lru_w_out", None), after_tt=lambda tt: norm_tile(l, 1, tt))

    def cm_layout():
        o = 0
        L = {}
        L["wv"] = reg[:, o:o + 4096].bitcast(BF16).rearrange("p (kc n) -> p kc n", n=1024); o += 4096
        L["Gf"] = reg[:, o:o + 1024].rearrange("p (g n) -> p g n", n=128); o += 1024
        L["Cc"] = reg[:, o:o + 1024].rearrange("p (g n) -> p g n", n=128); o += 1024
        L["wsT"] = reg[:, o:o + 512].bitcast(BF16).rearrange("p (g n) -> p g n", n=128); o += 512
        L["vg"] = [reg[:, o:o + 1024], reg[:, o + 1024:o + 2048]]; o += 2048
        L["vtm"] = [reg[:, o:o + 512].bitcast(BF16), reg[:, o + 512:o + 1024].bitcast(BF16)]; o += 1024
        return L

    CMR = {"wv": Res(), "setup": Res(), "vg": [Res(), Res()], "vtm": [Res(), Res()]}
    LRES = []

    def cm_setup():
        L = cm_layout()
        wsn = ssb[:, 0:1024].rearrange("p (g n) -> p g n", n=128)
        lnb_row = L["vg"][0][0:1, :]
        bs_row = L["vg"][1][0:1, :].rearrange("p (g n) -> p g n", n=128)
        rs_row = reg[0:1, 8704:9728].rearrange("p (g n) -> p g n", n=128)
        ROWS = Res()
        for q in range(4):
            P.dma("pool", "wvld", lambda e, q=q: e.dma_start(out=L["wv"][:, :, q * 256:(q + 1) * 256],
                                                              in_=A["d_cm_w_in"][:, 1024 + q * 256:1024 + (q + 1) * 256].rearrange(
                                                                  "(kc p) n -> p kc n", p=128)), wr=[CMR["wv"]] + LRES)
        P.dma("sp", "setup2", lambda e: e.dma_start(out=wsn, in_=A["d_cm_w_s"].rearrange("g p q -> p g q")), wr=SSB)
        P.dma("sp", "setup2", lambda e: e.dma_start(out=lnb_row, in_=A["d_cm_ln_b"][None, :]), wr=[ROWS] + LRES)
        P.dma("sp", "setup2", lambda e: e.dma_start(out=bs_row, in_=A["d_cm_b_s"][None, :, :]), wr=[ROWS] + LRES)
        tot = Tok(P.dsem["setup2"][0], P.dsem["setup2"][1], P.dsem["setup2"][2])
        for r in SSB + [ROWS]:
            r.w = tot
        for _ in range(A.get("cm_setup_delay", 6)):
            yield
        for half in range(2):
            b = getbank()
            P.group([(lambda e, g=g, b=b, half=half: e.transpose(out=banks[b][:, g * 128:(g + 1) * 128], in_=wsn[:, half * 4 + g, :],
                                                                 identity=identF[:])) for g in range(4)], rd=SSB + [CONST], wr=[BANK[b]])
            P.op("dve", lambda e, b=b, half=half: e.tensor_copy(out=L["wsT"][:, half * 4:half * 4 + 4, :],
                                                                in_=banks[b][:].rearrange("p (g n) -> p g n", n=128)),
                 rd=[BANK[b]], wr=[CMR["setup"]] + LRES)
        for half in range(2):
            b = getbank()
            P.group([(lambda e, g=g, b=b, half=half: e.matmul(banks[b][0:1, g * 128:(g + 1) * 128], lhsT=onesB[:, 0:1],
                                                              rhs=L["wsT"][:, half * 4 + g, :], start=True, stop=True)) for g in range(4)],
                    rd=[CMR["setup"], CONST], wr=[BANK[b]])
            P.op("dve", lambda e, b=b, half=half: e.tensor_copy(out=rs_row[:, half * 4:half * 4 + 4, :],
                                                                in_=banks[b][0:1, :].rearrange("p (g n) -> p g n", n=128)),
                 rd=[BANK[b]], wr=[ROWS] + LRES)
        onesrowF = A["onesrowF"]
        for half in range(2):
            b = getbank()
            fns = []
            for g4 in range(4):
                g = half * 4 + g4
                fns.append(lambda e, g=g, g4=g4, b=b: e.matmul(banks[b][:, g4 * 128:(g4 + 1) * 128], lhsT=lnb_row[:, g * 128:(g + 1) * 128],
                                                               rhs=rs_row[:, g, :], start=True, stop=False, skip_group_check=True))
                fns.append(lambda e, g=g, g4=g4, b=b: e.matmul(banks[b][:, g4 * 128:(g4 + 1) * 128], lhsT=onesrowF[0:1, :],
                                                               rhs=bs_row[:, g, :], start=False, stop=True, skip_group_check=True))
            P.group(fns, rd=[ROWS, CONST], wr=[BANK[b]])
            P.op("dve", lambda e, b=b, half=half: e.tensor_copy(out=L["Cc"][:, half * 4:half * 4 + 4, :],
                                                                in_=banks[b][:].rearrange("p (g n) -> p g n", n=128)),
                 rd=[BANK[b]], wr=[CMR["setup"]] + LRES)
        for g in range(KC):
            P.op("dve", lambda e, g=g: e.tensor_scalar(out=L["Gf"][:, g, :], in0=onesB[:], scalar1=col(R_LNG(g)), scalar2=None,
                                                       op0=ALU.mult), rd=[CONST], wr=[CMR["setup"]] + LRES)

    def cm_mixer(l):
        L = cm_layout()
        w_in = ("d_cm_w_in", None)
        binv = tmpp[0:1, 3, :].bitcast(BF16)
        P.dma("pool", "binv", lambda e: e.dma_start(out=binv, in_=A["d_cm_b_in"][None, 1024:2048]), wr=[TMP[3]])
        for pi in range(4):
            def evac(mm, tt, b, pi=pi):
                m = pi * 2 + mm
                P.op("act", lambda e: e.activation(out=S1[:, m, tsl(tt)], in_=banks[b][:], func=AF.Gelu_apprx_tanh,
                                                   bias=col(R_BINU(m)), scale=1.0), rd=[BANK[b], CONST], wr=[SR[(m, tt)]])
            proj_piece(w_in, pi * 256, h, HR, evac)
        stats = A["stats"]
        STAT = [Res(), Res()]
        CLS_V = (0, 4)
        CLS_M = (4, 4)

        pb = A["pb"]
        vpair = {"i": 0}
        mpair = {"i": 0}

        def v_mm(j):
            tt = j // 4
            jsl = slice(j * 128, (j + 1) * 128)
            u = j % 2
            vg = L["vg"][u]
            b = (vpair["i"] % 2) * 2
            vpair["i"] += 1
            for half in range(2):
                fns = [(lambda e, kc=kc: e.matmul(banks[b + half][:], lhsT=h[:, kc, jsl], rhs=L["wv"][:, kc, half * 512:(half + 1) * 512],
                                                  start=(kc == 0), stop=False)) for kc in range(KC)]
                fns.append(lambda e: e.matmul(banks[b + half][:], lhsT=onesB[0:1, :], rhs=binv[:, half * 512:(half + 1) * 512],
                                              start=False, stop=True))
                P.group(fns, rd=[HR[(kc, tt)] for kc in range(KC)] + [CMR["wv"], TMP[3], CONST], wr=[BANK[b + half]])
            P.op("act", lambda e: e.activation(out=vg.rearrange("p (a n) -> p a n", n=512), in_=pb[:, b:b + 2, :],
                                               func=AF.Gelu_apprx_tanh), rd=[BANK[b], BANK[b + 1]], wr=[CMR["vg"][u]])

        def v_stats(j):
            u = j % 2
            vg = L["vg"][u]
            vtm = L["vtm"][u]
            st = stats[:, u, :]
            for half in range(2):
                P.op("dve", lambda e: e.bn_stats(out=st[:, half * 6:(half + 1) * 6], in_=vg[:, half * 512:(half + 1) * 512]),
                     rd=[CMR["vg"][u]], wr=[STAT[u]])
            P.op("dve", lambda e: e.bn_aggr(out=st[:, 12:14], in_=st[:, 0:12]), wr=[STAT[u]])
            P.op("dve", lambda e: e.tensor_scalar(out=st[:, 14:15], in0=st[:, 13:14], scalar1=EPS, scalar2=None, op0=ALU.add),
                 wr=[STAT[u]])
            P.op("pool", lambda e: e.tensor_tensor(out=st[:, 15:16], in0=st[:, 14:15], in1=A["mhalf"][:, 0:1], op=ALU.pow),
                 rd=[CONST], wr=[STAT[u]])

        def v_vtm(j):
            u = j % 2
            vg = L["vg"][u]
            vtm = L["vtm"][u]
            st = stats[:, u, :]
            P.op("dve", lambda e: e.tensor_scalar(out=vtm, in0=vg, scalar1=st[:, 12:13], scalar2=st[:, 15:16],
                                                  op0=ALU.subtract, op1=ALU.mult), rd=[STAT[u], CMR["vg"][u]], wr=[CMR["vtm"][u]])

        mixb = {}

        def v_mix_mm(j):
            tt = j // 4
            jsl = slice(j * 128, (j + 1) * 128)
            u = j % 2
            vg = L["vg"][u]
            vtm = L["vtm"][u]
            b = 4 + (mpair["i"] % 2) * 2
            mpair["i"] += 1
            mixb[j] = b
            for half in range(2):
                P.group([(lambda e, g4=g4: e.matmul(banks[b + half][:, g4 * 128:(g4 + 1) * 128],
                                                    lhsT=vtm[:, (half * 4 + g4) * 128:(half * 4 + g4 + 1) * 128],
                                                    rhs=L["wsT"][:, half * 4 + g4, :], start=True, stop=True))
                         for g4 in range(4)], rd=[CMR["vtm"][u], CMR["setup"]], wr=[BANK[b + half]])

        def v_mix_evac(j):
            tt = j // 4
            jsl = slice(j * 128, (j + 1) * 128)
            u = j % 2
            vg = L["vg"][u]
            b = mixb[j]
            tv = vg.rearrange("p (g n) -> p g n", n=128)
            bk3 = pb[:, b:b + 2, :].rearrange("p a (g n) -> p (a g) n", n=128)
            P.op("dve", lambda e: e.tensor_tensor(out=tv, in0=bk3, in1=L["Gf"][:], op=ALU.mult),
                 rd=[BANK[b], BANK[b + 1], CMR["setup"]], wr=[CMR["vg"][u]])
            P.op("dve", lambda e: e.tensor_tensor(out=tv, in0=tv, in1=L["Cc"][:], op=ALU.add),
                 rd=[CMR["setup"]], wr=[CMR["vg"][u]])
            P.op("dve", lambda e: e.tensor_tensor(out=S1[:, :, jsl], in0=tv, in1=S1[:, :, jsl], op=ALU.mult),
                 rd=[CMR["vg"][u]], wr=[SR[(m, tt)] for m in range(KC)])

        v_mm(0)
        v_stats(0)
        v_vtm(0)
        v_mm(1)
        for k in range(1, 17):
            v_mix_mm(k - 1)
            if k < 16:
                v_stats(k)
            v_mix_evac(k - 1)
            if k < 16:
                v_vtm(k)
            if k + 1 < 16:
                v_mm(k + 1)
        out_proj(l, ("d_cm_w_out", None), after_tt=lambda tt: norm_tile(l, 1, tt))

    ystg = [reg[:, i * 1024:(i + 1) * 1024] for i in range(2)]
    YSTG = [Res(), Res()]
    ytoks = {}

    def final_tile(tt):
        norm_stats(tt)
        for kc in range(KC):
            P.op("dve", lambda e: e.scalar_tensor_tensor(out=x_fm[:, kc, tsl(tt)], in0=x_fm[:, kc, tsl(tt)],
                                                         scalar=col(R_FN(kc)), in1=ssb[:, tsl(tt)], op0=ALU.mult,
                                                         op1=ALU.mult), rd=[SSB[tt], CONST], wr=[XR[(kc, tt)]])
        for j in range(tt * 4, tt * 4 + 4):
            s_ = j % 2
            for hb in range(2):
                b = getbank()
                P.group([(lambda e, q=q: e.transpose(out=banks[b][:, q * 128:(q + 1) * 128],
                                                     in_=x_fm[:, hb * 4 + q, j * 128:(j + 1) * 128], identity=identF[:]))
                         for q in range(4)], rd=[XR[(m, tt)] for m in range(hb * 4, hb * 4 + 4)] + [CONST], wr=[BANK[b]])
                dst = ystg[s_][:, hb * 512:(hb + 1) * 512]
                if hb == 0:
                    P.op("dve", lambda e: e.tensor_copy(out=dst, in_=banks[b][:]), rd=[BANK[b]], wr=[YSTG[s_], CMR["wv"]])
                else:
                    P.op("act", lambda e: e.activation(out=dst, in_=banks[b][:], func=AF.Copy), rd=[BANK[b]], wr=[YSTG[s_], CMR["wv"]])
            ytoks[s_] = P.dma("sp", "ys%d" % s_, lambda e: e.dma_start(out=A["d_yout"][j * 128:(j + 1) * 128, :], in_=ystg[s_]),
                              rd=[YSTG[s_]])

    def run_all(fns):
        for _ in fns:
            pass

    stop = A.get("stop")
    import itertools
    MODGEN = [itertools.chain(mod_pieces(0, split=8, unpin=False), mod_pieces(1))]

    stages = [
        ("prologue", lambda: (prologue(side=MODGEN[0], nside=8), P.fence())),
        ("norm00", lambda: [norm_apply(0, 0, tt) for tt in range(NTT)]),
        ("lru", lambda: lru_mixer(0)),
        ("ffn0", lambda: ffn(0, side=cm_setup(), after_tt=lambda tt: norm_tile(1, 0, tt))),
        ("cm", lambda: cm_mixer(1)),
        ("ffn1", lambda: ffn(1, after_tt=(final_tile if stop is None else None))),
    ]
    Esp = P.engs["sp"]
    for name, fn in stages:
        fn()
        if stop == name:
            P.fence()
            dt = [P.dma("sp", "ys0", lambda e: e.dma_start(out=A["d_dbg_x"], in_=x_fm[:].rearrange("p a b -> p (a b)"))),
                  P.dma("sp", "ys0", lambda e: e.dma_start(out=A["d_dbg_h"], in_=h[:].rearrange("p a b -> p (a b)"))),
                  P.dma("sp", "ys0", lambda e: e.dma_start(out=A["d_dbg_s"], in_=S1[:].rearrange("p a b -> p (a b)"))),
                  P.dma("sp", "ys0", lambda e: e.dma_start(out=A["d_dbg_m"], in_=A["modT"][:].rearrange("p a b c -> p (a b c)")))]
            Esp.wait(dt[-1])
            if "_st_tok" in A:
                Esp.wait(A["_st_tok"])
            return
    for t in list(ytoks.values()) + [A["_st_tok"]]:
        Esp.wait(t)


def build_nc(plan, stop=None):
    nc = bass.Bass("TRN2", target_bir_lowering=False)
    A = {"stop": stop}
    A.update(TUNE)

    def din(name, shape):
        return nc.dram_tensor(name, shape, F32, kind="ExternalInput").ap()

    A["d_xin"] = din("xin", [T, D])
    A["d_ptab"] = din("ptab", [NROWS, 128])
    A["d_modw"] = din("modw", [2, D, 6 * D])
    A["d_lru_w_in"] = din("lru_w_in", [D, 2 * D])
    A["d_gwh"] = din("gwh", [2, 128, 2048])
    A["d_lru_w_out"] = din("lru_w_out", [D, D])
    A["d_cm_w_in"] = din("cm_w_in", [D, 2 * D])
    A["d_cm_b_in"] = din("cm_b_in", [2 * D])
    A["d_cm_ln_b"] = din("cm_ln_b", [D])
    A["d_cm_w_s"] = din("cm_w_s", [8, 128, 128])
    A["d_cm_b_s"] = din("cm_b_s", [8, 128])
    A["d_cm_w_out"] = din("cm_w_out", [D, D])
    A["d_ffn_g"] = din("ffn_g", [2, D, FF])
    A["d_ffn_u"] = din("ffn_u", [2, D, FF])
    A["d_ffn_d"] = din("ffn_d", [2, FF, D])
    A["d_yout"] = nc.dram_tensor("yout", [T, D], F32, kind="ExternalOutput").ap()
    A["d_stout"] = nc.dram_tensor("stout", [8, D], F32, kind="ExternalOutput").ap()
    if stop is not None:
        A["d_dbg_x"] = nc.dram_tensor("dbg_x", [128, KC * T], F32, kind="ExternalOutput").ap()
        A["d_dbg_h"] = nc.dram_tensor("dbg_h", [128, KC * T], BF16, kind="ExternalOutput").ap()
        A["d_dbg_s"] = nc.dram_tensor("dbg_s", [128, KC * T], BF16, kind="ExternalOutput").ap()
        A["d_dbg_m"] = nc.dram_tensor("dbg_m", [128, 192], F32, kind="ExternalOutput").ap()

    with ExitStack() as es:
        def sb(name, shape, dt):
            return es.enter_context(nc.sbuf_tensor(name, shape, dt))

        A["x_fm"] = sb("x_fm", [128, KC, T], F32)
        A["h"] = sb("h", [128, KC, T], BF16)
        A["S1"] = sb("S1", [128, KC, T], BF16)
        A["ring"] = sb("ring", [128, 4, 2048], BF16)
        A["reg"] = sb("reg", [128, 9744], F32)
        A["ssb"] = sb("ssb", [128, T], F32)
        A["tmpp"] = sb("tmpp", [128, 4, 512], F32)
        A["pcol"] = sb("pcol", [128, 384], F32)
        A["ptst"] = sb("ptst", [128, 3, 128], F32)
        A["identF"] = sb("identF", [128, 128], F32)
        A["identB"] = sb("identB", [128, 128], BF16)
        A["onesB"] = sb("onesB", [128, 128], BF16)
        A["onesrowF"] = sb("onesrowF", [1, 128], F32)
        A["mhalf"] = sb("mhalf", [128, 2], F32)
        A["dcol"] = sb("dcol", [128, 128], F32)
        A["condT"] = sb("condT", [128, KC, 2], BF16)
        A["modT"] = sb("modT", [128, 2, 48, 2], F32)
        A["amod"] = sb("amod", [128, 2, 2, 8, 2], F32)
        A["stfm"] = sb("stfm", [128, KC, 4, 2], F32)
        A["stats"] = sb("stats", [128, 2, 16], F32)
        A["mrow"] = sb("mrow", [2, 512], F32)
        A["pb"] = es.enter_context(nc.psum_tensor("pb", [128, 8, 512], F32))
        A["banks"] = [A["pb"][:, i, :] for i in range(8)]

        sems = {}
        for n in ["pe", "act", "dve", "pool"]:
            sems[n] = es.enter_context(nc.semaphore("s_" + n))
        dnames = ["ring0", "ring1", "ring2", "ring3", "setup", "setup2", "xs0", "xs1", "xs2", "xs3", "ys0", "ys1", "stout", "wvld", "binv", "gwld"]
        for n in dnames:
            sems[n] = es.enter_context(nc.semaphore("d_" + n))

        P = Prog(dry=False)
        for n in ["pe", "act", "dve", "pool"]:
            P.add_engine(n, sems[n])
        P.engs["sp"] = Eng("sp", -1, None)
        for n in dnames:
            P.add_dsem(n, sems[n])
        W = WStream(P, A, plan)
        emit_program(P, A, W)
        global _LAST_P
        _LAST_P = P
        if plan is None:
            return W.rec

        block = es.enter_context(nc.Block())

        @block.tensor
        def _(e):
            for calls in P.engs["pe"].q:
                replay(calls, e)

        @block.scalar
        def _(e):
            for calls in P.engs["act"].q:
                replay(calls, e)

        @block.vector
        def _(e):
            for calls in P.engs["dve"].q:
                replay(calls, e)

        @block.gpsimd
        def _(e):
            for calls in P.engs["pool"].q:
                replay(calls, e)

        @block.sync
        def _(e):
            for calls in P.engs["sp"].q:
                replay(calls, e)
    return nc


def _prep_inputs(inp):
    f = lambda a: np.ascontiguousarray(np.asarray(a, dtype=np.float32))
    xp, xs = f(inp["x_prompt"]), f(inp["x_sample"])
    shared = {
        "modw": f(inp["mod_w"]),
        "lru_w_in": f(inp["lru_w_in"][0]),
        "lru_w_out": f(inp["lru_w_out"][0]),
        "cm_w_in": f(inp["cm_w_in"][0]),
        "cm_b_in": f(inp["cm_b_in"][0]),
        "cm_ln_b": f(inp["cm_ln_b"][0]),
        "cm_w_s": f(inp["cm_w_s"][0]),
        "cm_b_s": f(inp["cm_b_s"][0]),
        "cm_w_out": f(inp["cm_w_out"][0]),
        "ffn_g": f(inp["ffn_w_gate"]),
        "ffn_u": f(inp["ffn_w_up"]),
        "ffn_d": f(inp["ffn_w_down"]),
    }
    ga, gx = f(inp["lru_ga_w"][0]), f(inp["lru_gx_w"][0])
    gwh = np.zeros((2, 128, 4, 4, 128), np.float32)
    for c in range(8):
        for idx in range(4):
            src = ga if idx < 2 else gx
            z = idx % 2
            for two in range(2):
                gwh[c // 4, two * 64:(two + 1) * 64, c % 4, idx, two * 64:(two + 1) * 64] = src[z, 2 * c + two]
    shared["gwh"] = gwh.reshape(2, 128, 2048)

    def rows(v):
        return f(v).reshape(-1, 128)

    common_rows = [rows(inp["mod_b"]), rows(inp["norm_mix"]), rows(inp["norm_ffn"]), rows(inp["final_norm"]),
                   rows(inp["lru_conv_w"][0]), rows(inp["lru_conv_b"][0]), rows(inp["lru_ga_b"][0]), rows(inp["lru_gx_b"][0]),
                   rows(inp["lru_lambda"][0]), rows(inp["cm_b_in"][0][:1024]), rows(inp["cm_ln_g"][0])]
    in_maps = []
    for i in range(NCORES):
        tab = np.zeros((NROWS, 128), np.float32)
        r = np.concatenate(common_rows + [rows(inp["state_lru"][i, 0]), rows(inp["c"][i]), rows(inp["c_ctx"])], axis=0)
        tab[:r.shape[0]] = r
        m = dict(shared)
        m["xin"] = np.ascontiguousarray(np.concatenate([xp[4 * i:4 * i + 4].reshape(1024, D), xs[i]], axis=0))
        m["ptab"] = tab
        in_maps.append(m)
    return in_maps


TUNE = {}
_NC_CACHE = {}
_LAST_P = None


def kernel(**inputs):
    in_maps = _prep_inputs(inputs)
    if "nc" not in _NC_CACHE:
        plan = build_nc(None)
        _NC_CACHE["nc"] = build_nc(plan)
    nc = _NC_CACHE["nc"]
    res = run_bass_kernel_spmd(nc, in_maps, core_ids=list(range(NCORES)))
    y_prompt = np.zeros((32, 256, D), np.float32)
    y_sample = np.zeros((8, 1024, D), np.float32)
    new_state = np.zeros((32, 1, 2, D), np.float32)
    for i in range(NCORES):
        y = np.asarray(res.results[i]["yout"], dtype=np.float32)
        y_prompt[4 * i:4 * i + 4] = y[:1024].reshape(4, 256, D)
        y_sample[i] = y[1024:]
        st = np.asarray(res.results[i]["stout"], dtype=np.float32)
        new_state[4 * i:4 * i + 4, 0] = st.reshape(4, 2, D)
    return (y_prompt, y_sample, new_state)
```

```python
import numpy as np
import concourse.bass as bass
import concourse.mybir as mybir
from concourse.bass_utils import run_bass_kernel_spmd
from contextlib import ExitStack

F32 = mybir.dt.float32
BF16 = mybir.dt.bfloat16
AF = mybir.ActivationFunctionType
ALU = mybir.AluOpType

NCORES = 8
D = 1024
KC = 8
T = 2048
NTT = 4
TT = 512
FF = 2816
FCH = 22
EPS = 1e-6
FGROUPS = [(0, 8), (8, 16), (16, 22)]

def R_MODB(l, j, c=0): return l * 48 + j * 8 + c
def R_NMIX(l, c=0): return 96 + l * 8 + c
def R_NFFN(l, c=0): return 112 + l * 8 + c
def R_FN(c=0): return 128 + c
def R_CONVW(k, c=0): return 136 + k * 8 + c
def R_CONVB(c=0): return 168 + c
def R_GAB(z, c=0): return 176 + z * 8 + c
def R_GXB(z, c=0): return 192 + z * 8 + c
def R_LAM(z, c=0): return 208 + z * 8 + c
def R_BINU(c=0): return 224 + c
def R_LNG(c=0): return 232 + c
def R_ST(z, c=0): return 240 + z * 8 + c
def R_C(c=0): return 256 + c
def R_CCTX(c=0): return 264 + c
NROWS = 384


class Tok:
    __slots__ = ("sid", "sem", "val")

    def __init__(self, sid, sem, val):
        self.sid, self.sem, self.val = sid, sem, val


class Res:
    __slots__ = ("w", "r")

    def __init__(self):
        self.w = None
        self.r = {}


class _H:
    def __init__(self, call):
        self.call = call

    def then_inc(self, sem, n):
        self.call.append((sem, n))
        return self


class RecEng:
    def __init__(self):
        self.calls = []

    def __getattr__(self, name):
        def f(*a, **k):
            c = [name, a, k]
            self.calls.append(c)
            return _H(c)
        return f


def replay(calls, e):
    for c in calls:
        ins = getattr(e, c[0])(*c[1], **c[2])
        for (sem, n) in c[3:]:
            ins.then_inc(sem, n)


class Eng:
    def __init__(self, name, sid, sem):
        self.name, self.sid, self.sem = name, sid, sem
        self.count = 0
        self.waited = {}
        self.q = []

    def wait(self, tok):
        if tok is None:
            return
        if self.waited.get(tok.sid, 0) >= tok.val:
            return
        self.waited[tok.sid] = tok.val
        self.q.append([["wait_ge", (tok.sem, tok.val), {}]])


class Prog:
    def __init__(self, dry):
        self.dry = dry
        self.engs = {}
        self.dsem = {}
        self.pe_fill = None
        self.nsid = 0

    def add_engine(self, name, sem):
        self.engs[name] = Eng(name, self.nsid, sem)
        self.nsid += 1

    def add_dsem(self, name, sem):
        self.dsem[name] = [self.nsid, sem, 0]
        self.nsid += 1

    def _deps(self, rd, wr, extra):
        deps = list(extra)
        for r in rd:
            if r.w is not None:
                deps.append(r.w)
        for w in wr:
            if w.w is not None:
                deps.append(w.w)
            deps.extend(w.r.values())
        return deps

    def _commit(self, tok, rd, wr):
        for r in rd:
            old = r.r.get(tok.sid)
            if old is None or old.val < tok.val:
                r.r[tok.sid] = tok
        for w in wr:
            w.w = tok
            w.r = {}

    def op(self, ename, fn, rd=(), wr=(), extra=()):
        E = self.engs[ename]
        for d in self._deps(rd, wr, extra):
            if ename == "pe" and d.sid == E.sid:
                continue
            E.wait(d)
        E.count += 1
        sem = E.sem
        rec = RecEng()
        fn(rec).then_inc(sem, 1)
        E.q.append(rec.calls)
        tok = Tok(E.sid, sem, E.count)
        self._commit(tok, rd, wr)
        return tok

    def group(self, fns, rd=(), wr=(), extra=()):
        E = self.engs["pe"]
        if self.pe_fill is not None:
            E.q.append(self.pe_fill())
        for d in self._deps(rd, wr, extra):
            if d.sid == E.sid:
                continue
            E.wait(d)
        rec = RecEng()
        for f in fns[:-1]:
            f(rec)
        E.count += 1
        sem = E.sem
        fns[-1](rec).then_inc(sem, 1)
        E.q.append(rec.calls)
        tok = Tok(E.sid, sem, E.count)
        self._commit(tok, rd, wr)
        return tok

    def dma(self, qname, semname, fn, rd=(), wr=(), extra=()):
        E = self.engs[qname]
        for d in self._deps(rd, wr, extra):
            E.wait(d)
        ds = self.dsem[semname]
        ds[2] += 16
        sem = ds[1]
        rec = RecEng()
        fn(rec).then_inc(sem, 16)
        E.q.append(rec.calls)
        tok = Tok(ds[0], sem, ds[2])
        self._commit(tok, rd, wr)
        return tok

    def fence(self, extra_toks=()):
        toks = [Tok(E.sid, E.sem, E.count) for E in self.engs.values() if E.count > 0 and E.sem is not None]
        toks += [Tok(ds[0], ds[1], ds[2]) for ds in self.dsem.values() if ds[2] > 0]
        toks += list(extra_toks)
        for E in self.engs.values():
            for t in toks:
                if t.sid != E.sid or E.name != "pe":
                    E.wait(t)


class WStream:
    NS = 4

    def __init__(self, P, A, plan):
        self.P, self.A, self.ring, self.plan = P, A, A["ring"], plan
        self.rec = []
        self.res = [Res() for _ in range(self.NS)]
        self.idx = 0
        self.issued = 0
        self.released = set()

    def _can_issue(self, k):
        return k < self.NS or (k - self.NS) in self.released

    def _issue(self, k):
        s = k % self.NS
        out_ap, in_ap = self.plan[k](self.A, self.ring[:, s, :])
        self.P.dma("pool", "ring%d" % s, lambda e: e.dma_start(out=out_ap, in_=in_ap), wr=[self.res[s]])

    def pump(self):
        if self.plan is None:
            return
        while self.issued < len(self.plan) and self._can_issue(self.issued):
            self._issue(self.issued)
            self.issued += 1

    def next(self, desc):
        self.rec.append(desc)
        i = self.idx
        self.idx += 1
        if self.plan is not None:
            self.pump()
            assert self.issued > i, "ring deadlock: piece %d cannot be issued" % i
        return self.ring[:, i % self.NS, :], self.res[i % self.NS], i

    def done(self, i):
        self.released.add(i)
        self.pump()


def emit_program(P, A, W):
    x_fm, h, S1, reg, ssb, tmpp, pcol = A["x_fm"], A["h"], A["S1"], A["reg"], A["ssb"], A["tmpp"], A["pcol"]
    banks = A["banks"]
    identF, identB, onesB = A["identF"], A["identB"], A["onesB"]

    XR = {(m, tt): Res() for m in range(KC) for tt in range(NTT)}
    HR = {(m, tt): Res() for m in range(KC) for tt in range(NTT)}
    SR = {(m, tt): Res() for m in range(KC) for tt in range(NTT)}
    BANK = [Res() for _ in range(8)]
    SSB = [Res() for _ in range(NTT)]
    TMP = [Res() for _ in range(4)]
    CONST = Res()
    pinned = set()
    bstate = {"i": 0}

    def getbank(cls=None):
        if cls is not None:
            lo, n = cls
            k = bstate.get(cls, 0)
            bstate[cls] = (k + 1) % n
            return lo + k
        while True:
            b = bstate["i"]
            bstate["i"] = (b + 1) % 8
            if b not in pinned:
                return b

    if A.get("y_in_loop", True):
        ng, nf, ny = A.get("bank_split", (3, 2, 2))
        CLS_G = (0, ng)
        CLS_F = (ng, nf)
        CLS_Y = (ng + nf, ny)
    else:
        CLS_G = (0, 4)
        CLS_F = (4, 3)
        CLS_Y = None

    tstate = {"i": 0}

    def gettmp(n=3):
        i = tstate["i"] % n
        tstate["i"] += 1
        return i

    def tsl(tt):
        return slice(tt * TT, (tt + 1) * TT)

    def grp(tt):
        return 0 if tt < 2 else 1

    P.op("pool", lambda e: e.memset(identF[:], 0.0), wr=[CONST])
    P.op("pool", lambda e: e.affine_select(out=identF[:], in_=identF[:], compare_op=ALU.not_equal, fill=1.0,
                                           base=0, pattern=[[-1, 128]], channel_multiplier=1), wr=[CONST])
    P.op("pool", lambda e: e.memset(onesB[:], 1.0), wr=[CONST])
    P.op("dve", lambda e: e.tensor_copy(out=identB[:], in_=identF[:]), rd=[CONST], wr=[CONST])
    P.op("pool", lambda e: e.memset(A["mhalf"][:], -0.5), wr=[CONST])
    P.op("pool", lambda e: e.memset(A["onesrowF"][:], 1.0), wr=[CONST])

    PT = Res()
    ptst = A["ptst"]
    P.dma("sp", "setup", lambda e: e.dma_start(out=ptst[:], in_=A["d_ptab"].rearrange("(g p) f -> p g f", p=128)), wr=[PT])
    b = getbank()
    bk = banks[b]
    P.group([(lambda e, g=g: e.transpose(out=bk[:, g * 128:(g + 1) * 128], in_=ptst[:, g, :], identity=identF[:]))
             for g in range(3)], rd=[PT, CONST], wr=[BANK[b]])
    P.op("dve", lambda e: e.tensor_copy(out=pcol[:, 0:384], in_=bk[:, 0:384]), rd=[BANK[b]], wr=[CONST])

    def col(r, n=1):
        return pcol[:, r:r + n]

    dcol = A["dcol"]
    NSP4 = 0
    HGA = 16
    HGX = 32
    H0X2 = 48
    SPT = 64
    condT = A["condT"]
    P.op("act", lambda e: e.activation(out=condT[:, :, 0], in_=col(R_CCTX(), 8), func=AF.Silu), rd=[CONST], wr=[CONST])
    P.op("act", lambda e: e.activation(out=condT[:, :, 1], in_=col(R_C(), 8), func=AF.Silu), rd=[CONST], wr=[CONST])
    P.op("act", lambda e: e.activation(out=dcol[:, SPT:SPT + 16], in_=col(R_LAM(0), 16), func=AF.Exp, scale=-1.0),
         rd=[CONST], wr=[CONST])
    P.op("act", lambda e: e.activation(out=dcol[:, SPT:SPT + 16], in_=dcol[:, SPT:SPT + 16], func=AF.Ln, bias=1.0),
         rd=[CONST], wr=[CONST])
    P.op("dve", lambda e: e.tensor_scalar(out=dcol[:, NSP4:NSP4 + 16], in0=dcol[:, SPT:SPT + 16], scalar1=-4.0, scalar2=None,
                                          op0=ALU.mult), rd=[CONST], wr=[CONST])
    P.op("dve", lambda e: e.tensor_scalar(out=dcol[:, HGA:HGA + 32], in0=col(R_GAB(0), 32), scalar1=0.5, scalar2=None,
                                          op0=ALU.mult), rd=[CONST], wr=[CONST])
    P.op("dve", lambda e: e.tensor_scalar(out=dcol[:, H0X2:H0X2 + 16], in0=col(R_ST(0), 16), scalar1=2.0, scalar2=None,
                                          op0=ALU.mult), rd=[CONST], wr=[CONST])

    modT = A["modT"]
    amod = A["amod"]
    MODR = [Res(), Res()]
    MROW = [Res(), Res()]

    MODBANK = 7
    MODCLS = [None]

    def mod_finalize(l, m0, m1, last):
        bk3 = banks[MODBANK][:, l * 96:(l + 1) * 96].rearrange("p (m g) -> p m g", g=2)
        for g in range(2):
            P.op("dve", lambda e: e.tensor_tensor(out=modT[:, l, m0:m1, g], in0=bk3[:, m0:m1, g], in1=col(R_MODB(l, 0) + m0, m1 - m0),
                                                  op=ALU.add), rd=[BANK[MODBANK], CONST], wr=[MODR[l]])
        for g in range(2):
            if m0 <= 8 and m1 >= 16:
                P.op("dve", lambda e: e.scalar_tensor_tensor(out=amod[:, l, 0, :, g], in0=modT[:, l, 8:16, g], scalar=1.0,
                                                             in1=col(R_NMIX(l), 8), op0=ALU.add, op1=ALU.mult),
                     rd=[CONST], wr=[MODR[l]])
            if m0 <= 32 and m1 >= 40:
                P.op("dve", lambda e: e.scalar_tensor_tensor(out=amod[:, l, 1, :, g], in0=modT[:, l, 32:40, g], scalar=1.0,
                                                             in1=col(R_NFFN(l), 8), op0=ALU.add, op1=ALU.mult),
                     rd=[CONST], wr=[MODR[l]])
        if last:
            pinned.discard(MODBANK)

    def mod_pieces(l, split=None, unpin=True):
        b = MODBANK
        pinned.add(b)
        bk = banks[b]
        mrow = A["mrow"]
        prev_finish = None
        for pi in range(24):
            def desc(A_, slot, l=l, pi=pi):
                return (slot.rearrange("p (kc n) -> p kc n", n=256),
                        A_["d_modw"][l, :, pi * 256:(pi + 1) * 256].rearrange("(kc p) n -> p kc n", p=128))
            wv, wres, wi = W.next(desc)
            wv3 = wv.rearrange("p (kc n) -> p kc n", n=256)
            b2 = getbank(MODCLS[0])
            P.group([(lambda e, kc=kc: e.matmul(banks[b2][0:2, 0:256], lhsT=condT[:, kc, :], rhs=wv3[:, kc, :],
                                                start=(kc == 0), stop=(kc == KC - 1))) for kc in range(KC)],
                    rd=[wres, CONST], wr=[BANK[b2]])
            W.done(wi)
            q = pi % 2
            P.op("dve", lambda e: e.tensor_copy(out=mrow[0:2, q * 256:(q + 1) * 256], in_=banks[b2][0:2, 0:256]),
                 rd=[BANK[b2]], wr=[MROW[q]])

            def finish(pi=pi, q=q):
                c0 = l * 96 + 2 * (2 * pi)
                P.group([(lambda e, mm=mm: e.transpose(out=bk[:, c0 + 2 * mm:c0 + 2 * mm + 2],
                                                       in_=mrow[0:2, q * 256 + mm * 128:q * 256 + (mm + 1) * 128],
                                                       identity=identF[0:2, 0:2])) for mm in range(2)],
                        rd=[MROW[q], CONST], wr=[BANK[b]])
            if prev_finish is not None:
                prev_finish()
            prev_finish = finish
            if split is not None and pi == split - 1:
                prev_finish()
                prev_finish = None
                mod_finalize(l, 0, 2 * split, False)
            yield
        if prev_finish is not None:
            prev_finish()
        mod_finalize(l, 0 if split is None else 2 * split, 48, unpin)
        yield

    xstg = [reg[:, i * 1024:(i + 1) * 1024] for i in range(4)]
    XSTG = [Res() for _ in range(4)]

    def prologue(side=None, nside=0):
        for j in range(16):
            s = j % 4
            P.dma("sp", "xs%d" % s, lambda e, j=j, s=s: e.dma_start(out=xstg[s], in_=A["d_xin"][j * 128:(j + 1) * 128, :]),
                  wr=[XSTG[s]])
            for hb in range(2):
                b = getbank()
                bk = banks[b]
                P.group([(lambda e, q=q, hb=hb, s=s, bk=bk: e.transpose(out=bk[:, q * 128:(q + 1) * 128],
                                                                         in_=xstg[s][:, (hb * 4 + q) * 128:(hb * 4 + q + 1) * 128],
                                                                         identity=identF[:])) for q in range(4)],
                        rd=[XSTG[s], CONST], wr=[BANK[b]])
                tt = j // 4
                dst = x_fm[:, hb * 4:hb * 4 + 4, j * 128:(j + 1) * 128]
                src = bk[:].rearrange("p (a b) -> p a b", b=128)
                eng = "dve" if hb == 0 else "act"
                if eng == "dve":
                    P.op("dve", lambda e, dst=dst, src=src: e.tensor_copy(out=dst, in_=src), rd=[BANK[b]],
                         wr=[XR[(m, tt)] for m in range(hb * 4, hb * 4 + 4)])
                else:
                    P.op("act", lambda e, dst=dst, src=src: e.activation(out=dst, in_=src, func=AF.Copy), rd=[BANK[b]],
                         wr=[XR[(m, tt)] for m in range(hb * 4, hb * 4 + 4)])
            if side is not None and j < nside:
                next(side, None)
            if j % 4 == 3:
                norm_stats(j // 4)

    def norm_stats(tt):
        P.op("act", lambda e: e.activation(out=h[:, :, tsl(tt)], in_=x_fm[:, :, tsl(tt)], func=AF.Square),
             rd=[XR[(m, tt)] for m in range(KC)], wr=[HR[(m, tt)] for m in range(KC)])
        b = getbank()
        bk = banks[b]
        P.group([(lambda e, kc=kc: e.matmul(bk[:], lhsT=onesB[:], rhs=h[:, kc, tsl(tt)], start=(kc == 0),
                                            stop=(kc == KC - 1))) for kc in range(KC)],
                rd=[HR[(m, tt)] for m in range(KC)] + [CONST], wr=[BANK[b]])
        P.op("dve", lambda e: e.tensor_scalar(out=ssb[:, tsl(tt)], in0=bk[:], scalar1=1.0 / D, scalar2=EPS,
                                              op0=ALU.mult, op1=ALU.add), rd=[BANK[b]], wr=[SSB[tt]])
        P.op("act", lambda e: e.activation(out=ssb[:, tsl(tt)], in_=ssb[:, tsl(tt)], func=AF.Ln), wr=[SSB[tt]])
        P.op("act", lambda e: e.activation(out=ssb[:, tsl(tt)], in_=ssb[:, tsl(tt)], func=AF.Exp, scale=-0.5), wr=[SSB[tt]])

    def norm_apply(l, kind, tt):
        jsh = 0 if kind == 0 else 3
        g = grp(tt)
        for kc in range(KC):
            ti = gettmp(3)
            tv = tmpp[:, ti, :]
            P.op("dve", lambda e: e.scalar_tensor_tensor(
                out=tv, in0=x_fm[:, kc, tsl(tt)], scalar=amod[:, l, kind, kc, g:g + 1], in1=ssb[:, tsl(tt)],
                op0=ALU.mult, op1=ALU.mult), rd=[XR[(kc, tt)], SSB[tt], MODR[l]], wr=[TMP[ti]])
            P.op("act", lambda e: e.activation(
                out=h[:, kc, tsl(tt)], in_=tv, func=AF.Identity, bias=modT[:, l, jsh * 8 + kc, g:g + 1], scale=1.0),
                rd=[TMP[ti], MODR[l]], wr=[HR[(kc, tt)]])

    def norm_tile(l, kind, tt):
        norm_stats(tt)
        norm_apply(l, kind, tt)

    def norm_mod(l, kind):
        for tt in range(NTT):
            norm_tile(l, kind, tt)

    def k1024_desc(wkey, c0):
        key, li = wkey
        def desc(A_, slot):
            w2 = A_[key] if li is None else A_[key][li]
            return (slot.rearrange("p (kc n) -> p kc n", n=256),
                    w2[:, c0:c0 + 256].rearrange("(kc p) n -> p kc n", p=128))
        return desc

    def proj_piece(wap2d, c0, src, SRC, evac):
        wv, wres, wi = W.next(k1024_desc(wap2d, c0))
        wv3 = wv.rearrange("p (kc n) -> p kc n", n=256)
        for mm in range(2):
            for tt in range(NTT):
                b = getbank()
                bk = banks[b]
                P.group([(lambda e, kc=kc, mm=mm, tt=tt, bk=bk: e.matmul(bk[:], lhsT=wv3[:, kc, mm * 128:(mm + 1) * 128],
                                                                        rhs=src[:, kc, tsl(tt)], start=(kc == 0), stop=(kc == KC - 1)))
                         for kc in range(KC)], rd=[wres] + [SRC[(kc, tt)] for kc in range(KC)], wr=[BANK[b]])
                evac(mm, tt, b)
        W.done(wi)

    def resid_evac(l, jgate, m, tt, b):
        g = grp(tt)
        bk = banks[b]
        P.op("dve", lambda e: e.scalar_tensor_tensor(out=x_fm[:, m, tsl(tt)], in0=bk[:], scalar=modT[:, l, jgate * 8 + m, g:g + 1],
                                                     in1=x_fm[:, m, tsl(tt)], op0=ALU.mult, op1=ALU.add),
             rd=[BANK[b], MODR[l]], wr=[XR[(m, tt)]])

    def lagged(after_tt):
        def f(tt, pi):
            if after_tt is None:
                return
            if pi == 1 and tt >= 1:
                after_tt(tt - 1)
            if pi == 3 and tt == NTT - 1:
                after_tt(tt)
        return f

    def out_proj(l, wap2d, after_tt=None):
        pcs = [W.next(k1024_desc(wap2d, pi * 256)) for pi in range(4)]
        lag = lagged(after_tt)
        for tt in range(NTT):
            for pi in range(4):
                wv3 = pcs[pi][0].rearrange("p (kc n) -> p kc n", n=256)
                for mm in range(2):
                    b = getbank()
                    P.group([(lambda e, kc=kc: e.matmul(banks[b][:], lhsT=wv3[:, kc, mm * 128:(mm + 1) * 128],
                                                        rhs=S1[:, kc, tsl(tt)], start=(kc == 0), stop=(kc == KC - 1)))
                             for kc in range(KC)], rd=[pcs[pi][1]] + [SR[(kc, tt)] for kc in range(KC)], wr=[BANK[b]])
                    resid_evac(l, 2, pi * 2 + mm, tt, b)
                lag(tt, pi)
        for pc in pcs:
            W.done(pc[2])

    def ffn(l, side=None, after_tt=None):
        wg, wu = ("d_ffn_g", l), ("d_ffn_u", l)
        for (j0, j1) in FGROUPS:
            nj = j1 - j0
            for jp in range(j0, j1, 2):
                wgv, wgres, wgi = W.next(k1024_desc(wg, jp * 128))
                wuv, wures, wui = W.next(k1024_desc(wu, jp * 128))
                wg3 = wgv.rearrange("p (kc n) -> p kc n", n=256)
                wu3 = wuv.rearrange("p (kc n) -> p kc n", n=256)
                for mm in range(2):
                    jj = jp + mm - j0
                    for tt in range(NTT):
                        bg = getbank()
                        bu = getbank()
                        P.group([(lambda e, kc=kc, mm=mm, tt=tt, bg=bg: e.matmul(banks[bg][:], lhsT=wg3[:, kc, mm * 128:(mm + 1) * 128],
                                                                                rhs=h[:, kc, tsl(tt)], start=(kc == 0), stop=(kc == KC - 1)))
                                 for kc in range(KC)], rd=[wgres] + [HR[(kc, tt)] for kc in range(KC)], wr=[BANK[bg]])
                        P.group([(lambda e, kc=kc, mm=mm, tt=tt, bu=bu: e.matmul(banks[bu][:], lhsT=wu3[:, kc, mm * 128:(mm + 1) * 128],
                                                                                rhs=h[:, kc, tsl(tt)], start=(kc == 0), stop=(kc == KC - 1)))
                                 for kc in range(KC)], rd=[wures] + [HR[(kc, tt)] for kc in range(KC)], wr=[BANK[bu]])
                        ti = gettmp(4)
                        sg = tmpp[:, ti, 0:256].bitcast(BF16)
                        P.op("act", lambda e, bg=bg, sg=sg: e.activation(out=sg, in_=banks[bg][:], func=AF.Silu),
                             rd=[BANK[bg]], wr=[TMP[ti]])
                        P.op("dve", lambda e, bu=bu, sg=sg, jj=jj, tt=tt: e.tensor_tensor(out=S1[:, jj, tsl(tt)], in0=banks[bu][:], in1=sg,
                                                                                         op=ALU.mult),
                             rd=[BANK[bu], TMP[ti]], wr=[SR[(jj, tt)]])
                W.done(wgi)
                W.done(wui)
                if side is not None:
                    for _ in range(3):
                        next(side, None)
            dpcs = []
            for pi in range(4):
                def desc(A_, slot, pi=pi, j0=j0, nj=nj, l=l):
                    return (slot[:, 0:nj * 256].rearrange("p (jj n) -> p jj n", n=256),
                            A_["d_ffn_d"][l][j0 * 128:(j0 + nj) * 128, pi * 256:(pi + 1) * 256].rearrange("(jj p) n -> p jj n", p=128))
                dpcs.append(W.next(desc))
            last = (j1 == FCH)
            lag = lagged(after_tt if last else None)
            order = [(tt, pi) for tt in range(NTT) for pi in range(4)] if last else [(tt, pi) for pi in range(4) for tt in range(NTT)]
            for oi, (tt, pi) in enumerate(order):
                wv, wres, wi = dpcs[pi]
                wv3 = wv[:, 0:nj * 256].rearrange("p (jj n) -> p jj n", n=256)
                for mm in range(2):
                    m = pi * 2 + mm
                    b = getbank()
                    P.group([(lambda e, jj=jj: e.matmul(banks[b][:], lhsT=wv3[:, jj, mm * 128:(mm + 1) * 128],
                                                        rhs=S1[:, jj, tsl(tt)], start=(jj == 0), stop=(jj == nj - 1)))
                             for jj in range(nj)], rd=[wres] + [SR[(jj, tt)] for jj in range(nj)], wr=[BANK[b]])
                    resid_evac(l, 5, m, tt, b)
                if last:
                    lag(tt, pi)
                if not last and tt == NTT - 1:
                    W.done(wi)
            if last:
                for pc in dpcs:
                    W.done(pc[2])
        if side is not None:
            for _ in side:
                pass

    def lru_mixer(l):
        w_in = ("d_lru_w_in", None)
        o = 0
        xrp = reg[:, o:o + 520].bitcast(BF16); o += 520
        xrs = reg[:, o:o + 520].bitcast(BF16); o += 520
        xcb = [reg[:, o:o + 512].bitcast(BF16), reg[:, o + 512:o + 1024].bitcast(BF16)]; o += 1024
        T1both = reg[:, o:o + 2048]
        T1 = [reg[:, o:o + 1024], reg[:, o + 1024:o + 2048]]; o += 2048
        T3 = [reg[:, o:o + 1024], reg[:, o + 1024:o + 2048]]; o += 2048
        T2both = reg[:, o:o + 2048]
        T2 = [reg[:, o:o + 1024], reg[:, o + 1024:o + 2048]]; o += 2048
        dcv = reg[:, o:o + 512].bitcast(BF16).rearrange("p (u k n) -> p u k n", u=2, k=4); o += 512
        HS = [reg[:, o:o + 1024], T2[1]]; o += 1024
        assert o <= 9744
        xc = [ssb[:, 0:1024], ssb[:, 1024:2048]]
        gwt = tmpp[:].rearrange("p a b -> p (a b)").bitcast(BF16).rearrange("p (g cc i n) -> p g cc i n", g=2, cc=4, i=4)
        XRP, XRS = Res(), Res()
        XCR, XCBR = [[Res(), Res()], [Res(), Res()]], [[Res(), Res()], [Res(), Res()]]
        T1R = [[Res(), Res()], [Res(), Res()]]
        T3R = [[Res(), Res()], [Res(), Res()]]
        T2R = [[Res(), Res()], [Res(), Res()]]
        HSR = [[Res(), Res()], T2R[1]]
        DCV = [Res(), Res()]
        GWR = Res()
        LRES.extend([XRP, XRS] + XCBR[0] + XCBR[1] + T1R[0] + T1R[1] + T3R[0] + T3R[1] + T2R[0] + T2R[1] + HSR[0] + DCV)
        xrp3 = xrp[:, 0:1036].rearrange("p (s w) -> p s w", w=259)

        P.op("pool", lambda e: e.memset(xrp[:], 0.0), wr=[XRP])
        P.op("pool", lambda e: e.memset(xrs[:], 0.0), wr=[XRS])
        for g in range(2):
            P.dma("pool", "gwld", lambda e, g=g: e.dma_start(out=gwt[:, g].rearrange("p cc i n -> p (cc i n)"), in_=A["d_gwh"][g]),
                  wr=[GWR] + TMP)

        side1 = MODGEN[0]
        YINL = A.get("y_in_loop", True)
        MODCLS[0] = CLS_Y
        if not YINL:
            for pi in range(4):
                def evac(mm, tt, b, pi=pi):
                    m = pi * 2 + mm
                    P.op("act", lambda e: e.activation(out=S1[:, m, tsl(tt)], in_=banks[b][:], func=AF.Gelu_apprx_tanh),
                         rd=[BANK[b]], wr=[SR[(m, tt)]])
                proj_piece(w_in, 1024 + pi * 256, h, HR, evac)
                for _ in range(4):
                    next(side1, None)

        def y_part(c, tts):
            def ydesc(A_, slot):
                return (slot[:, 0:1024].rearrange("p (kc n) -> p kc n", n=128),
                        A_["d_lru_w_in"][:, 1024 + c * 128:1024 + (c + 1) * 128].rearrange("(kc p) n -> p kc n", p=128))
            wyv, wyres, wyi = W.next(ydesc)
            wy3 = wyv[:, 0:1024].rearrange("p (kc n) -> p kc n", n=128)
            ybanks = []
            for tt in tts:
                b = getbank(CLS_Y)
                ybanks.append(b)
                P.group([(lambda e, kc=kc: e.matmul(banks[b][:], lhsT=wy3[:, kc, :], rhs=h[:, kc, tsl(tt)],
                                                    start=(kc == 0), stop=(kc == KC - 1))) for kc in range(KC)],
                        rd=[wyres] + [HR[(kc, tt)] for kc in range(KC)], wr=[BANK[b]])
            W.done(wyi)

            def evac():
                for tt, b in zip(tts, ybanks):
                    P.op("act", lambda e: e.activation(out=S1[:, c, tsl(tt)], in_=banks[b][:], func=AF.Gelu_apprx_tanh),
                         rd=[BANK[b]], wr=[SR[(c, tt)]])
            return evac

        stfm = A["stfm"]
        STR = Res()
        units = [(c, gi) for c in range(KC) for gi in range(2)]
        NU = len(units)
        st = {}
        wxidx = {}

        def front_mm(ui):
            c, gi = units[ui]
            u = c % 2
            tts = (0, 1) if gi == 0 else (2, 3)
            def wxdesc(A_, slot, c=c):
                return (slot[:, 0:1024].rearrange("p (kc n) -> p kc n", n=128),
                        A_["d_lru_w_in"][:, c * 128:(c + 1) * 128].rearrange("(kc p) n -> p kc n", p=128))
            wxv, wxres, wxi = W.next(wxdesc)
            st["wx"] = (wxv[:, 0:1024].rearrange("p (kc n) -> p kc n", n=128), wxres, wxi)
            if gi == 0:
                for k in range(4):
                    P.op("dve", lambda e, k=k: e.tensor_scalar(out=dcv[:, u, k, :], in0=identF[:], scalar1=col(R_CONVW(k, c)),
                                                               scalar2=None, op0=ALU.mult), rd=[CONST], wr=[DCV[u]])
            wx3, wxres, wxi = st["wx"]
            for tt in tts:
                b = getbank(CLS_F)
                P.group([(lambda e, kc=kc: e.matmul(banks[b][:], lhsT=wx3[:, kc, :],
                                                    rhs=h[:, kc, tsl(tt)], start=(kc == 0), stop=(kc == KC - 1)))
                         for kc in range(KC)], rd=[wxres] + [HR[(kc, tt)] for kc in range(KC)], wr=[BANK[b]])
                if gi == 0:
                    dst = xrp3[:, 2 * tt:2 * tt + 2, 1:257]
                    src = banks[b][:].rearrange("p (s w) -> p s w", w=256)
                    P.op("dve", lambda e: e.tensor_copy(out=dst, in_=src), rd=[BANK[b]], wr=[XRP])
                else:
                    dst = xrs[:, 1 + (tt - 2) * TT:1 + (tt - 1) * TT]
                    P.op("dve", lambda e: e.tensor_copy(out=dst, in_=banks[b][:]), rd=[BANK[b]], wr=[XRS])
            W.done(wxi)

        def front_conv(ui):
            c, gi = units[ui]
            p = ui % 2
            u = c % 2
            tts = (0, 1) if gi == 0 else (2, 3)
            for ti_, tt in enumerate(tts):
                b = getbank(CLS_F)
                if gi == 0:
                    rhs_k = lambda k: xrp3[:, 2 * tt:2 * tt + 2, k:k + 256]
                    outv = banks[b][:].rearrange("p (s w) -> p s w", w=256)
                else:
                    rhs_k = lambda k: xrs[:, (tt - 2) * TT + k:(tt - 2) * TT + k + TT]
                    outv = banks[b][:]
                P.group([(lambda e, k=k: e.matmul(outv, lhsT=dcv[:, u, k, :], rhs=rhs_k(k), start=(k == 0), stop=(k == 3)))
                         for k in range(4)], rd=[XRP if gi == 0 else XRS, DCV[u]], wr=[BANK[b]])
                hs = slice(ti_ * TT, (ti_ + 1) * TT)
                if A.get("xcevac_eng", "dve") == "dve":
                    P.op("dve", lambda e: e.tensor_scalar(out=xc[p][:, hs], in0=banks[b][:], scalar1=col(R_CONVB(c)),
                                                          scalar2=None, op0=ALU.add), rd=[BANK[b], CONST], wr=[XCR[p][ti_], SSB[2 * p + ti_]])
                else:
                    P.op("act", lambda e: e.activation(out=xc[p][:, hs], in_=banks[b][:], func=AF.Identity, bias=col(R_CONVB(c)),
                                                       scale=1.0), rd=[BANK[b], CONST], wr=[XCR[p][ti_], SSB[2 * p + ti_]])

        def front_cast(ui):
            p = ui % 2
            if A.get("cast_one", True) and A.get("cast_eng", "act") == "act":
                P.op("act", lambda e: e.activation(out=xcb[p][:], in_=xc[p][:], func=AF.Copy), rd=XCR[p], wr=XCBR[p])
                return
            for ti_ in range(2):
                hs = slice(ti_ * TT, (ti_ + 1) * TT)
                if A.get("cast_eng", "act") == "act":
                    P.op("act", lambda e: e.activation(out=xcb[p][:, hs], in_=xc[p][:, hs], func=AF.Copy),
                         rd=[XCR[p][ti_]], wr=[XCBR[p][ti_]])
                else:
                    P.op(A.get("cast_eng"), lambda e: e.tensor_copy(out=xcb[p][:, hs], in_=xc[p][:, hs]),
                         rd=[XCR[p][ti_]], wr=[XCBR[p][ti_]])

        def back_act(ui, z):
            c, gi = units[ui]
            p = ui % 2
            nwarm = A.get("warm", 0)
            if nwarm and ui < NU - 2:
                E = P.engs["pe"]
                rec = RecEng()
                for _ in range(nwarm):
                    rec.matmul(banks[7][:, 128:512], lhsT=onesB[:, :], rhs=h[:, 0, 0:384], start=True, stop=True)
                E.q.append(rec.calls)
            g4 = gwt[:, c // 4, c % 4]
            for (idx, dst, dstR, bcol) in ((z, T1[z], T1R[z], HGA), (2 + z, T3[z], T3R[z], HGX)):
                bb = [getbank(CLS_G), getbank(CLS_G)]
                for ti_ in range(2):
                    hs = slice(ti_ * TT, (ti_ + 1) * TT)
                    P.group([lambda e: e.matmul(banks[bb[ti_]][:], lhsT=g4[:, idx, :], rhs=xcb[p][:, hs], start=True, stop=True)],
                            rd=[GWR, XCBR[p][ti_]] + TMP, wr=[BANK[bb[ti_]]])
                for ti_ in range(2):
                    hs = slice(ti_ * TT, (ti_ + 1) * TT)
                    P.op("act", lambda e: e.activation(out=dst[:, hs], in_=banks[bb[ti_]][:], func=AF.Tanh,
                                                       bias=dcol[:, bcol + z * 8 + c:bcol + z * 8 + c + 1], scale=0.5),
                         rd=[BANK[bb[ti_]], CONST], wr=[dstR[ti_]])
            nsp = dcol[:, NSP4 + z * 8 + c:NSP4 + z * 8 + c + 1]
            P.op("act", lambda e: e.activation(out=T1[z][:], in_=T1[z][:], func=AF.Exp, bias=nsp, scale=nsp),
                 rd=[CONST], wr=T1R[z])
            if A.get("sq_merge", True):
                pass
            elif A.get("e2_eng", "act") == "act":
                P.op("act", lambda e: e.activation(out=T2[z][:], in_=T1[z][:], func=AF.Square), rd=T1R[z], wr=T2R[z])
            else:
                P.op("pool", lambda e: e.tensor_tensor(out=T2[z][:], in0=T1[z][:], in1=T1[z][:], op=ALU.mult), rd=T1R[z], wr=T2R[z])

        def back_sqrt(ui, z):
            if A.get("sq_merge", True):
                if z == 0 and A.get("sq_split", True):
                    for zz in range(2):
                        P.op("act", lambda e: e.activation(out=T2[zz][:], in_=T1[zz][:], func=AF.Square), rd=T1R[zz], wr=T2R[zz])
                        P.op("act", lambda e: e.activation(out=T2[zz][:], in_=T2[zz][:], func=AF.Sqrt, bias=1.0, scale=-1.0),
                             wr=T2R[zz])
                elif z == 0:
                    P.op("act", lambda e: e.activation(out=T2both, in_=T1both, func=AF.Square),
                         rd=T1R[0] + T1R[1], wr=T2R[0] + T2R[1])
                    if A.get("sqrt_split", True):
                        for zz in range(2):
                            P.op("act", lambda e: e.activation(out=T2[zz][:], in_=T2[zz][:], func=AF.Sqrt, bias=1.0, scale=-1.0),
                                 wr=T2R[zz])
                    else:
                        P.op("act", lambda e: e.activation(out=T2both, in_=T2both, func=AF.Sqrt, bias=1.0, scale=-1.0),
                             wr=T2R[0] + T2R[1])
                return
            P.op("act", lambda e: e.activation(out=T2[z][:], in_=T2[z][:], func=AF.Sqrt, bias=1.0, scale=-1.0), wr=T2R[z])

        def back_gi(ui, z):
            c, gi = units[ui]
            p = ui % 2
            P.op("dve", lambda e: e.scalar_tensor_tensor(out=T3[z][:], in0=T3[z][:], scalar=1.0, in1=xc[p][:], op0=ALU.add,
                                                         op1=ALU.mult), rd=XCR[p] + [SSB[2 * p], SSB[2 * p + 1]], wr=T3R[z])

        def back_dve(ui, z):
            c, gi = units[ui]
            p = ui % 2
            P.op("dve", lambda e: e.tensor_tensor(out=T3[z][:], in0=T3[z][:], in1=T2[z][:], op=ALU.mult), rd=T2R[z], wr=T3R[z])
            if gi == 0:
                a4 = T1[z][:].rearrange("p (s w) -> p s w", w=256)
                P.op("dve", lambda e: e.memset(a4[:, :, 0 if z == 0 else 255], 0.0), wr=T1R[z])
                sl_ = slice(None) if z == 0 else slice(None, None, -1)
                P.op("dve", lambda e: e.tensor_tensor_scan(out=HS[z][:, sl_], data0=T1[z][:, sl_], data1=T3[z][:, sl_],
                                                           initial=0.0, op0=ALU.mult, op1=ALU.add),
                     rd=T1R[z] + T3R[z], wr=HSR[z])
                h4 = HS[z][:].rearrange("p (s w) -> p s w", w=256)
                P.op("pool", lambda e: e.tensor_scalar(out=stfm[:, c, :, z], in0=h4[:, :, 255 if z == 0 else 0], scalar1=0.5,
                                                       scalar2=None, op0=ALU.mult), rd=HSR[z], wr=[STR])
            else:
                sl_ = slice(None) if z == 0 else slice(None, None, -1)
                P.op("dve", lambda e: e.tensor_tensor_scan(out=HS[z][:, sl_], data0=T1[z][:, sl_], data1=T3[z][:, sl_],
                                                           initial=dcol[:, H0X2 + z * 8 + c:H0X2 + z * 8 + c + 1],
                                                           op0=ALU.mult, op1=ALU.add),
                     rd=T1R[z] + T3R[z] + [CONST], wr=HSR[z])

        def combine_add(ui):
            P.op(A.get("add_eng", "dve"), lambda e: e.tensor_tensor(out=HS[0][:], in0=HS[0][:], in1=HS[1][:], op=ALU.add), rd=HSR[1], wr=HSR[0])

        def combine_stt(ui):
            c, gi = units[ui]
            tts = (0, 1) if gi == 0 else (2, 3)
            tsl2 = slice(gi * 1024, (gi + 1) * 1024)
            P.op("dve", lambda e: e.scalar_tensor_tensor(out=S1[:, c, tsl2], in0=HS[0][:], scalar=0.5, in1=S1[:, c, tsl2],
                                                         op0=ALU.mult, op1=ALU.mult),
                 rd=HSR[0], wr=[SR[(c, tts[0])], SR[(c, tts[1])]])

        front_mm(0)
        front_conv(0)
        front_cast(0)
        front_mm(1)
        sched = A.get("lru_sched", "a0 g0 fc a1 cs g1 fm md sq ca d0 d1 ad").split()
        nfill = A.get("fill", 0)

        def filler():
            rec = RecEng()
            for _ in range(nfill):
                rec.matmul(banks[7][:, 128:512], lhsT=onesB[:, :], rhs=h[:, 0, 0:384], start=True, stop=True)
            return rec.calls
        if YINL:
            y_part(0, (0, 1))()
            y_part(0, (2, 3))()
        yev = [None]
        for ui in range(NU):
            P.pe_fill = filler if (nfill and ui < NU - 2) else None
            if "md" not in sched:
                for _ in range(3 if YINL else (2 if ui % 2 == 0 else 1)):
                    next(side1, None)
            for tok in sched:
                if tok == "md":
                    for _ in range(3):
                        next(side1, None)
                elif tok == "ym":
                    if YINL and units[ui][0] + 1 < KC:
                        yev[0] = y_part(units[ui][0] + 1, (0, 1) if units[ui][1] == 0 else (2, 3))
                elif tok == "a0":
                    back_act(ui, 0)
                elif tok == "a1":
                    back_act(ui, 1)
                elif tok == "g0":
                    back_gi(ui, 0)
                elif tok == "g1":
                    back_gi(ui, 1)
                elif tok == "fc" and ui + 1 < NU:
                    front_conv(ui + 1)
                elif tok == "sq":
                    back_sqrt(ui, 0)
                    back_sqrt(ui, 1)
                elif tok == "s0":
                    back_sqrt(ui, 0)
                elif tok == "s1":
                    back_sqrt(ui, 1)
                elif tok == "ca" and ui + 1 < NU:
                    front_cast(ui + 1)
                elif tok == "cs" and ui >= 1:
                    combine_stt(ui - 1)
                elif tok == "d0":
                    back_dve(ui, 0)
                elif tok == "d1":
                    back_dve(ui, 1)
                elif tok == "fm" and ui + 2 < NU:
                    front_mm(ui + 2)
                elif tok == "ad":
                    combine_add(ui)
            if YINL and units[ui][0] + 1 < KC:
                if yev[0] is None:
                    yev[0] = y_part(units[ui][0] + 1, (0, 1) if units[ui][1] == 0 else (2, 3))
                yev[0]()
                yev[0] = None
        P.pe_fill = None
        MODCLS[0] = None
        combine_stt(NU - 1)
        if side1 is not None:
            for _ in side1:
                pass

        strow = xc[0][0:8, :]
        for half in range(2):
            b = getbank()
            P.group([(lambda e, c4=c4: e.transpose(out=banks[b][0:8, c4 * 128:(c4 + 1) * 128],
                                                   in_=stfm[:, half * 4 + c4, :, :].rearrange("p s z -> p (s z)"),
                                                   identity=identF[:])) for c4 in range(4)], rd=[STR, CONST], wr=[BANK[b]])
            P.op("dve", lambda e: e.tensor_copy(out=strow[:, half * 512:(half + 1) * 512], in_=banks[b][0:8, :]),
                 rd=[BANK[b]], wr=XCR[0] + [SSB[0], SSB[1]])
        A["_st_tok"] = P.dma("sp", "stout", lambda e: e.dma_start(out=A["d_stout"][:, :], in_=strow[:]), rd=XCR[0] + [SSB[0], SSB[1]])
        out_proj(l, ("d_lru_w_out", None), after_tt=lambda tt: norm_tile(l, 1, tt))

    def cm_layout():
        o = 0
        L = {}
        L["wv"] = reg[:, o:o + 4096].bitcast(BF16).rearrange("p (kc n) -> p kc n", n=1024); o += 4096
        L["Gf"] = reg[:, o:o + 1024].rearrange("p (g n) -> p g n", n=128); o += 1024
        L["Cc"] = reg[:, o:o + 1024].rearrange("p (g n) -> p g n", n=128); o += 1024
        L["wsT"] = reg[:, o:o + 512].bitcast(BF16).rearrange("p (g n) -> p g n", n=128); o += 512
        L["vg"] = [reg[:, o:o + 1024], reg[:, o + 1024:o + 2048]]; o += 2048
        L["vtm"] = [reg[:, o:o + 512].bitcast(BF16), reg[:, o + 512:o + 1024].bitcast(BF16)]; o += 1024
        return L

    CMR = {"wv": Res(), "setup": Res(), "vg": [Res(), Res()], "vtm": [Res(), Res()]}
    LRES = []

    def cm_setup():
        L = cm_layout()
        wsn = ssb[:, 0:1024].rearrange("p (g n) -> p g n", n=128)
        lnb_row = L["vg"][0][0:1, :]
        bs_row = L["vg"][1][0:1, :].rearrange("p (g n) -> p g n", n=128)
        rs_row = reg[0:1, 8704:9728].rearrange("p (g n) -> p g n", n=128)
        ROWS = Res()
        for q in range(4):
            P.dma("pool", "wvld", lambda e, q=q: e.dma_start(out=L["wv"][:, :, q * 256:(q + 1) * 256],
                                                              in_=A["d_cm_w_in"][:, 1024 + q * 256:1024 + (q + 1) * 256].rearrange(
                                                                  "(kc p) n -> p kc n", p=128)), wr=[CMR["wv"]] + LRES)
        P.dma("sp", "setup2", lambda e: e.dma_start(out=wsn, in_=A["d_cm_w_s"].rearrange("g p q -> p g q")), wr=SSB)
        P.dma("sp", "setup2", lambda e: e.dma_start(out=lnb_row, in_=A["d_cm_ln_b"][None, :]), wr=[ROWS] + LRES)
        P.dma("sp", "setup2", lambda e: e.dma_start(out=bs_row, in_=A["d_cm_b_s"][None, :, :]), wr=[ROWS] + LRES)
        tot = Tok(P.dsem["setup2"][0], P.dsem["setup2"][1], P.dsem["setup2"][2])
        for r in SSB + [ROWS]:
            r.w = tot
        for _ in range(A.get("cm_setup_delay", 6)):
            yield
        for half in range(2):
            b = getbank()
            P.group([(lambda e, g=g, b=b, half=half: e.transpose(out=banks[b][:, g * 128:(g + 1) * 128], in_=wsn[:, half * 4 + g, :],
                                                                 identity=identF[:])) for g in range(4)], rd=SSB + [CONST], wr=[BANK[b]])
            P.op("dve", lambda e, b=b, half=half: e.tensor_copy(out=L["wsT"][:, half * 4:half * 4 + 4, :],
                                                                in_=banks[b][:].rearrange("p (g n) -> p g n", n=128)),
                 rd=[BANK[b]], wr=[CMR["setup"]] + LRES)
        for half in range(2):
            b = getbank()
            P.group([(lambda e, g=g, b=b, half=half: e.matmul(banks[b][0:1, g * 128:(g + 1) * 128], lhsT=onesB[:, 0:1],
                                                              rhs=L["wsT"][:, half * 4 + g, :], start=True, stop=True)) for g in range(4)],
                    rd=[CMR["setup"], CONST], wr=[BANK[b]])
            P.op("dve", lambda e, b=b, half=half: e.tensor_copy(out=rs_row[:, half * 4:half * 4 + 4, :],
                                                                in_=banks[b][0:1, :].rearrange("p (g n) -> p g n", n=128)),
                 rd=[BANK[b]], wr=[ROWS] + LRES)
        onesrowF = A["onesrowF"]
        for half in range(2):
            b = getbank()
            fns = []
            for g4 in range(4):
                g = half * 4 + g4
                fns.append(lambda e, g=g, g4=g4, b=b: e.matmul(banks[b][:, g4 * 128:(g4 + 1) * 128], lhsT=lnb_row[:, g * 128:(g + 1) * 128],
                                                               rhs=rs_row[:, g, :], start=True, stop=False, skip_group_check=True))
                fns.append(lambda e, g=g, g4=g4, b=b: e.matmul(banks[b][:, g4 * 128:(g4 + 1) * 128], lhsT=onesrowF[0:1, :],
                                                               rhs=bs_row[:, g, :], start=False, stop=True, skip_group_check=True))
            P.group(fns, rd=[ROWS, CONST], wr=[BANK[b]])
            P.op("dve", lambda e, b=b, half=half: e.tensor_copy(out=L["Cc"][:, half * 4:half * 4 + 4, :],
                                                                in_=banks[b][:].rearrange("p (g n) -> p g n", n=128)),
                 rd=[BANK[b]], wr=[CMR["setup"]] + LRES)
        for g in range(KC):
            P.op("dve", lambda e, g=g: e.tensor_scalar(out=L["Gf"][:, g, :], in0=onesB[:], scalar1=col(R_LNG(g)), scalar2=None,
                                                       op0=ALU.mult), rd=[CONST], wr=[CMR["setup"]] + LRES)

    def cm_mixer(l):
        L = cm_layout()
        w_in = ("d_cm_w_in", None)
        binv = tmpp[0:1, 3, :].bitcast(BF16)
        P.dma("pool", "binv", lambda e: e.dma_start(out=binv, in_=A["d_cm_b_in"][None, 1024:2048]), wr=[TMP[3]])
        for pi in range(4):
            def evac(mm, tt, b, pi=pi):
                m = pi * 2 + mm
                P.op("act", lambda e: e.activation(out=S1[:, m, tsl(tt)], in_=banks[b][:], func=AF.Gelu_apprx_tanh,
                                                   bias=col(R_BINU(m)), scale=1.0), rd=[BANK[b], CONST], wr=[SR[(m, tt)]])
            proj_piece(w_in, pi * 256, h, HR, evac)
        stats = A["stats"]
        STAT = [Res(), Res()]
        CLS_V = (0, 4)
        CLS_M = (4, 4)

        pb = A["pb"]
        vpair = {"i": 0}
        mpair = {"i": 0}

        def v_mm(j):
            tt = j // 4
            jsl = slice(j * 128, (j + 1) * 128)
            u = j % 2
            vg = L["vg"][u]
            b = (vpair["i"] % 2) * 2
            vpair["i"] += 1
            for half in range(2):
                fns = [(lambda e, kc=kc: e.matmul(banks[b + half][:], lhsT=h[:, kc, jsl], rhs=L["wv"][:, kc, half * 512:(half + 1) * 512],
                                                  start=(kc == 0), stop=False)) for kc in range(KC)]
                fns.append(lambda e: e.matmul(banks[b + half][:], lhsT=onesB[0:1, :], rhs=binv[:, half * 512:(half + 1) * 512],
                                              start=False, stop=True))
                P.group(fns, rd=[HR[(kc, tt)] for kc in range(KC)] + [CMR["wv"], TMP[3], CONST], wr=[BANK[b + half]])
            P.op("act", lambda e: e.activation(out=vg.rearrange("p (a n) -> p a n", n=512), in_=pb[:, b:b + 2, :],
                                               func=AF.Gelu_apprx_tanh), rd=[BANK[b], BANK[b + 1]], wr=[CMR["vg"][u]])

        def v_stats(j):
            u = j % 2
            vg = L["vg"][u]
            vtm = L["vtm"][u]
            st = stats[:, u, :]
            for half in range(2):
                P.op("dve", lambda e: e.bn_stats(out=st[:, half * 6:(half + 1) * 6], in_=vg[:, half * 512:(half + 1) * 512]),
                     rd=[CMR["vg"][u]], wr=[STAT[u]])
            P.op("dve", lambda e: e.bn_aggr(out=st[:, 12:14], in_=st[:, 0:12]), wr=[STAT[u]])
            P.op("dve", lambda e: e.tensor_scalar(out=st[:, 14:15], in0=st[:, 13:14], scalar1=EPS, scalar2=None, op0=ALU.add),
                 wr=[STAT[u]])
            P.op("pool", lambda e: e.tensor_tensor(out=st[:, 15:16], in0=st[:, 14:15], in1=A["mhalf"][:, 0:1], op=ALU.pow),
                 rd=[CONST], wr=[STAT[u]])

        def v_vtm(j):
            u = j % 2
            vg = L["vg"][u]
            vtm = L["vtm"][u]
            st = stats[:, u, :]
            P.op("dve", lambda e: e.tensor_scalar(out=vtm, in0=vg, scalar1=st[:, 12:13], scalar2=st[:, 15:16],
                                                  op0=ALU.subtract, op1=ALU.mult), rd=[STAT[u], CMR["vg"][u]], wr=[CMR["vtm"][u]])

        mixb = {}

        def v_mix_mm(j):
            tt = j // 4
            jsl = slice(j * 128, (j + 1) * 128)
            u = j % 2
            vg = L["vg"][u]
            vtm = L["vtm"][u]
            b = 4 + (mpair["i"] % 2) * 2
            mpair["i"] += 1
            mixb[j] = b
            for half in range(2):
                P.group([(lambda e, g4=g4: e.matmul(banks[b + half][:, g4 * 128:(g4 + 1) * 128],
                                                    lhsT=vtm[:, (half * 4 + g4) * 128:(half * 4 + g4 + 1) * 128],
                                                    rhs=L["wsT"][:, half * 4 + g4, :], start=True, stop=True))
                         for g4 in range(4)], rd=[CMR["vtm"][u], CMR["setup"]], wr=[BANK[b + half]])

        def v_mix_evac(j):
            tt = j // 4
            jsl = slice(j * 128, (j + 1) * 128)
            u = j % 2
            vg = L["vg"][u]
            b = mixb[j]
            tv = vg.rearrange("p (g n) -> p g n", n=128)
            bk3 = pb[:, b:b + 2, :].rearrange("p a (g n) -> p (a g) n", n=128)
            P.op("dve", lambda e: e.tensor_tensor(out=tv, in0=bk3, in1=L["Gf"][:], op=ALU.mult),
                 rd=[BANK[b], BANK[b + 1], CMR["setup"]], wr=[CMR["vg"][u]])
            P.op("dve", lambda e: e.tensor_tensor(out=tv, in0=tv, in1=L["Cc"][:], op=ALU.add),
                 rd=[CMR["setup"]], wr=[CMR["vg"][u]])
            P.op("dve", lambda e: e.tensor_tensor(out=S1[:, :, jsl], in0=tv, in1=S1[:, :, jsl], op=ALU.mult),
                 rd=[CMR["vg"][u]], wr=[SR[(m, tt)] for m in range(KC)])

        v_mm(0)
        v_stats(0)
        v_vtm(0)
        v_mm(1)
        for k in range(1, 17):
            v_mix_mm(k - 1)
            if k < 16:
                v_stats(k)
            v_mix_evac(k - 1)
            if k < 16:
                v_vtm(k)
            if k + 1 < 16:
                v_mm(k + 1)
        out_proj(l, ("d_cm_w_out", None), after_tt=lambda tt: norm_tile(l, 1, tt))

    ystg = [reg[:, i * 1024:(i + 1) * 1024] for i in range(2)]
    YSTG = [Res(), Res()]
    ytoks = {}

    def final_tile(tt):
        norm_stats(tt)
        for kc in range(KC):
            P.op("dve", lambda e: e.scalar_tensor_tensor(out=x_fm[:, kc, tsl(tt)], in0=x_fm[:, kc, tsl(tt)],
                                                         scalar=col(R_FN(kc)), in1=ssb[:, tsl(tt)], op0=ALU.mult,
                                                         op1=ALU.mult), rd=[SSB[tt], CONST], wr=[XR[(kc, tt)]])
        for j in range(tt * 4, tt * 4 + 4):
            s_ = j % 2
            for hb in range(2):
                b = getbank()
                P.group([(lambda e, q=q: e.transpose(out=banks[b][:, q * 128:(q + 1) * 128],
                                                     in_=x_fm[:, hb * 4 + q, j * 128:(j + 1) * 128], identity=identF[:]))
                         for q in range(4)], rd=[XR[(m, tt)] for m in range(hb * 4, hb * 4 + 4)] + [CONST], wr=[BANK[b]])
                dst = ystg[s_][:, hb * 512:(hb + 1) * 512]
                if hb == 0:
                    P.op("dve", lambda e: e.tensor_copy(out=dst, in_=banks[b][:]), rd=[BANK[b]], wr=[YSTG[s_], CMR["wv"]])
                else:
                    P.op("act", lambda e: e.activation(out=dst, in_=banks[b][:], func=AF.Copy), rd=[BANK[b]], wr=[YSTG[s_], CMR["wv"]])
            ytoks[s_] = P.dma("sp", "ys%d" % s_, lambda e: e.dma_start(out=A["d_yout"][j * 128:(j + 1) * 128, :], in_=ystg[s_]),
                              rd=[YSTG[s_]])

    def run_all(fns):
        for _ in fns:
            pass

    stop = A.get("stop")
    import itertools
    MODGEN = [itertools.chain(mod_pieces(0, split=8, unpin=False), mod_pieces(1))]

    stages = [
        ("prologue", lambda: (prologue(side=MODGEN[0], nside=8), P.fence())),
        ("norm00", lambda: [norm_apply(0, 0, tt) for tt in range(NTT)]),
        ("lru", lambda: lru_mixer(0)),
        ("ffn0", lambda: ffn(0, side=cm_setup(), after_tt=lambda tt: norm_tile(1, 0, tt))),
        ("cm", lambda: cm_mixer(1)),
        ("ffn1", lambda: ffn(1, after_tt=(final_tile if stop is None else None))),
    ]
    Esp = P.engs["sp"]
    for name, fn in stages:
        fn()
        if stop == name:
            P.fence()
            dt = [P.dma("sp", "ys0", lambda e: e.dma_start(out=A["d_dbg_x"], in_=x_fm[:].rearrange("p a b -> p (a b)"))),
                  P.dma("sp", "ys0", lambda e: e.dma_start(out=A["d_dbg_h"], in_=h[:].rearrange("p a b -> p (a b)"))),
                  P.dma("sp", "ys0", lambda e: e.dma_start(out=A["d_dbg_s"], in_=S1[:].rearrange("p a b -> p (a b)"))),
                  P.dma("sp", "ys0", lambda e: e.dma_start(out=A["d_dbg_m"], in_=A["modT"][:].rearrange("p a b c -> p (a b c)")))]
            Esp.wait(dt[-1])
            if "_st_tok" in A:
                Esp.wait(A["_st_tok"])
            return
    for t in list(ytoks.values()) + [A["_st_tok"]]:
        Esp.wait(t)


def build_nc(plan, stop=None):
    nc = bass.Bass("TRN2", target_bir_lowering=False)
    A = {"stop": stop}
    A.update(TUNE)

    def din(name, shape):
        return nc.dram_tensor(name, shape, F32, kind="ExternalInput").ap()

    A["d_xin"] = din("xin", [T, D])
    A["d_ptab"] = din("ptab", [NROWS, 128])
    A["d_modw"] = din("modw", [2, D, 6 * D])
    A["d_lru_w_in"] = din("lru_w_in", [D, 2 * D])
    A["d_gwh"] = din("gwh", [2, 128, 2048])
    A["d_lru_w_out"] = din("lru_w_out", [D, D])
    A["d_cm_w_in"] = din("cm_w_in", [D, 2 * D])
    A["d_cm_b_in"] = din("cm_b_in", [2 * D])
    A["d_cm_ln_b"] = din("cm_ln_b", [D])
    A["d_cm_w_s"] = din("cm_w_s", [8, 128, 128])
    A["d_cm_b_s"] = din("cm_b_s", [8, 128])
    A["d_cm_w_out"] = din("cm_w_out", [D, D])
    A["d_ffn_g"] = din("ffn_g", [2, D, FF])
    A["d_ffn_u"] = din("ffn_u", [2, D, FF])
    A["d_ffn_d"] = din("ffn_d", [2, FF, D])
    A["d_yout"] = nc.dram_tensor("yout", [T, D], F32, kind="ExternalOutput").ap()
    A["d_stout"] = nc.dram_tensor("stout", [8, D], F32, kind="ExternalOutput").ap()
    if stop is not None:
        A["d_dbg_x"] = nc.dram_tensor("dbg_x", [128, KC * T], F32, kind="ExternalOutput").ap()
        A["d_dbg_h"] = nc.dram_tensor("dbg_h", [128, KC * T], BF16, kind="ExternalOutput").ap()
        A["d_dbg_s"] = nc.dram_tensor("dbg_s", [128, KC * T], BF16, kind="ExternalOutput").ap()
        A["d_dbg_m"] = nc.dram_tensor("dbg_m", [128, 192], F32, kind="ExternalOutput").ap()

    with ExitStack() as es:
        def sb(name, shape, dt):
            return es.enter_context(nc.sbuf_tensor(name, shape, dt))

        A["x_fm"] = sb("x_fm", [128, KC, T], F32)
        A["h"] = sb("h", [128, KC, T], BF16)
        A["S1"] = sb("S1", [128, KC, T], BF16)
        A["ring"] = sb("ring", [128, 4, 2048], BF16)
        A["reg"] = sb("reg", [128, 9744], F32)
        A["ssb"] = sb("ssb", [128, T], F32)
        A["tmpp"] = sb("tmpp", [128, 4, 512], F32)
        A["pcol"] = sb("pcol", [128, 384], F32)
        A["ptst"] = sb("ptst", [128, 3, 128], F32)
        A["identF"] = sb("identF", [128, 128], F32)
        A["identB"] = sb("identB", [128, 128], BF16)
        A["onesB"] = sb("onesB", [128, 128], BF16)
        A["onesrowF"] = sb("onesrowF", [1, 128], F32)
        A["mhalf"] = sb("mhalf", [128, 2], F32)
        A["dcol"] = sb("dcol", [128, 128], F32)
        A["condT"] = sb("condT", [128, KC, 2], BF16)
        A["modT"] = sb("modT", [128, 2, 48, 2], F32)
        A["amod"] = sb("amod", [128, 2, 2, 8, 2], F32)
        A["stfm"] = sb("stfm", [128, KC, 4, 2], F32)
        A["stats"] = sb("stats", [128, 2, 16], F32)
        A["mrow"] = sb("mrow", [2, 512], F32)
        A["pb"] = es.enter_context(nc.psum_tensor("pb", [128, 8, 512], F32))
        A["banks"] = [A["pb"][:, i, :] for i in range(8)]

        sems = {}
        for n in ["pe", "act", "dve", "pool"]:
            sems[n] = es.enter_context(nc.semaphore("s_" + n))
        dnames = ["ring0", "ring1", "ring2", "ring3", "setup", "setup2", "xs0", "xs1", "xs2", "xs3", "ys0", "ys1", "stout", "wvld", "binv", "gwld"]
        for n in dnames:
            sems[n] = es.enter_context(nc.semaphore("d_" + n))

        P = Prog(dry=False)
        for n in ["pe", "act", "dve", "pool"]:
            P.add_engine(n, sems[n])
        P.engs["sp"] = Eng("sp", -1, None)
        for n in dnames:
            P.add_dsem(n, sems[n])
        W = WStream(P, A, plan)
        emit_program(P, A, W)
        global _LAST_P
        _LAST_P = P
        if plan is None:
            return W.rec

        block = es.enter_context(nc.Block())

        @block.tensor
        def _(e):
            for calls in P.engs["pe"].q:
                replay(calls, e)

        @block.scalar
        def _(e):
            for calls in P.engs["act"].q:
                replay(calls, e)

        @block.vector
        def _(e):
            for calls in P.engs["dve"].q:
                replay(calls, e)

        @block.gpsimd
        def _(e):
            for calls in P.engs["pool"].q:
                replay(calls, e)

        @block.sync
        def _(e):
            for calls in P.engs["sp"].q:
                replay(calls, e)
    return nc


def _prep_inputs(inp):
    f = lambda a: np.ascontiguousarray(np.asarray(a, dtype=np.float32))
    xp, xs = f(inp["x_prompt"]), f(inp["x_sample"])
    shared = {
        "modw": f(inp["mod_w"]),
        "lru_w_in": f(inp["lru_w_in"][0]),
        "lru_w_out": f(inp["lru_w_out"][0]),
        "cm_w_in": f(inp["cm_w_in"][0]),
        "cm_b_in": f(inp["cm_b_in"][0]),
        "cm_ln_b": f(inp["cm_ln_b"][0]),
        "cm_w_s": f(inp["cm_w_s"][0]),
        "cm_b_s": f(inp["cm_b_s"][0]),
        "cm_w_out": f(inp["cm_w_out"][0]),
        "ffn_g": f(inp["ffn_w_gate"]),
        "ffn_u": f(inp["ffn_w_up"]),
        "ffn_d": f(inp["ffn_w_down"]),
    }
    ga, gx = f(inp["lru_ga_w"][0]), f(inp["lru_gx_w"][0])
    gwh = np.zeros((2, 128, 4, 4, 128), np.float32)
    for c in range(8):
        for idx in range(4):
            src = ga if idx < 2 else gx
            z = idx % 2
            for two in range(2):
                gwh[c // 4, two * 64:(two + 1) * 64, c % 4, idx, two * 64:(two + 1) * 64] = src[z, 2 * c + two]
    shared["gwh"] = gwh.reshape(2, 128, 2048)

    def rows(v):
        return f(v).reshape(-1, 128)

    common_rows = [rows(inp["mod_b"]), rows(inp["norm_mix"]), rows(inp["norm_ffn"]), rows(inp["final_norm"]),
                   rows(inp["lru_conv_w"][0]), rows(inp["lru_conv_b"][0]), rows(inp["lru_ga_b"][0]), rows(inp["lru_gx_b"][0]),
                   rows(inp["lru_lambda"][0]), rows(inp["cm_b_in"][0][:1024]), rows(inp["cm_ln_g"][0])]
    in_maps = []
    for i in range(NCORES):
        tab = np.zeros((NROWS, 128), np.float32)
        r = np.concatenate(common_rows + [rows(inp["state_lru"][i, 0]), rows(inp["c"][i]), rows(inp["c_ctx"])], axis=0)
        tab[:r.shape[0]] = r
        m = dict(shared)
        m["xin"] = np.ascontiguousarray(np.concatenate([xp[4 * i:4 * i + 4].reshape(1024, D), xs[i]], axis=0))
        m["ptab"] = tab
        in_maps.append(m)
    return in_maps


TUNE = {}
_NC_CACHE = {}
_LAST_P = None


def kernel(**inputs):
    in_maps = _prep_inputs(inputs)
    if "nc" not in _NC_CACHE:
        plan = build_nc(None)
        _NC_CACHE["nc"] = build_nc(plan)
    nc = _NC_CACHE["nc"]
    res = run_bass_kernel_spmd(nc, in_maps, core_ids=list(range(NCORES)))
    y_prompt = np.zeros((32, 256, D), np.float32)
    y_sample = np.zeros((8, 1024, D), np.float32)
    new_state = np.zeros((32, 1, 2, D), np.float32)
    for i in range(NCORES):
        y = np.asarray(res.results[i]["yout"], dtype=np.float32)
        y_prompt[4 * i:4 * i + 4] = y[:1024].reshape(4, 256, D)
        y_sample[i] = y[1024:]
        st = np.asarray(res.results[i]["stout"], dtype=np.float32)
        new_state[4 * i:4 * i + 4, 0] = st.reshape(4, 2, D)
    return (y_prompt, y_sample, new_state)
```

```python
import numpy as np
import concourse.bass as bass
import concourse.mybir as mybir
from concourse.bass_utils import run_bass_kernel_spmd
from contextlib import ExitStack

F32 = mybir.dt.float32
BF16 = mybir.dt.bfloat16
AF = mybir.ActivationFunctionType
ALU = mybir.AluOpType

NCORES = 8
D = 1024
KC = 8
T = 2048
NTT = 4
TT = 512
FF = 2816
FCH = 22
EPS = 1e-6
FGROUPS = [(0, 8), (8, 16), (16, 22)]

def R_MODB(l, j, c=0): return l * 48 + j * 8 + c
def R_NMIX(l, c=0): return 96 + l * 8 + c
def R_NFFN(l, c=0): return 112 + l * 8 + c
def R_FN(c=0): return 128 + c
def R_CONVW(k, c=0): return 136 + k * 8 + c
def R_CONVB(c=0): return 168 + c
def R_GAB(z, c=0): return 176 + z * 8 + c
def R_GXB(z, c=0): return 192 + z * 8 + c
def R_LAM(z, c=0): return 208 + z * 8 + c
def R_BINU(c=0): return 224 + c
def R_LNG(c=0): return 232 + c
def R_ST(z, c=0): return 240 + z * 8 + c
def R_C(c=0): return 256 + c
def R_CCTX(c=0): return 264 + c
NROWS = 384


class Tok:
    __slots__ = ("sid", "sem", "val")

    def __init__(self, sid, sem, val):
        self.sid, self.sem, self.val = sid, sem, val


class Res:
    __slots__ = ("w", "r")

    def __init__(self):
        self.w = None
        self.r = {}


class _H:
    def __init__(self, call):
        self.call = call

    def then_inc(self, sem, n):
        self.call.append((sem, n))
        return self


class RecEng:
    def __init__(self):
        self.calls = []

    def __getattr__(self, name):
        def f(*a, **k):
            c = [name, a, k]
            self.calls.append(c)
            return _H(c)
        return f


def replay(calls, e):
    for c in calls:
        ins = getattr(e, c[0])(*c[1], **c[2])
        for (sem, n) in c[3:]:
            ins.then_inc(sem, n)


class Eng:
    def __init__(self, name, sid, sem):
        self.name, self.sid, self.sem = name, sid, sem
        self.count = 0
        self.waited = {}
        self.q = []

    def wait(self, tok):
        if tok is None:
            return
        if self.waited.get(tok.sid, 0) >= tok.val:
            return
        self.waited[tok.sid] = tok.val
        self.q.append([["wait_ge", (tok.sem, tok.val), {}]])


class Prog:
    def __init__(self, dry):
        self.dry = dry
        self.engs = {}
        self.dsem = {}
        self.pe_fill = None
        self.nsid = 0

    def add_engine(self, name, sem):
        self.engs[name] = Eng(name, self.nsid, sem)
        self.nsid += 1

    def add_dsem(self, name, sem):
        self.dsem[name] = [self.nsid, sem, 0]
        self.nsid += 1

    def _deps(self, rd, wr, extra):
        deps = list(extra)
        for r in rd:
            if r.w is not None:
                deps.append(r.w)
        for w in wr:
            if w.w is not None:
                deps.append(w.w)
            deps.extend(w.r.values())
        return deps

    def _commit(self, tok, rd, wr):
        for r in rd:
            old = r.r.get(tok.sid)
            if old is None or old.val < tok.val:
                r.r[tok.sid] = tok
        for w in wr:
            w.w = tok
            w.r = {}

    def op(self, ename, fn, rd=(), wr=(), extra=()):
        E = self.engs[ename]
        for d in self._deps(rd, wr, extra):
            if ename == "pe" and d.sid == E.sid:
                continue
            E.wait(d)
        E.count += 1
        sem = E.sem
        rec = RecEng()
        fn(rec).then_inc(sem, 1)
        E.q.append(rec.calls)
        tok = Tok(E.sid, sem, E.count)
        self._commit(tok, rd, wr)
        return tok

    def group(self, fns, rd=(), wr=(), extra=()):
        E = self.engs["pe"]
        if self.pe_fill is not None:
            E.q.append(self.pe_fill())
        for d in self._deps(rd, wr, extra):
            if d.sid == E.sid:
                continue
            E.wait(d)
        rec = RecEng()
        for f in fns[:-1]:
            f(rec)
        E.count += 1
        sem = E.sem
        fns[-1](rec).then_inc(sem, 1)
        E.q.append(rec.calls)
        tok = Tok(E.sid, sem, E.count)
        self._commit(tok, rd, wr)
        return tok

    def dma(self, qname, semname, fn, rd=(), wr=(), extra=()):
        E = self.engs[qname]
        for d in self._deps(rd, wr, extra):
            E.wait(d)
        ds = self.dsem[semname]
        ds[2] += 16
        sem = ds[1]
        rec = RecEng()
        fn(rec).then_inc(sem, 16)
        E.q.append(rec.calls)
        tok = Tok(ds[0], sem, ds[2])
        self._commit(tok, rd, wr)
        return tok

    def fence(self, extra_toks=()):
        toks = [Tok(E.sid, E.sem, E.count) for E in self.engs.values() if E.count > 0 and E.sem is not None]
        toks += [Tok(ds[0], ds[1], ds[2]) for ds in self.dsem.values() if ds[2] > 0]
        toks += list(extra_toks)
        for E in self.engs.values():
            for t in toks:
                if t.sid != E.sid or E.name != "pe":
                    E.wait(t)


class WStream:
    NS = 4

    def __init__(self, P, A, plan):
        self.P, self.A, self.ring, self.plan = P, A, A["ring"], plan
        self.rec = []
        self.res = [Res() for _ in range(self.NS)]
        self.idx = 0
        self.issued = 0
        self.released = set()

    def _can_issue(self, k):
        return k < self.NS or (k - self.NS) in self.released

    def _issue(self, k):
        s = k % self.NS
        out_ap, in_ap = self.plan[k](self.A, self.ring[:, s, :])
        self.P.dma("pool", "ring%d" % s, lambda e: e.dma_start(out=out_ap, in_=in_ap), wr=[self.res[s]])

    def pump(self):
        if self.plan is None:
            return
        while self.issued < len(self.plan) and self._can_issue(self.issued):
            self._issue(self.issued)
            self.issued += 1

    def next(self, desc):
        self.rec.append(desc)
        i = self.idx
        self.idx += 1
        if self.plan is not None:
            self.pump()
            assert self.issued > i, "ring deadlock: piece %d cannot be issued" % i
        return self.ring[:, i % self.NS, :], self.res[i % self.NS], i

    def done(self, i):
        self.released.add(i)
        self.pump()


def emit_program(P, A, W):
    x_fm, h, S1, reg, ssb, tmpp, pcol = A["x_fm"], A["h"], A["S1"], A["reg"], A["ssb"], A["tmpp"], A["pcol"]
    banks = A["banks"]
    identF, identB, onesB = A["identF"], A["identB"], A["onesB"]

    XR = {(m, tt): Res() for m in range(KC) for tt in range(NTT)}
    HR = {(m, tt): Res() for m in range(KC) for tt in range(NTT)}
    SR = {(m, tt): Res() for m in range(KC) for tt in range(NTT)}
    BANK = [Res() for _ in range(8)]
    SSB = [Res() for _ in range(NTT)]
    TMP = [Res() for _ in range(4)]
    CONST = Res()
    pinned = set()
    bstate = {"i": 0}

    def getbank(cls=None):
        if cls is not None:
            lo, n = cls
            k = bstate.get(cls, 0)
            bstate[cls] = (k + 1) % n
            return lo + k
        while True:
            b = bstate["i"]
            bstate["i"] = (b + 1) % 8
            if b not in pinned:
                return b

    if A.get("y_in_loop", True):
        ng, nf, ny = A.get("bank_split", (3, 2, 2))
        CLS_G = (0, ng)
        CLS_F = (ng, nf)
        CLS_Y = (ng + nf, ny)
    else:
        CLS_G = (0, 4)
        CLS_F = (4, 3)
        CLS_Y = None

    tstate = {"i": 0}

    def gettmp(n=3):
        i = tstate["i"] % n
        tstate["i"] += 1
        return i

    def tsl(tt):
        return slice(tt * TT, (tt + 1) * TT)

    def grp(tt):
        return 0 if tt < 2 else 1

    P.op("pool", lambda e: e.memset(identF[:], 0.0), wr=[CONST])
    P.op("pool", lambda e: e.affine_select(out=identF[:], in_=identF[:], compare_op=ALU.not_equal, fill=1.0,
                                           base=0, pattern=[[-1, 128]], channel_multiplier=1), wr=[CONST])
    P.op("pool", lambda e: e.memset(onesB[:], 1.0), wr=[CONST])
    P.op("dve", lambda e: e.tensor_copy(out=identB[:], in_=identF[:]), rd=[CONST], wr=[CONST])
    P.op("pool", lambda e: e.memset(A["mhalf"][:], -0.5), wr=[CONST])
    P.op("pool", lambda e: e.memset(A["onesrowF"][:], 1.0), wr=[CONST])

    PT = Res()
    ptst = A["ptst"]
    P.dma("sp", "setup", lambda e: e.dma_start(out=ptst[:], in_=A["d_ptab"].rearrange("(g p) f -> p g f", p=128)), wr=[PT])
    b = getbank()
    bk = banks[b]
    P.group([(lambda e, g=g: e.transpose(out=bk[:, g * 128:(g + 1) * 128], in_=ptst[:, g, :], identity=identF[:]))
             for g in range(3)], rd=[PT, CONST], wr=[BANK[b]])
    P.op("dve", lambda e: e.tensor_copy(out=pcol[:, 0:384], in_=bk[:, 0:384]), rd=[BANK[b]], wr=[CONST])

    def col(r, n=1):
        return pcol[:, r:r + n]

    dcol = A["dcol"]
    NSP4 = 0
    HGA = 16
    HGX = 32
    H0X2 = 48
    SPT = 64
    condT = A["condT"]
    P.op("act", lambda e: e.activation(out=condT[:, :, 0], in_=col(R_CCTX(), 8), func=AF.Silu), rd=[CONST], wr=[CONST])
    P.op("act", lambda e: e.activation(out=condT[:, :, 1], in_=col(R_C(), 8), func=AF.Silu), rd=[CONST], wr=[CONST])
    P.op("act", lambda e: e.activation(out=dcol[:, SPT:SPT + 16], in_=col(R_LAM(0), 16), func=AF.Exp, scale=-1.0),
         rd=[CONST], wr=[CONST])
    P.op("act", lambda e: e.activation(out=dcol[:, SPT:SPT + 16], in_=dcol[:, SPT:SPT + 16], func=AF.Ln, bias=1.0),
         rd=[CONST], wr=[CONST])
    P.op("dve", lambda e: e.tensor_scalar(out=dcol[:, NSP4:NSP4 + 16], in0=dcol[:, SPT:SPT + 16], scalar1=-4.0, scalar2=None,
                                          op0=ALU.mult), rd=[CONST], wr=[CONST])
    P.op("dve", lambda e: e.tensor_scalar(out=dcol[:, HGA:HGA + 32], in0=col(R_GAB(0), 32), scalar1=0.5, scalar2=None,
                                          op0=ALU.mult), rd=[CONST], wr=[CONST])
    P.op("dve", lambda e: e.tensor_scalar(out=dcol[:, H0X2:H0X2 + 16], in0=col(R_ST(0), 16), scalar1=2.0, scalar2=None,
                                          op0=ALU.mult), rd=[CONST], wr=[CONST])

    modT = A["modT"]
    amod = A["amod"]
    MODR = [Res(), Res()]
    MROW = [Res(), Res()]

    MODBANK = 7
    MODCLS = [None]

    def mod_finalize(l, m0, m1, last):
        bk3 = banks[MODBANK][:, l * 96:(l + 1) * 96].rearrange("p (m g) -> p m g", g=2)
        for g in range(2):
            P.op("dve", lambda e: e.tensor_tensor(out=modT[:, l, m0:m1, g], in0=bk3[:, m0:m1, g], in1=col(R_MODB(l, 0) + m0, m1 - m0),
                                                  op=ALU.add), rd=[BANK[MODBANK], CONST], wr=[MODR[l]])
        for g in range(2):
            if m0 <= 8 and m1 >= 16:
                P.op("dve", lambda e: e.scalar_tensor_tensor(out=amod[:, l, 0, :, g], in0=modT[:, l, 8:16, g], scalar=1.0,
                                                             in1=col(R_NMIX(l), 8), op0=ALU.add, op1=ALU.mult),
                     rd=[CONST], wr=[MODR[l]])
            if m0 <= 32 and m1 >= 40:
                P.op("dve", lambda e: e.scalar_tensor_tensor(out=amod[:, l, 1, :, g], in0=modT[:, l, 32:40, g], scalar=1.0,
                                                             in1=col(R_NFFN(l), 8), op0=ALU.add, op1=ALU.mult),
                     rd=[CONST], wr=[MODR[l]])
        if last:
            pinned.discard(MODBANK)

    def mod_pieces(l, split=None, unpin=True):
        b = MODBANK
        pinned.add(b)
        bk = banks[b]
        mrow = A["mrow"]
        prev_finish = None
        for pi in range(24):
            def desc(A_, slot, l=l, pi=pi):
                return (slot.rearrange("p (kc n) -> p kc n", n=256),
                        A_["d_modw"][l, :, pi * 256:(pi + 1) * 256].rearrange("(kc p) n -> p kc n", p=128))
            wv, wres, wi = W.next(desc)
            wv3 = wv.rearrange("p (kc n) -> p kc n", n=256)
            b2 = getbank(MODCLS[0])
            P.group([(lambda e, kc=kc: e.matmul(banks[b2][0:2, 0:256], lhsT=condT[:, kc, :], rhs=wv3[:, kc, :],
                                                start=(kc == 0), stop=(kc == KC - 1))) for kc in range(KC)],
                    rd=[wres, CONST], wr=[BANK[b2]])
            W.done(wi)
            q = pi % 2
            P.op("dve", lambda e: e.tensor_copy(out=mrow[0:2, q * 256:(q + 1) * 256], in_=banks[b2][0:2, 0:256]),
                 rd=[BANK[b2]], wr=[MROW[q]])

            def finish(pi=pi, q=q):
                c0 = l * 96 + 2 * (2 * pi)
                P.group([(lambda e, mm=mm: e.transpose(out=bk[:, c0 + 2 * mm:c0 + 2 * mm + 2],
                                                       in_=mrow[0:2, q * 256 + mm * 128:q * 256 + (mm + 1) * 128],
                                                       identity=identF[0:2, 0:2])) for mm in range(2)],
                        rd=[MROW[q], CONST], wr=[BANK[b]])
            if prev_finish is not None:
                prev_finish()
            prev_finish = finish
            if split is not None and pi == split - 1:
                prev_finish()
                prev_finish = None
                mod_finalize(l, 0, 2 * split, False)
            yield
        if prev_finish is not None:
            prev_finish()
        mod_finalize(l, 0 if split is None else 2 * split, 48, unpin)
        yield

    xstg = [reg[:, i * 1024:(i + 1) * 1024] for i in range(4)]
    XSTG = [Res() for _ in range(4)]

    def prologue(side=None, nside=0):
        for j in range(16):
            s = j % 4
            P.dma("sp", "xs%d" % s, lambda e, j=j, s=s: e.dma_start(out=xstg[s], in_=A["d_xin"][j * 128:(j + 1) * 128, :]),
                  wr=[XSTG[s]])
            for hb in range(2):
                b = getbank()
                bk = banks[b]
                P.group([(lambda e, q=q, hb=hb, s=s, bk=bk: e.transpose(out=bk[:, q * 128:(q + 1) * 128],
                                                                         in_=xstg[s][:, (hb * 4 + q) * 128:(hb * 4 + q + 1) * 128],
                                                                         identity=identF[:])) for q in range(4)],
                        rd=[XSTG[s], CONST], wr=[BANK[b]])
                tt = j // 4
                dst = x_fm[:, hb * 4:hb * 4 + 4, j * 128:(j + 1) * 128]
                src = bk[:].rearrange("p (a b) -> p a b", b=128)
                eng = "dve" if hb == 0 else "act"
                if eng == "dve":
                    P.op("dve", lambda e, dst=dst, src=src: e.tensor_copy(out=dst, in_=src), rd=[BANK[b]],
                         wr=[XR[(m, tt)] for m in range(hb * 4, hb * 4 + 4)])
                else:
                    P.op("act", lambda e, dst=dst, src=src: e.activation(out=dst, in_=src, func=AF.Copy), rd=[BANK[b]],
                         wr=[XR[(m, tt)] for m in range(hb * 4, hb * 4 + 4)])
            if side is not None and j < nside:
                next(side, None)
            if j % 4 == 3:
                norm_stats(j // 4)

    def norm_stats(tt):
        P.op("act", lambda e: e.activation(out=h[:, :, tsl(tt)], in_=x_fm[:, :, tsl(tt)], func=AF.Square),
             rd=[XR[(m, tt)] for m in range(KC)], wr=[HR[(m, tt)] for m in range(KC)])
        b = getbank()
        bk = banks[b]
        P.group([(lambda e, kc=kc: e.matmul(bk[:], lhsT=onesB[:], rhs=h[:, kc, tsl(tt)], start=(kc == 0),
                                            stop=(kc == KC - 1))) for kc in range(KC)],
                rd=[HR[(m, tt)] for m in range(KC)] + [CONST], wr=[BANK[b]])
        P.op("dve", lambda e: e.tensor_scalar(out=ssb[:, tsl(tt)], in0=bk[:], scalar1=1.0 / D, scalar2=EPS,
                                              op0=ALU.mult, op1=ALU.add), rd=[BANK[b]], wr=[SSB[tt]])
        P.op("act", lambda e: e.activation(out=ssb[:, tsl(tt)], in_=ssb[:, tsl(tt)], func=AF.Ln), wr=[SSB[tt]])
        P.op("act", lambda e: e.activation(out=ssb[:, tsl(tt)], in_=ssb[:, tsl(tt)], func=AF.Exp, scale=-0.5), wr=[SSB[tt]])

    def norm_apply(l, kind, tt):
        jsh = 0 if kind == 0 else 3
        g = grp(tt)
        for kc in range(KC):
            ti = gettmp(3)
            tv = tmpp[:, ti, :]
            P.op("dve", lambda e: e.scalar_tensor_tensor(
                out=tv, in0=x_fm[:, kc, tsl(tt)], scalar=amod[:, l, kind, kc, g:g + 1], in1=ssb[:, tsl(tt)],
                op0=ALU.mult, op1=ALU.mult), rd=[XR[(kc, tt)], SSB[tt], MODR[l]], wr=[TMP[ti]])
            P.op("act", lambda e: e.activation(
                out=h[:, kc, tsl(tt)], in_=tv, func=AF.Identity, bias=modT[:, l, jsh * 8 + kc, g:g + 1], scale=1.0),
                rd=[TMP[ti], MODR[l]], wr=[HR[(kc, tt)]])

    def norm_tile(l, kind, tt):
        norm_stats(tt)
        norm_apply(l, kind, tt)

    def norm_mod(l, kind):
        for tt in range(NTT):
            norm_tile(l, kind, tt)

    def k1024_desc(wkey, c0):
        key, li = wkey
        def desc(A_, slot):
            w2 = A_[key] if li is None else A_[key][li]
            return (slot.rearrange("p (kc n) -> p kc n", n=256),
                    w2[:, c0:c0 + 256].rearrange("(kc p) n -> p kc n", p=128))
        return desc

    def proj_piece(wap2d, c0, src, SRC, evac):
        wv, wres, wi = W.next(k1024_desc(wap2d, c0))
        wv3 = wv.rearrange("p (kc n) -> p kc n", n=256)
        for mm in range(2):
            for tt in range(NTT):
                b = getbank()
                bk = banks[b]
                P.group([(lambda e, kc=kc, mm=mm, tt=tt, bk=bk: e.matmul(bk[:], lhsT=wv3[:, kc, mm * 128:(mm + 1) * 128],
                                                                        rhs=src[:, kc, tsl(tt)], start=(kc == 0), stop=(kc == KC - 1)))
                         for kc in range(KC)], rd=[wres] + [SRC[(kc, tt)] for kc in range(KC)], wr=[BANK[b]])
                evac(mm, tt, b)
        W.done(wi)

    def resid_evac(l, jgate, m, tt, b):
        g = grp(tt)
        bk = banks[b]
        P.op("dve", lambda e: e.scalar_tensor_tensor(out=x_fm[:, m, tsl(tt)], in0=bk[:], scalar=modT[:, l, jgate * 8 + m, g:g + 1],
                                                     in1=x_fm[:, m, tsl(tt)], op0=ALU.mult, op1=ALU.add),
             rd=[BANK[b], MODR[l]], wr=[XR[(m, tt)]])

    def lagged(after_tt):
        def f(tt, pi):
            if after_tt is None:
                return
            if pi == 1 and tt >= 1:
                after_tt(tt - 1)
            if pi == 3 and tt == NTT - 1:
                after_tt(tt)
        return f

    def out_proj(l, wap2d, after_tt=None):
        pcs = [W.next(k1024_desc(wap2d, pi * 256)) for pi in range(4)]
        lag = lagged(after_tt)
        for tt in range(NTT):
            for pi in range(4):
                wv3 = pcs[pi][0].rearrange("p (kc n) -> p kc n", n=256)
                for mm in range(2):
                    b = getbank()
                    P.group([(lambda e, kc=kc: e.matmul(banks[b][:], lhsT=wv3[:, kc, mm * 128:(mm + 1) * 128],
                                                        rhs=S1[:, kc, tsl(tt)], start=(kc == 0), stop=(kc == KC - 1)))
                             for kc in range(KC)], rd=[pcs[pi][1]] + [SR[(kc, tt)] for kc in range(KC)], wr=[BANK[b]])
                    resid_evac(l, 2, pi * 2 + mm, tt, b)
                lag(tt, pi)
        for pc in pcs:
            W.done(pc[2])

    def ffn(l, side=None, after_tt=None):
        wg, wu = ("d_ffn_g", l), ("d_ffn_u", l)
        for (j0, j1) in FGROUPS:
            nj = j1 - j0
            for jp in range(j0, j1, 2):
                wgv, wgres, wgi = W.next(k1024_desc(wg, jp * 128))
                wuv, wures, wui = W.next(k1024_desc(wu, jp * 128))
                wg3 = wgv.rearrange("p (kc n) -> p kc n", n=256)
                wu3 = wuv.rearrange("p (kc n) -> p kc n", n=256)
                for mm in range(2):
                    jj = jp + mm - j0
                    for tt in range(NTT):
                        bg = getbank()
                        bu = getbank()
                        P.group([(lambda e, kc=kc, mm=mm, tt=tt, bg=bg: e.matmul(banks[bg][:], lhsT=wg3[:, kc, mm * 128:(mm + 1) * 128],
                                                                                rhs=h[:, kc, tsl(tt)], start=(kc == 0), stop=(kc == KC - 1)))
                                 for kc in range(KC)], rd=[wgres] + [HR[(kc, tt)] for kc in range(KC)], wr=[BANK[bg]])
                        P.group([(lambda e, kc=kc, mm=mm, tt=tt, bu=bu: e.matmul(banks[bu][:], lhsT=wu3[:, kc, mm * 128:(mm + 1) * 128],
                                                                                rhs=h[:, kc, tsl(tt)], start=(kc == 0), stop=(kc == KC - 1)))
                                 for kc in range(KC)], rd=[wures] + [HR[(kc, tt)] for kc in range(KC)], wr=[BANK[bu]])
                        ti = gettmp(4)
                        sg = tmpp[:, ti, 0:256].bitcast(BF16)
                        P.op("act", lambda e, bg=bg, sg=sg: e.activation(out=sg, in_=banks[bg][:], func=AF.Silu),
                             rd=[BANK[bg]], wr=[TMP[ti]])
                        P.op("dve", lambda e, bu=bu, sg=sg, jj=jj, tt=tt: e.tensor_tensor(out=S1[:, jj, tsl(tt)], in0=banks[bu][:], in1=sg,
                                                                                         op=ALU.mult),
                             rd=[BANK[bu], TMP[ti]], wr=[SR[(jj, tt)]])
                W.done(wgi)
                W.done(wui)
                if side is not None:
                    for _ in range(3):
                        next(side, None)
            dpcs = []
            for pi in range(4):
                def desc(A_, slot, pi=pi, j0=j0, nj=nj, l=l):
                    return (slot[:, 0:nj * 256].rearrange("p (jj n) -> p jj n", n=256),
                            A_["d_ffn_d"][l][j0 * 128:(j0 + nj) * 128, pi * 256:(pi + 1) * 256].rearrange("(jj p) n -> p jj n", p=128))
                dpcs.append(W.next(desc))
            last = (j1 == FCH)
            lag = lagged(after_tt if last else None)
            order = [(tt, pi) for tt in range(NTT) for pi in range(4)] if last else [(tt, pi) for pi in range(4) for tt in range(NTT)]
            for oi, (tt, pi) in enumerate(order):
                wv, wres, wi = dpcs[pi]
                wv3 = wv[:, 0:nj * 256].rearrange("p (jj n) -> p jj n", n=256)
                for mm in range(2):
                    m = pi * 2 + mm
                    b = getbank()
                    P.group([(lambda e, jj=jj: e.matmul(banks[b][:], lhsT=wv3[:, jj, mm * 128:(mm + 1) * 128],
                                                        rhs=S1[:, jj, tsl(tt)], start=(jj == 0), stop=(jj == nj - 1)))
                             for jj in range(nj)], rd=[wres] + [SR[(jj, tt)] for jj in range(nj)], wr=[BANK[b]])
                    resid_evac(l, 5, m, tt, b)
                if last:
                    lag(tt, pi)
                if not last and tt == NTT - 1:
                    W.done(wi)
            if last:
                for pc in dpcs:
                    W.done(pc[2])
        if side is not None:
            for _ in side:
                pass

    def lru_mixer(l):
        w_in = ("d_lru_w_in", None)
        o = 0
        xrp = reg[:, o:o + 520].bitcast(BF16); o += 520
        xrs = reg[:, o:o + 520].bitcast(BF16); o += 520
        xcb = [reg[:, o:o + 512].bitcast(BF16), reg[:, o + 512:o + 1024].bitcast(BF16)]; o += 1024
        T1both = reg[:, o:o + 2048]
        T1 = [reg[:, o:o + 1024], reg[:, o + 1024:o + 2048]]; o += 2048
        T3 = [reg[:, o:o + 1024], reg[:, o + 1024:o + 2048]]; o += 2048
        T2both = reg[:, o:o + 2048]
        T2 = [reg[:, o:o + 1024], reg[:, o + 1024:o + 2048]]; o += 2048
        dcv = reg[:, o:o + 512].bitcast(BF16).rearrange("p (u k n) -> p u k n", u=2, k=4); o += 512
        HS = [reg[:, o:o + 1024], T2[1]]; o += 1024
        assert o <= 9744
        xc = [ssb[:, 0:1024], ssb[:, 1024:2048]]
        gwt = tmpp[:].rearrange("p a b -> p (a b)").bitcast(BF16).rearrange("p (g cc i n) -> p g cc i n", g=2, cc=4, i=4)
        XRP, XRS = Res(), Res()
        XCR, XCBR = [[Res(), Res()], [Res(), Res()]], [[Res(), Res()], [Res(), Res()]]
        T1R = [[Res(), Res()], [Res(), Res()]]
        T3R = [[Res(), Res()], [Res(), Res()]]
        T2R = [[Res(), Res()], [Res(), Res()]]
        HSR = [[Res(), Res()], T2R[1]]
        DCV = [Res(), Res()]
        GWR = Res()
        LRES.extend([XRP, XRS] + XCBR[0] + XCBR[1] + T1R[0] + T1R[1] + T3R[0] + T3R[1] + T2R[0] + T2R[1] + HSR[0] + DCV)
        xrp3 = xrp[:, 0:1036].rearrange("p (s w) -> p s w", w=259)

        P.op("pool", lambda e: e.memset(xrp[:], 0.0), wr=[XRP])
        P.op("pool", lambda e: e.memset(xrs[:], 0.0), wr=[XRS])
        for g in range(2):
            P.dma("pool", "gwld", lambda e, g=g: e.dma_start(out=gwt[:, g].rearrange("p cc i n -> p (cc i n)"), in_=A["d_gwh"][g]),
                  wr=[GWR] + TMP)

        side1 = MODGEN[0]
        YINL = A.get("y_in_loop", True)
        MODCLS[0] = CLS_Y
        if not YINL:
            for pi in range(4):
                def evac(mm, tt, b, pi=pi):
                    m = pi * 2 + mm
                    P.op("act", lambda e: e.activation(out=S1[:, m, tsl(tt)], in_=banks[b][:], func=AF.Gelu_apprx_tanh),
                         rd=[BANK[b]], wr=[SR[(m, tt)]])
                proj_piece(w_in, 1024 + pi * 256, h, HR, evac)
                for _ in range(4):
                    next(side1, None)

        def y_part(c, tts):
            def ydesc(A_, slot):
                return (slot[:, 0:1024].rearrange("p (kc n) -> p kc n", n=128),
                        A_["d_lru_w_in"][:, 1024 + c * 128:1024 + (c + 1) * 128].rearrange("(kc p) n -> p kc n", p=128))
            wyv, wyres, wyi = W.next(ydesc)
            wy3 = wyv[:, 0:1024].rearrange("p (kc n) -> p kc n", n=128)
            ybanks = []
            for tt in tts:
                b = getbank(CLS_Y)
                ybanks.append(b)
                P.group([(lambda e, kc=kc: e.matmul(banks[b][:], lhsT=wy3[:, kc, :], rhs=h[:, kc, tsl(tt)],
                                                    start=(kc == 0), stop=(kc == KC - 1))) for kc in range(KC)],
                        rd=[wyres] + [HR[(kc, tt)] for kc in range(KC)], wr=[BANK[b]])
            W.done(wyi)

            def evac():
                for tt, b in zip(tts, ybanks):
                    P.op("act", lambda e: e.activation(out=S1[:, c, tsl(tt)], in_=banks[b][:], func=AF.Gelu_apprx_tanh),
                         rd=[BANK[b]], wr=[SR[(c, tt)]])
            return evac

        stfm = A["stfm"]
        STR = Res()
        units = [(c, gi) for c in range(KC) for gi in range(2)]
        NU = len(units)
        st = {}
        wxidx = {}

        def front_mm(ui):
            c, gi = units[ui]
            u = c % 2
            tts = (0, 1) if gi == 0 else (2, 3)
            def wxdesc(A_, slot, c=c):
                return (slot[:, 0:1024].rearrange("p (kc n) -> p kc n", n=128),
                        A_["d_lru_w_in"][:, c * 128:(c + 1) * 128].rearrange("(kc p) n -> p kc n", p=128))
            wxv, wxres, wxi = W.next(wxdesc)
            st["wx"] = (wxv[:, 0:1024].rearrange("p (kc n) -> p kc n", n=128), wxres, wxi)
            if gi == 0:
                for k in range(4):
                    P.op("dve", lambda e, k=k: e.tensor_scalar(out=dcv[:, u, k, :], in0=identF[:], scalar1=col(R_CONVW(k, c)),
                                                               scalar2=None, op0=ALU.mult), rd=[CONST], wr=[DCV[u]])
            wx3, wxres, wxi = st["wx"]
            for tt in tts:
                b = getbank(CLS_F)
                P.group([(lambda e, kc=kc: e.matmul(banks[b][:], lhsT=wx3[:, kc, :],
                                                    rhs=h[:, kc, tsl(tt)], start=(kc == 0), stop=(kc == KC - 1)))
                         for kc in range(KC)], rd=[wxres] + [HR[(kc, tt)] for kc in range(KC)], wr=[BANK[b]])
                if gi == 0:
                    dst = xrp3[:, 2 * tt:2 * tt + 2, 1:257]
                    src = banks[b][:].rearrange("p (s w) -> p s w", w=256)
                    P.op("dve", lambda e: e.tensor_copy(out=dst, in_=src), rd=[BANK[b]], wr=[XRP])
                else:
                    dst = xrs[:, 1 + (tt - 2) * TT:1 + (tt - 1) * TT]
                    P.op("dve", lambda e: e.tensor_copy(out=dst, in_=banks[b][:]), rd=[BANK[b]], wr=[XRS])
            W.done(wxi)

        def front_conv(ui):
            c, gi = units[ui]
            p = ui % 2
            u = c % 2
            tts = (0, 1) if gi == 0 else (2, 3)
            for ti_, tt in enumerate(tts):
                b = getbank(CLS_F)
                if gi == 0:
                    rhs_k = lambda k: xrp3[:, 2 * tt:2 * tt + 2, k:k + 256]
                    outv = banks[b][:].rearrange("p (s w) -> p s w", w=256)
                else:
                    rhs_k = lambda k: xrs[:, (tt - 2) * TT + k:(tt - 2) * TT + k + TT]
                    outv = banks[b][:]
                P.group([(lambda e, k=k: e.matmul(outv, lhsT=dcv[:, u, k, :], rhs=rhs_k(k), start=(k == 0), stop=(k == 3)))
                         for k in range(4)], rd=[XRP if gi == 0 else XRS, DCV[u]], wr=[BANK[b]])
                hs = slice(ti_ * TT, (ti_ + 1) * TT)
                if A.get("xcevac_eng", "dve") == "dve":
                    P.op("dve", lambda e: e.tensor_scalar(out=xc[p][:, hs], in0=banks[b][:], scalar1=col(R_CONVB(c)),
                                                          scalar2=None, op0=ALU.add), rd=[BANK[b], CONST], wr=[XCR[p][ti_], SSB[2 * p + ti_]])
                else:
                    P.op("act", lambda e: e.activation(out=xc[p][:, hs], in_=banks[b][:], func=AF.Identity, bias=col(R_CONVB(c)),
                                                       scale=1.0), rd=[BANK[b], CONST], wr=[XCR[p][ti_], SSB[2 * p + ti_]])

        def front_cast(ui):
            p = ui % 2
            if A.get("cast_eng", "dma") == "dma":
                P.dma("pool", "xcb%d" % p, lambda e: e.dma_start(out=xcb[p][:], in_=xc[p][:]), rd=XCR[p], wr=XCBR[p])
                return
            if A.get("cast_one", True) and A.get("cast_eng", "act") == "act":
                P.op("act", lambda e: e.activation(out=xcb[p][:], in_=xc[p][:], func=AF.Copy), rd=XCR[p], wr=XCBR[p])
                return
            for ti_ in range(2):
                hs = slice(ti_ * TT, (ti_ + 1) * TT)
                if A.get("cast_eng", "act") == "act":
                    P.op("act", lambda e: e.activation(out=xcb[p][:, hs], in_=xc[p][:, hs], func=AF.Copy),
                         rd=[XCR[p][ti_]], wr=[XCBR[p][ti_]])
                else:
                    P.op(A.get("cast_eng"), lambda e: e.tensor_copy(out=xcb[p][:, hs], in_=xc[p][:, hs]),
                         rd=[XCR[p][ti_]], wr=[XCBR[p][ti_]])

        def back_act(ui, z):
            c, gi = units[ui]
            p = ui % 2
            nwarm = A.get("warm", 0)
            if nwarm and ui < NU - 2:
                E = P.engs["pe"]
                rec = RecEng()
                for _ in range(nwarm):
                    rec.matmul(banks[7][:, 128:512], lhsT=onesB[:, :], rhs=h[:, 0, 0:384], start=True, stop=True)
                E.q.append(rec.calls)
            g4 = gwt[:, c // 4, c % 4]
            for (idx, dst, dstR, bcol) in ((z, T1[z], T1R[z], HGA), (2 + z, T3[z], T3R[z], HGX)):
                bb = [getbank(CLS_G), getbank(CLS_G)]
                for ti_ in range(2):
                    hs = slice(ti_ * TT, (ti_ + 1) * TT)
                    P.group([lambda e: e.matmul(banks[bb[ti_]][:], lhsT=g4[:, idx, :], rhs=xcb[p][:, hs], start=True, stop=True)],
                            rd=[GWR, XCBR[p][ti_]] + TMP, wr=[BANK[bb[ti_]]])
                for ti_ in range(2):
                    hs = slice(ti_ * TT, (ti_ + 1) * TT)
                    P.op("act", lambda e: e.activation(out=dst[:, hs], in_=banks[bb[ti_]][:], func=AF.Tanh,
                                                       bias=dcol[:, bcol + z * 8 + c:bcol + z * 8 + c + 1], scale=0.5),
                         rd=[BANK[bb[ti_]], CONST], wr=[dstR[ti_]])
            nsp = dcol[:, NSP4 + z * 8 + c:NSP4 + z * 8 + c + 1]
            P.op("act", lambda e: e.activation(out=T1[z][:], in_=T1[z][:], func=AF.Exp, bias=nsp, scale=nsp),
                 rd=[CONST], wr=T1R[z])
            if A.get("sq_merge", True):
                pass
            elif A.get("e2_eng", "act") == "act":
                P.op("act", lambda e: e.activation(out=T2[z][:], in_=T1[z][:], func=AF.Square), rd=T1R[z], wr=T2R[z])
            else:
                P.op("pool", lambda e: e.tensor_tensor(out=T2[z][:], in0=T1[z][:], in1=T1[z][:], op=ALU.mult), rd=T1R[z], wr=T2R[z])

        def back_sqrt(ui, z):
            if A.get("sq_merge", True):
                if z == 0 and A.get("sq_split", True):
                    for zz in range(2):
                        P.op("act", lambda e: e.activation(out=T2[zz][:], in_=T1[zz][:], func=AF.Square), rd=T1R[zz], wr=T2R[zz])
                        P.op("act", lambda e: e.activation(out=T2[zz][:], in_=T2[zz][:], func=AF.Sqrt, bias=1.0, scale=-1.0),
                             wr=T2R[zz])
                elif z == 0:
                    P.op("act", lambda e: e.activation(out=T2both, in_=T1both, func=AF.Square),
                         rd=T1R[0] + T1R[1], wr=T2R[0] + T2R[1])
                    if A.get("sqrt_split", True):
                        for zz in range(2):
                            P.op("act", lambda e: e.activation(out=T2[zz][:], in_=T2[zz][:], func=AF.Sqrt, bias=1.0, scale=-1.0),
                                 wr=T2R[zz])
                    else:
                        P.op("act", lambda e: e.activation(out=T2both, in_=T2both, func=AF.Sqrt, bias=1.0, scale=-1.0),
                             wr=T2R[0] + T2R[1])
                return
            P.op("act", lambda e: e.activation(out=T2[z][:], in_=T2[z][:], func=AF.Sqrt, bias=1.0, scale=-1.0), wr=T2R[z])

        def back_gi(ui, z):
            c, gi = units[ui]
            p = ui % 2
            P.op("dve", lambda e: e.scalar_tensor_tensor(out=T3[z][:], in0=T3[z][:], scalar=1.0, in1=xc[p][:], op0=ALU.add,
                                                         op1=ALU.mult), rd=XCR[p] + [SSB[2 * p], SSB[2 * p + 1]], wr=T3R[z])

        def back_dve(ui, z):
            c, gi = units[ui]
            p = ui % 2
            P.op("dve", lambda e: e.tensor_tensor(out=T3[z][:], in0=T3[z][:], in1=T2[z][:], op=ALU.mult), rd=T2R[z], wr=T3R[z])
            if gi == 0:
                a4 = T1[z][:].rearrange("p (s w) -> p s w", w=256)
                P.op("dve", lambda e: e.memset(a4[:, :, 0 if z == 0 else 255], 0.0), wr=T1R[z])
                sl_ = slice(None) if z == 0 else slice(None, None, -1)
                P.op("dve", lambda e: e.tensor_tensor_scan(out=HS[z][:, sl_], data0=T1[z][:, sl_], data1=T3[z][:, sl_],
                                                           initial=0.0, op0=ALU.mult, op1=ALU.add),
                     rd=T1R[z] + T3R[z], wr=HSR[z])
                h4 = HS[z][:].rearrange("p (s w) -> p s w", w=256)
                P.op("pool", lambda e: e.tensor_scalar(out=stfm[:, c, :, z], in0=h4[:, :, 255 if z == 0 else 0], scalar1=0.5,
                                                       scalar2=None, op0=ALU.mult), rd=HSR[z], wr=[STR])
            else:
                sl_ = slice(None) if z == 0 else slice(None, None, -1)
                P.op("dve", lambda e: e.tensor_tensor_scan(out=HS[z][:, sl_], data0=T1[z][:, sl_], data1=T3[z][:, sl_],
                                                           initial=dcol[:, H0X2 + z * 8 + c:H0X2 + z * 8 + c + 1],
                                                           op0=ALU.mult, op1=ALU.add),
                     rd=T1R[z] + T3R[z] + [CONST], wr=HSR[z])

        def combine_add(ui):
            P.op(A.get("add_eng", "dve"), lambda e: e.tensor_tensor(out=HS[0][:], in0=HS[0][:], in1=HS[1][:], op=ALU.add), rd=HSR[1], wr=HSR[0])

        def combine_stt(ui):
            c, gi = units[ui]
            tts = (0, 1) if gi == 0 else (2, 3)
            tsl2 = slice(gi * 1024, (gi + 1) * 1024)
            P.op("dve", lambda e: e.scalar_tensor_tensor(out=S1[:, c, tsl2], in0=HS[0][:], scalar=0.5, in1=S1[:, c, tsl2],
                                                         op0=ALU.mult, op1=ALU.mult),
                 rd=HSR[0], wr=[SR[(c, tts[0])], SR[(c, tts[1])]])

        front_mm(0)
        front_conv(0)
        front_cast(0)
        front_mm(1)
        sched = A.get("lru_sched", "a0 g0 fc ca a1 cs g1 fm md sq d0 d1 ad").split()
        nfill = A.get("fill", 0)

        def filler():
            rec = RecEng()
            for _ in range(nfill):
                rec.matmul(banks[7][:, 128:512], lhsT=onesB[:, :], rhs=h[:, 0, 0:384], start=True, stop=True)
            return rec.calls
        if YINL:
            y_part(0, (0, 1))()
            y_part(0, (2, 3))()
        yev = [None]
        for ui in range(NU):
            P.pe_fill = filler if (nfill and ui < NU - 2) else None
            if "md" not in sched:
                for _ in range(3 if YINL else (2 if ui % 2 == 0 else 1)):
                    next(side1, None)
            for tok in sched:
                if tok == "md":
                    for _ in range(3):
                        next(side1, None)
                elif tok == "ym":
                    if YINL and units[ui][0] + 1 < KC:
                        yev[0] = y_part(units[ui][0] + 1, (0, 1) if units[ui][1] == 0 else (2, 3))
                elif tok == "a0":
                    back_act(ui, 0)
                elif tok == "a1":
                    back_act(ui, 1)
                elif tok == "g0":
                    back_gi(ui, 0)
                elif tok == "g1":
                    back_gi(ui, 1)
                elif tok == "fc" and ui + 1 < NU:
                    front_conv(ui + 1)
                elif tok == "sq":
                    back_sqrt(ui, 0)
                    back_sqrt(ui, 1)
                elif tok == "s0":
                    back_sqrt(ui, 0)
                elif tok == "s1":
                    back_sqrt(ui, 1)
                elif tok == "ca" and ui + 1 < NU:
                    front_cast(ui + 1)
                elif tok == "cs" and ui >= 1:
                    combine_stt(ui - 1)
                elif tok == "d0":
                    back_dve(ui, 0)
                elif tok == "d1":
                    back_dve(ui, 1)
                elif tok == "fm" and ui + 2 < NU:
                    front_mm(ui + 2)
                elif tok == "ad":
                    combine_add(ui)
            if YINL and units[ui][0] + 1 < KC:
                if yev[0] is None:
                    yev[0] = y_part(units[ui][0] + 1, (0, 1) if units[ui][1] == 0 else (2, 3))
                yev[0]()
                yev[0] = None
        P.pe_fill = None
        MODCLS[0] = None
        combine_stt(NU - 1)
        if side1 is not None:
            for _ in side1:
                pass

        strow = xc[0][0:8, :]
        for half in range(2):
            b = getbank()
            P.group([(lambda e, c4=c4: e.transpose(out=banks[b][0:8, c4 * 128:(c4 + 1) * 128],
                                                   in_=stfm[:, half * 4 + c4, :, :].rearrange("p s z -> p (s z)"),
                                                   identity=identF[:])) for c4 in range(4)], rd=[STR, CONST], wr=[BANK[b]])
            P.op("dve", lambda e: e.tensor_copy(out=strow[:, half * 512:(half + 1) * 512], in_=banks[b][0:8, :]),
                 rd=[BANK[b]], wr=XCR[0] + [SSB[0], SSB[1]])
        A["_st_tok"] = P.dma("sp", "stout", lambda e: e.dma_start(out=A["d_stout"][:, :], in_=strow[:]), rd=XCR[0] + [SSB[0], SSB[1]])
        out_proj(l, ("d_lru_w_out", None), after_tt=lambda tt: norm_tile(l, 1, tt))

    def cm_layout():
        o = 0
        L = {}
        L["wv"] = reg[:, o:o + 4096].bitcast(BF16).rearrange("p (kc n) -> p kc n", n=1024); o += 4096
        L["Gf"] = reg[:, o:o + 1024].rearrange("p (g n) -> p g n", n=128); o += 1024
        L["Cc"] = reg[:, o:o + 1024].rearrange("p (g n) -> p g n", n=128); o += 1024
        L["wsT"] = reg[:, o:o + 512].bitcast(BF16).rearrange("p (g n) -> p g n", n=128); o += 512
        L["vg"] = [reg[:, o:o + 1024], reg[:, o + 1024:o + 2048]]; o += 2048
        L["vtm"] = [reg[:, o:o + 512].bitcast(BF16), reg[:, o + 512:o + 1024].bitcast(BF16)]; o += 1024
        return L

    CMR = {"wv": Res(), "setup": Res(), "vg": [Res(), Res()], "vtm": [Res(), Res()]}
    LRES = []

    def cm_setup():
        L = cm_layout()
        wsn = ssb[:, 0:1024].rearrange("p (g n) -> p g n", n=128)
        lnb_row = L["vg"][0][0:1, :]
        bs_row = L["vg"][1][0:1, :].rearrange("p (g n) -> p g n", n=128)
        rs_row = reg[0:1, 8704:9728].rearrange("p (g n) -> p g n", n=128)
        ROWS = Res()
        for q in range(4):
            P.dma("pool", "wvld", lambda e, q=q: e.dma_start(out=L["wv"][:, :, q * 256:(q + 1) * 256],
                                                              in_=A["d_cm_w_in"][:, 1024 + q * 256:1024 + (q + 1) * 256].rearrange(
                                                                  "(kc p) n -> p kc n", p=128)), wr=[CMR["wv"]] + LRES)
        P.dma("sp", "setup2", lambda e: e.dma_start(out=wsn, in_=A["d_cm_w_s"].rearrange("g p q -> p g q")), wr=SSB)
        P.dma("sp", "setup2", lambda e: e.dma_start(out=lnb_row, in_=A["d_cm_ln_b"][None, :]), wr=[ROWS] + LRES)
        P.dma("sp", "setup2", lambda e: e.dma_start(out=bs_row, in_=A["d_cm_b_s"][None, :, :]), wr=[ROWS] + LRES)
        tot = Tok(P.dsem["setup2"][0], P.dsem["setup2"][1], P.dsem["setup2"][2])
        for r in SSB + [ROWS]:
            r.w = tot
        for _ in range(A.get("cm_setup_delay", 6)):
            yield
        for half in range(2):
            b = getbank()
            P.group([(lambda e, g=g, b=b, half=half: e.transpose(out=banks[b][:, g * 128:(g + 1) * 128], in_=wsn[:, half * 4 + g, :],
                                                                 identity=identF[:])) for g in range(4)], rd=SSB + [CONST], wr=[BANK[b]])
            P.op("dve", lambda e, b=b, half=half: e.tensor_copy(out=L["wsT"][:, half * 4:half * 4 + 4, :],
                                                                in_=banks[b][:].rearrange("p (g n) -> p g n", n=128)),
                 rd=[BANK[b]], wr=[CMR["setup"]] + LRES)
        for half in range(2):
            b = getbank()
            P.group([(lambda e, g=g, b=b, half=half: e.matmul(banks[b][0:1, g * 128:(g + 1) * 128], lhsT=onesB[:, 0:1],
                                                              rhs=L["wsT"][:, half * 4 + g, :], start=True, stop=True)) for g in range(4)],
                    rd=[CMR["setup"], CONST], wr=[BANK[b]])
            P.op("dve", lambda e, b=b, half=half: e.tensor_copy(out=rs_row[:, half * 4:half * 4 + 4, :],
                                                                in_=banks[b][0:1, :].rearrange("p (g n) -> p g n", n=128)),
                 rd=[BANK[b]], wr=[ROWS] + LRES)
        onesrowF = A["onesrowF"]
        for half in range(2):
            b = getbank()
            fns = []
            for g4 in range(4):
                g = half * 4 + g4
                fns.append(lambda e, g=g, g4=g4, b=b: e.matmul(banks[b][:, g4 * 128:(g4 + 1) * 128], lhsT=lnb_row[:, g * 128:(g + 1) * 128],
                                                               rhs=rs_row[:, g, :], start=True, stop=False, skip_group_check=True))
                fns.append(lambda e, g=g, g4=g4, b=b: e.matmul(banks[b][:, g4 * 128:(g4 + 1) * 128], lhsT=onesrowF[0:1, :],
                                                               rhs=bs_row[:, g, :], start=False, stop=True, skip_group_check=True))
            P.group(fns, rd=[ROWS, CONST], wr=[BANK[b]])
            P.op("dve", lambda e, b=b, half=half: e.tensor_copy(out=L["Cc"][:, half * 4:half * 4 + 4, :],
                                                                in_=banks[b][:].rearrange("p (g n) -> p g n", n=128)),
                 rd=[BANK[b]], wr=[CMR["setup"]] + LRES)
        for g in range(KC):
            P.op("dve", lambda e, g=g: e.tensor_scalar(out=L["Gf"][:, g, :], in0=onesB[:], scalar1=col(R_LNG(g)), scalar2=None,
                                                       op0=ALU.mult), rd=[CONST], wr=[CMR["setup"]] + LRES)

    def cm_mixer(l):
        L = cm_layout()
        w_in = ("d_cm_w_in", None)
        binv = tmpp[0:1, 3, :].bitcast(BF16)
        P.dma("pool", "binv", lambda e: e.dma_start(out=binv, in_=A["d_cm_b_in"][None, 1024:2048]), wr=[TMP[3]])
        for pi in range(4):
            def evac(mm, tt, b, pi=pi):
                m = pi * 2 + mm
                P.op("act", lambda e: e.activation(out=S1[:, m, tsl(tt)], in_=banks[b][:], func=AF.Gelu_apprx_tanh,
                                                   bias=col(R_BINU(m)), scale=1.0), rd=[BANK[b], CONST], wr=[SR[(m, tt)]])
            proj_piece(w_in, pi * 256, h, HR, evac)
        stats = A["stats"]
        STAT = [Res(), Res()]
        CLS_V = (0, 4)
        CLS_M = (4, 4)

        pb = A["pb"]
        vpair = {"i": 0}
        mpair = {"i": 0}

        def v_mm(j):
            tt = j // 4
            jsl = slice(j * 128, (j + 1) * 128)
            u = j % 2
            vg = L["vg"][u]
            b = (vpair["i"] % 2) * 2
            vpair["i"] += 1
            for half in range(2):
                fns = [(lambda e, kc=kc: e.matmul(banks[b + half][:], lhsT=h[:, kc, jsl], rhs=L["wv"][:, kc, half * 512:(half + 1) * 512],
                                                  start=(kc == 0), stop=False)) for kc in range(KC)]
                fns.append(lambda e: e.matmul(banks[b + half][:], lhsT=onesB[0:1, :], rhs=binv[:, half * 512:(half + 1) * 512],
                                              start=False, stop=True))
                P.group(fns, rd=[HR[(kc, tt)] for kc in range(KC)] + [CMR["wv"], TMP[3], CONST], wr=[BANK[b + half]])
            P.op("act", lambda e: e.activation(out=vg.rearrange("p (a n) -> p a n", n=512), in_=pb[:, b:b + 2, :],
                                               func=AF.Gelu_apprx_tanh), rd=[BANK[b], BANK[b + 1]], wr=[CMR["vg"][u]])

        def v_stats(j):
            u = j % 2
            vg = L["vg"][u]
            vtm = L["vtm"][u]
            st = stats[:, u, :]
            for half in range(2):
                P.op("dve", lambda e: e.bn_stats(out=st[:, half * 6:(half + 1) * 6], in_=vg[:, half * 512:(half + 1) * 512]),
                     rd=[CMR["vg"][u]], wr=[STAT[u]])
            P.op("dve", lambda e: e.bn_aggr(out=st[:, 12:14], in_=st[:, 0:12]), wr=[STAT[u]])
            P.op("dve", lambda e: e.tensor_scalar(out=st[:, 14:15], in0=st[:, 13:14], scalar1=EPS, scalar2=None, op0=ALU.add),
                 wr=[STAT[u]])
            P.op("pool", lambda e: e.tensor_tensor(out=st[:, 15:16], in0=st[:, 14:15], in1=A["mhalf"][:, 0:1], op=ALU.pow),
                 rd=[CONST], wr=[STAT[u]])

        def v_vtm(j):
            u = j % 2
            vg = L["vg"][u]
            vtm = L["vtm"][u]
            st = stats[:, u, :]
            P.op("dve", lambda e: e.tensor_scalar(out=vtm, in0=vg, scalar1=st[:, 12:13], scalar2=st[:, 15:16],
                                                  op0=ALU.subtract, op1=ALU.mult), rd=[STAT[u], CMR["vg"][u]], wr=[CMR["vtm"][u]])

        mixb = {}

        def v_mix_mm(j):
            tt = j // 4
            jsl = slice(j * 128, (j + 1) * 128)
            u = j % 2
            vg = L["vg"][u]
            vtm = L["vtm"][u]
            b = 4 + (mpair["i"] % 2) * 2
            mpair["i"] += 1
            mixb[j] = b
            for half in range(2):
                P.group([(lambda e, g4=g4: e.matmul(banks[b + half][:, g4 * 128:(g4 + 1) * 128],
                                                    lhsT=vtm[:, (half * 4 + g4) * 128:(half * 4 + g4 + 1) * 128],
                                                    rhs=L["wsT"][:, half * 4 + g4, :], start=True, stop=True))
                         for g4 in range(4)], rd=[CMR["vtm"][u], CMR["setup"]], wr=[BANK[b + half]])

        def v_mix_evac(j):
            tt = j // 4
            jsl = slice(j * 128, (j + 1) * 128)
            u = j % 2
            vg = L["vg"][u]
            b = mixb[j]
            tv = vg.rearrange("p (g n) -> p g n", n=128)
            bk3 = pb[:, b:b + 2, :].rearrange("p a (g n) -> p (a g) n", n=128)
            P.op("dve", lambda e: e.tensor_tensor(out=tv, in0=bk3, in1=L["Gf"][:], op=ALU.mult),
                 rd=[BANK[b], BANK[b + 1], CMR["setup"]], wr=[CMR["vg"][u]])
            P.op("dve", lambda e: e.tensor_tensor(out=tv, in0=tv, in1=L["Cc"][:], op=ALU.add),
                 rd=[CMR["setup"]], wr=[CMR["vg"][u]])
            P.op("dve", lambda e: e.tensor_tensor(out=S1[:, :, jsl], in0=tv, in1=S1[:, :, jsl], op=ALU.mult),
                 rd=[CMR["vg"][u]], wr=[SR[(m, tt)] for m in range(KC)])

        v_mm(0)
        v_stats(0)
        v_vtm(0)
        v_mm(1)
        for k in range(1, 17):
            v_mix_mm(k - 1)
            if k < 16:
                v_stats(k)
            v_mix_evac(k - 1)
            if k < 16:
                v_vtm(k)
            if k + 1 < 16:
                v_mm(k + 1)
        out_proj(l, ("d_cm_w_out", None), after_tt=lambda tt: norm_tile(l, 1, tt))

    ystg = [reg[:, i * 1024:(i + 1) * 1024] for i in range(2)]
    YSTG = [Res(), Res()]
    ytoks = {}

    def final_tile(tt):
        norm_stats(tt)
        for kc in range(KC):
            P.op("dve", lambda e: e.scalar_tensor_tensor(out=x_fm[:, kc, tsl(tt)], in0=x_fm[:, kc, tsl(tt)],
                                                         scalar=col(R_FN(kc)), in1=ssb[:, tsl(tt)], op0=ALU.mult,
                                                         op1=ALU.mult), rd=[SSB[tt], CONST], wr=[XR[(kc, tt)]])
        for j in range(tt * 4, tt * 4 + 4):
            s_ = j % 2
            for hb in range(2):
                b = getbank()
                P.group([(lambda e, q=q: e.transpose(out=banks[b][:, q * 128:(q + 1) * 128],
                                                     in_=x_fm[:, hb * 4 + q, j * 128:(j + 1) * 128], identity=identF[:]))
                         for q in range(4)], rd=[XR[(m, tt)] for m in range(hb * 4, hb * 4 + 4)] + [CONST], wr=[BANK[b]])
                dst = ystg[s_][:, hb * 512:(hb + 1) * 512]
                if hb == 0:
                    P.op("dve", lambda e: e.tensor_copy(out=dst, in_=banks[b][:]), rd=[BANK[b]], wr=[YSTG[s_], CMR["wv"]])
                else:
                    P.op("act", lambda e: e.activation(out=dst, in_=banks[b][:], func=AF.Copy), rd=[BANK[b]], wr=[YSTG[s_], CMR["wv"]])
            ytoks[s_] = P.dma("sp", "ys%d" % s_, lambda e: e.dma_start(out=A["d_yout"][j * 128:(j + 1) * 128, :], in_=ystg[s_]),
                              rd=[YSTG[s_]])

    def run_all(fns):
        for _ in fns:
            pass

    stop = A.get("stop")
    import itertools
    MODGEN = [itertools.chain(mod_pieces(0, split=8, unpin=False), mod_pieces(1))]

    stages = [
        ("prologue", lambda: (prologue(side=MODGEN[0], nside=8), P.fence())),
        ("norm00", lambda: [norm_apply(0, 0, tt) for tt in range(NTT)]),
        ("lru", lambda: lru_mixer(0)),
        ("ffn0", lambda: ffn(0, side=cm_setup(), after_tt=lambda tt: norm_tile(1, 0, tt))),
        ("cm", lambda: cm_mixer(1)),
        ("ffn1", lambda: ffn(1, after_tt=(final_tile if stop is None else None))),
    ]
    Esp = P.engs["sp"]
    for name, fn in stages:
        fn()
        if stop == name:
            P.fence()
            dt = [P.dma("sp", "ys0", lambda e: e.dma_start(out=A["d_dbg_x"], in_=x_fm[:].rearrange("p a b -> p (a b)"))),
                  P.dma("sp", "ys0", lambda e: e.dma_start(out=A["d_dbg_h"], in_=h[:].rearrange("p a b -> p (a b)"))),
                  P.dma("sp", "ys0", lambda e: e.dma_start(out=A["d_dbg_s"], in_=S1[:].rearrange("p a b -> p (a b)"))),
                  P.dma("sp", "ys0", lambda e: e.dma_start(out=A["d_dbg_m"], in_=A["modT"][:].rearrange("p a b c -> p (a b c)")))]
            Esp.wait(dt[-1])
            if "_st_tok" in A:
                Esp.wait(A["_st_tok"])
            return
    for t in list(ytoks.values()) + [A["_st_tok"]]:
        Esp.wait(t)


def build_nc(plan, stop=None):
    nc = bass.Bass("TRN2", target_bir_lowering=False)
    A = {"stop": stop}
    A.update(TUNE)

    def din(name, shape):
        return nc.dram_tensor(name, shape, F32, kind="ExternalInput").ap()

    A["d_xin"] = din("xin", [T, D])
    A["d_ptab"] = din("ptab", [NROWS, 128])
    A["d_modw"] = din("modw", [2, D, 6 * D])
    A["d_lru_w_in"] = din("lru_w_in", [D, 2 * D])
    A["d_gwh"] = din("gwh", [2, 128, 2048])
    A["d_lru_w_out"] = din("lru_w_out", [D, D])
    A["d_cm_w_in"] = din("cm_w_in", [D, 2 * D])
    A["d_cm_b_in"] = din("cm_b_in", [2 * D])
    A["d_cm_ln_b"] = din("cm_ln_b", [D])
    A["d_cm_w_s"] = din("cm_w_s", [8, 128, 128])
    A["d_cm_b_s"] = din("cm_b_s", [8, 128])
    A["d_cm_w_out"] = din("cm_w_out", [D, D])
    A["d_ffn_g"] = din("ffn_g", [2, D, FF])
    A["d_ffn_u"] = din("ffn_u", [2, D, FF])
    A["d_ffn_d"] = din("ffn_d", [2, FF, D])
    A["d_yout"] = nc.dram_tensor("yout", [T, D], F32, kind="ExternalOutput").ap()
    A["d_stout"] = nc.dram_tensor("stout", [8, D], F32, kind="ExternalOutput").ap()
    if stop is not None:
        A["d_dbg_x"] = nc.dram_tensor("dbg_x", [128, KC * T], F32, kind="ExternalOutput").ap()
        A["d_dbg_h"] = nc.dram_tensor("dbg_h", [128, KC * T], BF16, kind="ExternalOutput").ap()
        A["d_dbg_s"] = nc.dram_tensor("dbg_s", [128, KC * T], BF16, kind="ExternalOutput").ap()
        A["d_dbg_m"] = nc.dram_tensor("dbg_m", [128, 192], F32, kind="ExternalOutput").ap()

    with ExitStack() as es:
        def sb(name, shape, dt):
            return es.enter_context(nc.sbuf_tensor(name, shape, dt))

        A["x_fm"] = sb("x_fm", [128, KC, T], F32)
        A["h"] = sb("h", [128, KC, T], BF16)
        A["S1"] = sb("S1", [128, KC, T], BF16)
        A["ring"] = sb("ring", [128, 4, 2048], BF16)
        A["reg"] = sb("reg", [128, 9744], F32)
        A["ssb"] = sb("ssb", [128, T], F32)
        A["tmpp"] = sb("tmpp", [128, 4, 512], F32)
        A["pcol"] = sb("pcol", [128, 384], F32)
        A["ptst"] = sb("ptst", [128, 3, 128], F32)
        A["identF"] = sb("identF", [128, 128], F32)
        A["identB"] = sb("identB", [128, 128], BF16)
        A["onesB"] = sb("onesB", [128, 128], BF16)
        A["onesrowF"] = sb("onesrowF", [1, 128], F32)
        A["mhalf"] = sb("mhalf", [128, 2], F32)
        A["dcol"] = sb("dcol", [128, 128], F32)
        A["condT"] = sb("condT", [128, KC, 2], BF16)
        A["modT"] = sb("modT", [128, 2, 48, 2], F32)
        A["amod"] = sb("amod", [128, 2, 2, 8, 2], F32)
        A["stfm"] = sb("stfm", [128, KC, 4, 2], F32)
        A["stats"] = sb("stats", [128, 2, 16], F32)
        A["mrow"] = sb("mrow", [2, 512], F32)
        A["pb"] = es.enter_context(nc.psum_tensor("pb", [128, 8, 512], F32))
        A["banks"] = [A["pb"][:, i, :] for i in range(8)]

        sems = {}
        for n in ["pe", "act", "dve", "pool"]:
            sems[n] = es.enter_context(nc.semaphore("s_" + n))
        dnames = ["ring0", "ring1", "ring2", "ring3", "setup", "setup2", "xs0", "xs1", "xs2", "xs3", "ys0", "ys1", "stout", "wvld", "binv", "gwld", "xcb0", "xcb1"]
        for n in dnames:
            sems[n] = es.enter_context(nc.semaphore("d_" + n))

        P = Prog(dry=False)
        for n in ["pe", "act", "dve", "pool"]:
            P.add_engine(n, sems[n])
        P.engs["sp"] = Eng("sp", -1, None)
        for n in dnames:
            P.add_dsem(n, sems[n])
        W = WStream(P, A, plan)
        emit_program(P, A, W)
        global _LAST_P
        _LAST_P = P
        if plan is None:
            return W.rec

        block = es.enter_context(nc.Block())

        @block.tensor
        def _(e):
            for calls in P.engs["pe"].q:
                replay(calls, e)

        @block.scalar
        def _(e):
            for calls in P.engs["act"].q:
                replay(calls, e)

        @block.vector
        def _(e):
            for calls in P.engs["dve"].q:
                replay(calls, e)

        @block.gpsimd
        def _(e):
            for calls in P.engs["pool"].q:
                replay(calls, e)

        @block.sync
        def _(e):
            for calls in P.engs["sp"].q:
                replay(calls, e)
    return nc


def _prep_inputs(inp):
    f = lambda a: np.ascontiguousarray(np.asarray(a, dtype=np.float32))
    xp, xs = f(inp["x_prompt"]), f(inp["x_sample"])
    shared = {
        "modw": f(inp["mod_w"]),
        "lru_w_in": f(inp["lru_w_in"][0]),
        "lru_w_out": f(inp["lru_w_out"][0]),
        "cm_w_in": f(inp["cm_w_in"][0]),
        "cm_b_in": f(inp["cm_b_in"][0]),
        "cm_ln_b": f(inp["cm_ln_b"][0]),
        "cm_w_s": f(inp["cm_w_s"][0]),
        "cm_b_s": f(inp["cm_b_s"][0]),
        "cm_w_out": f(inp["cm_w_out"][0]),
        "ffn_g": f(inp["ffn_w_gate"]),
        "ffn_u": f(inp["ffn_w_up"]),
        "ffn_d": f(inp["ffn_w_down"]),
    }
    ga, gx = f(inp["lru_ga_w"][0]), f(inp["lru_gx_w"][0])
    gwh = np.zeros((2, 128, 4, 4, 128), np.float32)
    for c in range(8):
        for idx in range(4):
            src = ga if idx < 2 else gx
            z = idx % 2
            for two in range(2):
                gwh[c // 4, two * 64:(two + 1) * 64, c % 4, idx, two * 64:(two + 1) * 64] = src[z, 2 * c + two]
    shared["gwh"] = gwh.reshape(2, 128, 2048)

    def rows(v):
        return f(v).reshape(-1, 128)

    common_rows = [rows(inp["mod_b"]), rows(inp["norm_mix"]), rows(inp["norm_ffn"]), rows(inp["final_norm"]),
                   rows(inp["lru_conv_w"][0]), rows(inp["lru_conv_b"][0]), rows(inp["lru_ga_b"][0]), rows(inp["lru_gx_b"][0]),
                   rows(inp["lru_lambda"][0]), rows(inp["cm_b_in"][0][:1024]), rows(inp["cm_ln_g"][0])]
    in_maps = []
    for i in range(NCORES):
        tab = np.zeros((NROWS, 128), np.float32)
        r = np.concatenate(common_rows + [rows(inp["state_lru"][i, 0]), rows(inp["c"][i]), rows(inp["c_ctx"])], axis=0)
        tab[:r.shape[0]] = r
        m = dict(shared)
        m["xin"] = np.ascontiguousarray(np.concatenate([xp[4 * i:4 * i + 4].reshape(1024, D), xs[i]], axis=0))
        m["ptab"] = tab
        in_maps.append(m)
    return in_maps


TUNE = {}
_NC_CACHE = {}
_LAST_P = None


def kernel(**inputs):
    in_maps = _prep_inputs(inputs)
    if "nc" not in _NC_CACHE:
        plan = build_nc(None)
        _NC_CACHE["nc"] = build_nc(plan)
    nc = _NC_CACHE["nc"]
    res = run_bass_kernel_spmd(nc, in_maps, core_ids=list(range(NCORES)))
    y_prompt = np.zeros((32, 256, D), np.float32)
    y_sample = np.zeros((8, 1024, D), np.float32)
    new_state = np.zeros((32, 1, 2, D), np.float32)
    for i in range(NCORES):
        y = np.asarray(res.results[i]["yout"], dtype=np.float32)
        y_prompt[4 * i:4 * i + 4] = y[:1024].reshape(4, 256, D)
        y_sample[i] = y[1024:]
        st = np.asarray(res.results[i]["stout"], dtype=np.float32)
        new_state[4 * i:4 * i + 4, 0] = st.reshape(4, 2, D)
    return (y_prompt, y_sample, new_state)
```
